# Optimizing a Trainium2 kernel written in Bass

```python
import math
import jax, jax.numpy as jnp
from jax import lax
import numpy as np

D_MODEL = 2048
BATCH = 32
SEQ = 256
DEPTH = 1
DEC_BATCH = 4
DEC_SEQ = 4096
PAST_LEN = 256

GRID_W = 64
CHUNK = 128
D_A = D_MODEL // 2
N_GROUPS_A = 4
D_B = D_MODEL // 2
SSM_IN = 16
N_GROUPS_B = D_B // SSM_IN
SSM_STATE = 64
N_DIR = 2
D_FF = 4 * D_MODEL
N_MOD = 6
D_IN = 2 * D_A + D_B + 2 * D_MODEL
EPS = 1e-6
POS_BASE = 10000.0

kernel_name = "hybrid_gmlp_s5_diffusion_step"


def rmsnorm(x, g):
    xf = x.astype(jnp.float32)
    y = xf * lax.rsqrt(jnp.mean(xf * xf, axis=-1, keepdims=True) + EPS)
    return (y * g.astype(jnp.float32)).astype(x.dtype)


def modulate(h, shift, scale):
    return h * (1 + scale[:, None, :]) + shift[:, None, :]


def grid_pos_embed(n_tokens, dtype):
    rows = n_tokens // GRID_W
    quarter = D_MODEL // 4
    omega = 1.0 / (POS_BASE ** (jnp.arange(quarter, dtype=jnp.float32) / quarter))
    r = jnp.arange(rows, dtype=jnp.float32)[:, None] * omega
    col = jnp.arange(GRID_W, dtype=jnp.float32)[:, None] * omega
    e_r = jnp.concatenate([jnp.sin(r), jnp.cos(r)], axis=-1)
    e_c = jnp.concatenate([jnp.sin(col), jnp.cos(col)], axis=-1)
    emb = jnp.concatenate([
        jnp.broadcast_to(e_r[:, None, :], (rows, GRID_W, D_MODEL // 2)),
        jnp.broadcast_to(e_c[None, :, :], (rows, GRID_W, D_MODEL // 2))], axis=-1)
    return emb.reshape(n_tokens, D_MODEL).astype(dtype)


def chunk_mlp(u, v, g_sgu, w_s, b_s):
    bsz, n, _ = u.shape
    u = jax.nn.gelu(u)
    v = rmsnorm(jax.nn.gelu(v), g_sgu)
    vc = v.reshape(bsz, n // CHUNK, CHUNK, N_GROUPS_A, D_A // N_GROUPS_A)
    s = jnp.einsum('gij,bnjgc->bnigc', w_s, vc) + jnp.transpose(b_s)[None, None, :, :, None]
    return u * s.reshape(bsz, n, D_A)


def _cmul(ar, ai, br, bi):
    return ar * br - ai * bi, ar * bi + ai * br


def _scan_combine(e1, e2):
    a1r, a1i, b1r, b1i = e1
    a2r, a2i, b2r, b2i = e2
    ar, ai = _cmul(a2r, a2i, a1r, a1i)
    br, bi = _cmul(a2r, a2i, b1r, b1i)
    return ar, ai, br + b2r, bi + b2i


def s5_direction(u, h0_re, h0_im, a_re, a_im, log_dt, b_re, b_im, c_re, c_im):
    n = u.shape[0]
    dt = jnp.exp(log_dt)[:, None]
    mag = jnp.exp(a_re * dt)
    ab_re, ab_im = mag * jnp.cos(a_im * dt), mag * jnp.sin(a_im * dt)
    den = a_re * a_re + a_im * a_im
    f_re, f_im = _cmul(ab_re - 1.0, ab_im, a_re / den, -a_im / den)
    bu_re = jnp.einsum('lbgc,gpc->lbgp', u, b_re)
    bu_im = jnp.einsum('lbgc,gpc->lbgp', u, b_im)
    x_re, x_im = _cmul(f_re, f_im, bu_re, bu_im)
    i_re, i_im = _cmul(ab_re, ab_im, h0_re, h0_im)
    x_re = x_re.at[0].add(i_re)
    x_im = x_im.at[0].add(i_im)
    a_seq_re = jnp.broadcast_to(ab_re[None, None], (n, 1) + ab_re.shape)
    a_seq_im = jnp.broadcast_to(ab_im[None, None], (n, 1) + ab_im.shape)
    _, _, s_re, s_im = lax.associative_scan(_scan_combine, (a_seq_re, a_seq_im, x_re, x_im), axis=0)
    y = jnp.einsum('lbgp,gcp->lbgc', s_re, c_re) - jnp.einsum('lbgp,gcp->lbgc', s_im, c_im)
    return y, s_re[-1], s_im[-1]


def s5_mixer(xb, h0_re, h0_im, a_re, a_im, log_dt, b_re, b_im, c_re, c_im, d, w_glu, b_glu):
    f32 = jnp.float32
    bsz, n, _ = xb.shape
    xf = xb.astype(f32)
    u = jnp.transpose(xf.reshape(bsz, n, N_GROUPS_B, SSM_IN), (1, 0, 2, 3))
    a_re, a_im, log_dt = a_re.astype(f32), a_im.astype(f32), log_dt.astype(f32)
    b_re, b_im, c_re, c_im = b_re.astype(f32), b_im.astype(f32), c_re.astype(f32), c_im.astype(f32)
    h0_re, h0_im = h0_re.astype(f32), h0_im.astype(f32)
    y_f, fw_re, fw_im = s5_direction(u, h0_re[:, 0], h0_im[:, 0], a_re[0], a_im[0], log_dt[0],
                                     b_re[0], b_im[0], c_re[0], c_im[0])
    y_b, bw_re, bw_im = s5_direction(u[::-1], h0_re[:, 1], h0_im[:, 1], a_re[1], a_im[1], log_dt[1],
                                     b_re[1], b_im[1], c_re[1], c_im[1])
    y = y_f + y_b[::-1]
    y = jnp.transpose(y, (1, 0, 2, 3)).reshape(bsz, n, D_B) + d.astype(f32) * xf
    y = jax.nn.gelu(y).astype(xb.dtype)
    y = y * jax.nn.sigmoid(y @ w_glu + b_glu)
    return y, jnp.stack([fw_re, bw_re], axis=1), jnp.stack([fw_im, bw_im], axis=1)


def trunk_layer(x, mod, h0_re, h0_im, g_norm_mix, w_in, g_sgu, w_spatial, b_spatial,
                ssm_a_re, ssm_a_im, ssm_log_dt, ssm_b_re, ssm_b_im, ssm_c_re, ssm_c_im, ssm_d,
                w_glu, b_glu, w_proj_a, w_proj_b, w_out, g_norm_mlp, w_mlp_in, w_mlp_out):
    shift1, scale1, gate1, shift2, scale2, gate2 = jnp.split(mod, N_MOD, axis=-1)
    h = modulate(rmsnorm(x, g_norm_mix), shift1, scale1)
    z = h @ w_in
    u_a, v_a, x_b, gl_a, gl_b = jnp.split(
        z, [D_A, 2 * D_A, 2 * D_A + D_B, 2 * D_A + D_B + D_MODEL], axis=-1)
    y_a = chunk_mlp(u_a, v_a, g_sgu, w_spatial, b_spatial)
    y_b, s_re, s_im = s5_mixer(x_b, h0_re, h0_im, ssm_a_re, ssm_a_im, ssm_log_dt,
                               ssm_b_re, ssm_b_im, ssm_c_re, ssm_c_im, ssm_d, w_glu, b_glu)
    m = jax.nn.sigmoid(gl_a) * (y_a @ w_proj_a) + jax.nn.sigmoid(gl_b) * (y_b @ w_proj_b)
    x = x + gate1[:, None, :] * (m @ w_out)
    h2 = modulate(rmsnorm(x, g_norm_mlp), shift2, scale2)
    x = x + gate2[:, None, :] * (jnp.square(jax.nn.relu(h2 @ w_mlp_in)) @ w_mlp_out)
    return x, s_re, s_im


def setup_inputs(seed: int = 0) -> dict:
    key = jax.random.key(seed)
    ks = jax.random.split(key, 30)
    f32 = jnp.float32

    def nrm(k, shape, scale):
        return jax.random.normal(k, shape, f32) * scale

    G, P = N_GROUPS_B, SSM_STATE
    n_idx = jnp.arange(P, dtype=f32)
    ssm_shape = (DEPTH, N_DIR, G, P)
    return {
        "x_prompt": nrm(ks[0], (BATCH, SEQ, D_MODEL), 1.0),
        "x_sample": nrm(ks[1], (DEC_BATCH, DEC_SEQ, D_MODEL), 1.0),
        "state_ssm_re": nrm(ks[2], (DEC_BATCH, DEPTH, N_DIR, G, P), 0.5),
        "state_ssm_im": nrm(ks[3], (DEC_BATCH, DEPTH, N_DIR, G, P), 0.5),
        "c": nrm(ks[4], (DEC_BATCH, D_MODEL), 1.0),
        "c_ctx": nrm(ks[5], (D_MODEL,), 1.0),
        "w_ada": nrm(ks[6], (DEPTH, D_MODEL, N_MOD * D_MODEL), 0.5 * D_MODEL ** -0.5),
        "b_ada": nrm(ks[7], (DEPTH, N_MOD * D_MODEL), 0.02),
        "g_norm_mix": 1.0 + nrm(ks[8], (DEPTH, D_MODEL), 0.02),
        "w_in": nrm(ks[9], (DEPTH, D_MODEL, D_IN), D_MODEL ** -0.5),
        "g_sgu": 1.0 + nrm(ks[10], (DEPTH, D_A), 0.02),
        "w_spatial": nrm(ks[11], (DEPTH, N_GROUPS_A, CHUNK, CHUNK), CHUNK ** -0.5),
        "b_spatial": 1.0 + nrm(ks[12], (DEPTH, N_GROUPS_A, CHUNK), 0.02),
        "ssm_a_re": -0.5 + nrm(ks[13], ssm_shape, 0.01),
        "ssm_a_im": math.pi * n_idx + nrm(ks[14], ssm_shape, 0.01),
        "ssm_log_dt": jax.random.uniform(ks[15], (DEPTH, N_DIR, G), f32,
                                         math.log(1e-3), math.log(1e-1)),
        "ssm_b_re": nrm(ks[16], (DEPTH, N_DIR, G, P, SSM_IN), (2 * SSM_IN) ** -0.5),
        "ssm_b_im": nrm(ks[17], (DEPTH, N_DIR, G, P, SSM_IN), (2 * SSM_IN) ** -0.5),
        "ssm_c_re": nrm(ks[18], (DEPTH, N_DIR, G, SSM_IN, P), (2 * P) ** -0.5),
        "ssm_c_im": nrm(ks[19], (DEPTH, N_DIR, G, SSM_IN, P), (2 * P) ** -0.5),
        "ssm_d": nrm(ks[20], (DEPTH, D_B), 1.0),
        "w_glu": nrm(ks[21], (DEPTH, D_B, D_B), D_B ** -0.5),
        "b_glu": nrm(ks[22], (DEPTH, D_B), 0.02),
        "w_proj_a": nrm(ks[23], (DEPTH, D_A, D_MODEL), D_A ** -0.5),
        "w_proj_b": nrm(ks[24], (DEPTH, D_B, D_MODEL), D_B ** -0.5),
        "w_out": nrm(ks[25], (DEPTH, D_MODEL, D_MODEL), D_MODEL ** -0.5),
        "g_norm_mlp": 1.0 + nrm(ks[26], (DEPTH, D_MODEL), 0.02),
        "w_mlp_in": nrm(ks[27], (DEPTH, D_MODEL, D_FF), D_MODEL ** -0.5),
        "w_mlp_out": nrm(ks[28], (DEPTH, D_FF, D_MODEL), D_FF ** -0.5),
        "g_final": 1.0 + nrm(ks[29], (D_MODEL,), 0.02),
    }


def reference(x_prompt, x_sample, state_ssm_re, state_ssm_im, c, c_ctx, w_ada, b_ada,
              g_norm_mix, w_in, g_sgu, w_spatial, b_spatial, ssm_a_re, ssm_a_im, ssm_log_dt,
              ssm_b_re, ssm_b_im, ssm_c_re, ssm_c_im, ssm_d, w_glu, b_glu, w_proj_a, w_proj_b,
              w_out, g_norm_mlp, w_mlp_in, w_mlp_out, g_final):
    bsz_p = x_prompt.shape[0]
    zero_state = jnp.zeros((bsz_p, N_DIR, N_GROUPS_B, SSM_STATE), jnp.float32)
    xp = x_prompt
    xs = x_sample + grid_pos_embed(x_sample.shape[1], x_sample.dtype)[None]
    ctx_re, ctx_im = [], []
    for l in range(DEPTH):
        mod_ctx = jax.nn.silu(c_ctx)[None, :] @ w_ada[l] + b_ada[l]
        mod_lat = jax.nn.silu(c) @ w_ada[l] + b_ada[l]
        xp, st_re, st_im = trunk_layer(
            xp, mod_ctx, zero_state, zero_state, g_norm_mix[l], w_in[l], g_sgu[l],
            w_spatial[l], b_spatial[l], ssm_a_re[l], ssm_a_im[l], ssm_log_dt[l],
            ssm_b_re[l], ssm_b_im[l], ssm_c_re[l], ssm_c_im[l], ssm_d[l], w_glu[l], b_glu[l],
            w_proj_a[l], w_proj_b[l], w_out[l], g_norm_mlp[l], w_mlp_in[l], w_mlp_out[l])
        ctx_re.append(st_re)
        ctx_im.append(st_im)
        xs, _, _ = trunk_layer(
            xs, mod_lat, state_ssm_re[:, l], state_ssm_im[:, l], g_norm_mix[l], w_in[l], g_sgu[l],
            w_spatial[l], b_spatial[l], ssm_a_re[l], ssm_a_im[l], ssm_log_dt[l],
            ssm_b_re[l], ssm_b_im[l], ssm_c_re[l], ssm_c_im[l], ssm_d[l], w_glu[l], b_glu[l],
            w_proj_a[l], w_proj_b[l], w_out[l], g_norm_mlp[l], w_mlp_in[l], w_mlp_out[l])
    new_ssm_re = jnp.stack(ctx_re, axis=1)
    new_ssm_im = jnp.stack(ctx_im, axis=1)
    y_prompt = rmsnorm(xp, g_final)
    y_sample = rmsnorm(xs, g_final)
    return (y_prompt, y_sample, new_ssm_re, new_ssm_im)
```

```python
import os
import math
import numpy as np
import concourse.bass as bass
import concourse.mybir as mybir
from concourse.bass_utils import run_bass_kernel_spmd

F32 = mybir.dt.float32
BF16 = mybir.dt.bfloat16
I32 = mybir.dt.int32
ALU = mybir.AluOpType
AF = mybir.ActivationFunctionType

D = 2048
DA = 1024
DB = 1024
DFF = 8192
DIN = 7168
G = 64
NP = 64
EPS = 1e-6
ENGS = ("pe", "act", "dve", "pool", "sp")
SEM_ROT = 12000
MAGIC = 12582912.0
TWO_PI = 2.0 * math.pi

STAGE = int(os.environ.get("MK_STAGE", "9"))


class Buf:
    __slots__ = ("name", "w", "r")

    def __init__(self, name):
        self.name = name
        self.w = None
        self.r = {}


class Prog:
    def __init__(self, nc, n_dma_sems=(("sp", 8), ("pool", 6), ("act", 4))):
        self.nc = nc
        self.q = {e: [] for e in ENGS}
        self.cnt = {e: 0 for e in ENGS}
        self.sems = {e: [nc.alloc_semaphore(f"s_{e}_0")] for e in ENGS}
        self.seen = {e: {} for e in ENGS}
        self.dma_ring = {}
        for e, n in n_dma_sems:
            self.dma_ring[e] = dict(sems=[nc.alloc_semaphore(f"d_{e}_{i}") for i in range(n)],
                                    val=[0] * n, pos=0)
        self.out_tokens = []
        self.own = {e: set() for e in ENGS}

    def _need(self, eng, tok):
        if tok is None:
            return
        sem, val = tok
        if eng == "pe" and sem.num in self.own["pe"]:
            return
        if self.seen[eng].get(sem.num, 0) >= val:
            return
        self.seen[eng][sem.num] = val
        self.q[eng].append(lambda e, s=sem, v=val: e.wait_ge(s, v))

    def _next_token(self, eng):
        if self.cnt[eng] >= SEM_ROT:
            self.sems[eng].append(self.nc.alloc_semaphore(f"s_{eng}_{len(self.sems[eng])}"))
            self.cnt[eng] = 0
        self.cnt[eng] += 1
        sem = self.sems[eng][-1]
        self.own[eng].add(sem.num)
        return (sem, self.cnt[eng])

    def _deps(self, eng, reads, writes):
        for b in reads:
            self._need(eng, b.w)
        for b in writes:
            self._need(eng, b.w)
            for t in b.r.values():
                self._need(eng, t)

    def _commit(self, tok, reads, writes):
        for b in reads:
            b.r[tok[0].num] = tok
        for b in writes:
            b.w = tok
            b.r = {}

    def op(self, eng, fn, reads=(), writes=()):
        self._deps(eng, reads, writes)
        tok = self._next_token(eng)
        sem = tok[0]
        self.q[eng].append(lambda e, f=fn, s=sem: f(e).then_inc(s, 1))
        self._commit(tok, reads, writes)
        return tok

    def dma(self, eng, out_ap, in_ap, reads=(), writes=(), is_output=False, **kw):
        ring = self.dma_ring[eng]
        i = ring["pos"]
        ring["pos"] = (i + 1) % len(ring["sems"])
        sem = ring["sems"][i]
        if ring["val"][i] > 0:
            self._need(eng, (sem, ring["val"][i]))
        self._deps(eng, reads, writes)
        ring["val"][i] += 16
        tok = (sem, ring["val"][i])
        self.q[eng].append(
            lambda e, o=out_ap, a=in_ap, s=sem, k=kw: e.dma_start(out=o, in_=a, **k).then_inc(s, 16))
        self._commit(tok, reads, writes)
        if is_output:
            self.out_tokens.append(tok)
        return tok

    def barrier(self):
        last = []
        for e in ENGS:
            if self.cnt[e] > 0:
                last.append((self.sems[e][-1], self.cnt[e]))
        for ring in self.dma_ring.values():
            for sem, v in zip(ring["sems"], ring["val"]):
                if v > 0:
                    last.append((sem, v))
        for e in ENGS:
            for tok in last:
                self._need(e, tok)

    def emit(self):
        nc = self.nc
        for tok in self.out_tokens:
            self._need("sp", tok)
        for e in ENGS:
            if e != "sp" and self.cnt[e] > 0:
                self._need("sp", (self.sems[e][-1], self.cnt[e]))
        with nc.Block() as block:
            @block.tensor
            def _(e):
                for f in self.q["pe"]:
                    f(e)

            @block.scalar
            def _(e):
                for f in self.q["act"]:
                    f(e)

            @block.vector
            def _(e):
                for f in self.q["dve"]:
                    f(e)

            @block.gpsimd
            def _(e):
                for f in self.q["pool"]:
                    f(e)

            @block.sync
            def _(e):
                for f in self.q["sp"]:
                    f(e)


class Arena:
    def __init__(self, nc, words):
        self.t32 = nc.alloc_sbuf_tensor("arena", [128, words], F32)
        self.t16 = self.t32.bitcast(BF16)
        self.ti = self.t32.bitcast(I32)
        self.words = words
        self.top = 0

    def alloc(self, words):
        o = self.top
        self.top += (words + 15) // 16 * 16
        assert self.top <= self.words, (self.top, self.words)
        return o

    def f32(self, off, n):
        return self.t32[:, off:off + n]

    def bf(self, off, n):
        return self.t16[:, 2 * off:2 * off + n]

    def i32(self, off, n):
        return self.ti[:, off:off + n]


def build_program():
    nc = bass.Bass("TRN2", target_bir_lowering=False)

    def din(name, shape):
        return nc.dram_tensor(name, list(shape), F32, kind="ExternalInput").ap()

    def dout(name, shape):
        return nc.dram_tensor(name, list(shape), F32, kind="ExternalOutput").ap()

    xs_own = din("xs_own", [2048, D])
    xs_oth = din("xs_oth", [2048, D])
    xp = din("xp", [1024, D])
    oh = din("oh", [128, 4096])
    cvec = din("cvec", [2, D])
    h0re = din("h0re", [2, G, NP])
    h0im = din("h0im", [2, G, NP])
    w_ada = din("w_ada", [D, 6 * D])
    b_ada = din("b_ada", [6 * D])
    g_mix = din("g_mix", [D])
    w_in = din("w_in", [D, DIN])
    g_sgu = din("g_sgu", [DA])
    w_s = din("w_s", [4, 128, 128])
    b_s = din("b_s", [4, 128])
    a_re = din("a_re", [2, G, NP])
    a_im = din("a_im", [2, G, NP])
    log_dt = din("log_dt", [2, G])
    b_re = din("b_re", [2, G, NP, 16])
    b_im = din("b_im", [2, G, NP, 16])
    c_re = din("c_re", [2, G, 16, NP])
    c_im = din("c_im", [2, G, 16, NP])
    ssm_d = din("ssm_d", [DB])
    w_glu = din("w_glu", [DB, DB])
    b_glu = din("b_glu", [DB])
    w_pa = din("w_pa", [DA, D])
    w_pb = din("w_pb", [DB, D])
    w_out = din("w_out", [D, D])
    g_mlp = din("g_mlp", [D])
    w_mi = din("w_mi", [D, DFF])
    w_mo = din("w_mo", [DFF, D])
    g_fin = din("g_fin", [D])
    yp = dout("yp", [1024, D])
    ys = dout("ys", [2048, D])
    sre = dout("sre", [4, 2, G, NP])
    sim = dout("sim", [4, 2, G, NP])
    smat = nc.dram_tensor("smat", [5, 128, G, 128], BF16,
                          kind="ExternalOutput" if os.environ.get("MK_DBG", "0") == "1" else "Internal").ap()

    P = Prog(nc)
    A = Arena(nc, 53000)
    DBG = os.environ.get("MK_DBG", "0") == "1"
    dbg_names = []

    def dbg(name, ap, bufs):
        if not DBG:
            return
        shape = [int(x) for x in ap.shape]
        dt_ = ap.dtype
        t = nc.dram_tensor("dbg_" + name, shape, dt_, kind="ExternalOutput").ap()
        P.dma("sp", t, ap, reads=bufs, is_output=True)
        dbg_names.append("dbg_" + name)
    psum = nc.alloc_psum_tensor("psum", [128, 4096], F32)
    psum16 = psum.bitcast(BF16)
    pb = [Buf(f"bank{i}") for i in range(8)]

    def ps32(i, n=512, off=0):
        return psum[:, i * 512 + off:i * 512 + off + n]

    def ps16(i, n=1024, off=0):
        return psum16[:, i * 1024 + off:i * 1024 + off + n]

    bank_rr = [0]

    def nextbank():
        b = bank_rr[0]
        bank_rr[0] = (b + 1) % 8
        return b

    quad_rr = [0]

    def nextquad():
        q = quad_rr[0]
        quad_rr[0] = 1 - q
        return q * 4

    alt = [0]

    def evac_eng():
        alt[0] ^= 1
        return "act" if alt[0] else "dve"

    o_id16 = A.alloc(64)
    o_id32 = A.alloc(128)
    o_mod = A.alloc(128)
    o_gate = A.alloc(2 * D)
    o_gfin = A.alloc(D)
    o_bglu = A.alloc(8)
    o_gsgu = A.alloc(8)
    o_dsk = A.alloc(64)
    o_bsbc = A.alloc(512)
    o_wsT = A.alloc(256)
    o_E = A.alloc(1024)
    o_cols = A.alloc(64)
    o_ybT = A.alloc(8192)
    s5p_off = A.alloc(512)
    NRING = 4
    o_ring = [A.alloc(2048) for _ in range(NRING)]
    o_region = A.top
    REGION_WORDS = A.words - o_region

    id16 = A.bf(o_id16, 128)
    id32 = A.f32(o_id32, 128)
    Bconst = Buf("const")
    modc = A.f32(o_mod, 128).rearrange("p (k f m) -> p k f m", k=4, f=16)
    Bmod = Buf("mod")
    gate_bc = A.f32(o_gate, 2 * D)
    Bgate = Buf("gate")
    gfin_bc = A.f32(o_gfin, D)
    bglu_c = A.f32(o_bglu, 8)
    gsgu_c = A.f32(o_gsgu, 8)
    dsk_c = A.f32(o_dsk, 64)
    bs_bc = A.f32(o_bsbc, 512).rearrange("p (g i) -> p g i", g=4)
    wsT = A.bf(o_wsT, 512).rearrange("p (g i) -> p g i", g=4)
    Etab = A.f32(o_E, 1024)
    cols = A.f32(o_cols, 64)
    Bcols = Buf("cols")
    ybT = A.bf(o_ybT, 16384).rearrange("p (k n) -> p k n", k=8)
    BybT = [Buf(f"ybT{k}") for k in range(8)]

    ring_pos = [0]
    ringB = [Buf(f"ring{i}") for i in range(NRING)]

    def ring_next():
        i = ring_pos[0]
        ring_pos[0] = (i + 1) % NRING
        return o_ring[i], ringB[i]

    def wload(dram3, k0, nk, c0, ncol, q="pool"):
        off, B = ring_next()
        dst = A.bf(off, nk * ncol).rearrange("p (k n) -> p k n", k=nk)
        P.dma(q, dst, dram3[:, k0:k0 + nk, c0:c0 + ncol], writes=[B])
        return dst, B

    def wload2(dA, dB_, k0, nk, c0, ncol):
        off, B = ring_next()
        d1 = A.bf(off, nk * ncol).rearrange("p (k n) -> p k n", k=nk)
        d2 = A.bf(off + nk * ncol // 2, nk * ncol).rearrange("p (k n) -> p k n", k=nk)
        P.dma("pool", d1, dA[:, k0:k0 + nk, c0:c0 + ncol], writes=[B])
        P.dma("pool", d2, dB_[:, k0:k0 + nk, c0:c0 + ncol], writes=[B])
        return d1, d2, B

    def wload32(dram3, k0, nk, c0, ncol, q="sp"):
        off, B = ring_next()
        dst = A.f32(off, nk * ncol).rearrange("p (k n) -> p k n", k=nk)
        P.dma(q, dst, dram3[:, k0:k0 + nk, c0:c0 + ncol], writes=[B])
        return dst, B

    w_in3 = w_in.rearrange("(k p) n -> p k n", p=128)
    w_ada3 = w_ada.rearrange("(k p) n -> p k n", p=128)
    w_pa3 = w_pa.rearrange("(k p) n -> p k n", p=128)
    w_pb3 = w_pb.rearrange("(k p) n -> p k n", p=128)
    w_out3 = w_out.rearrange("(k p) n -> p k n", p=128)
    w_mi3 = w_mi.rearrange("(k p) n -> p k n", p=128)
    w_mo3 = w_mo.rearrange("(k p) n -> p k n", p=128)
    w_glu3 = w_glu.rearrange("(k p) n -> p k n", p=128)

    def phase0():
        top = o_region
        o_iot = top; top += 128
        o_tmp = top; top += 2048
        o_sc = top; top += 32
        o_sg = top; top += 32
        o_screp = top; top += 2 * 2048
        o_bcol = top; top += 64
        o_gcol = top; top += 32
        o_ws = top; top += 512
        assert top - o_region <= REGION_WORDS
        Biot, Btmp, Bsc, Bscr, Bbcol, Bgcol, Bws = [Buf(n) for n in "iot tmp sc scr bcol gcol ws".split()]
        iot = A.i32(o_iot, 128)
        iotf = A.f32(o_tmp, 128)
        P.op("pool", lambda e: e.iota(iot, [[1, 128]], base=0, channel_multiplier=-1), writes=[Biot])
        P.op("dve", lambda e: e.tensor_copy(iotf, iot), reads=[Biot], writes=[Btmp])
        P.op("dve", lambda e: e.tensor_scalar(id32, iotf, 0.0, None, ALU.is_equal), reads=[Btmp], writes=[Bconst])
        P.op("dve", lambda e: e.tensor_copy(id16, id32), reads=[Bconst], writes=[Bconst])
        P.dma("sp", gfin_bc, g_fin.partition_broadcast(128), writes=[Bconst])
        P.dma("sp", A.f32(o_bsbc, 512), b_s.rearrange("g i -> (g i)").partition_broadcast(128), writes=[Bconst])
        P.dma("sp", bglu_c, b_glu.rearrange("(k p) -> p k", p=128), writes=[Bconst], allow_slow_non_contiguous=True)
        P.dma("sp", gsgu_c, g_sgu.rearrange("(k p) -> p k", p=128), writes=[Bconst], allow_slow_non_contiguous=True)
        for j in range(8):
            P.dma("sp", A.t32[j * 16:(j + 1) * 16, o_dsk:o_dsk + 64], ssm_d.rearrange("(g c) -> c g", c=16),
                  writes=[Bconst], allow_slow_non_contiguous=True)
        ws32 = A.f32(o_ws, 512).rearrange("p (g j) -> p g j", g=4)
        P.dma("sp", ws32, w_s.rearrange("g i j -> i g j"), writes=[Bws])
        for g4 in range(4):
            bk = nextbank()
            P.op("pe", lambda e, g4=g4, bk=bk: e.transpose(ps32(bk, 128), ws32[:, g4, :], id32),
                 reads=[Bws, Bconst], writes=[pb[bk]])
            P.op("dve", lambda e, g4=g4, bk=bk: e.tensor_copy(wsT[:, g4, :], ps32(bk, 128)),
                 reads=[pb[bk]], writes=[Bconst])
        qi = A.i32(o_tmp, 512)
        qf = A.f32(o_tmp + 512, 512)
        ang = A.f32(o_tmp + 1024, 512)
        kf = A.f32(o_tmp + 1536, 512)
        ri = A.i32(o_iot, 1)
        rf = A.f32(o_iot + 16, 1)
        P.op("pool", lambda e: e.iota(qi, [[1, 512]], base=0, channel_multiplier=0), reads=[Btmp], writes=[Btmp])
        P.op("dve", lambda e: e.tensor_copy(qf, qi), reads=[Btmp], writes=[Btmp])
        P.op("act", lambda e: e.activation(qf, qf, AF.Exp, scale=-math.log(10000.0) / 512.0), reads=[Btmp], writes=[Btmp])
        P.op("pool", lambda e: e.iota(ri, [[1, 1]], base=0, channel_multiplier=1), reads=[Biot], writes=[Biot])
        P.op("dve", lambda e: e.tensor_copy(rf, ri), reads=[Biot], writes=[Biot])
        P.op("dve", lambda e: e.tensor_scalar(A.f32(o_iot + 32, 1), rf, 64.0, -64.0, ALU.is_ge, ALU.mult),
             reads=[Biot], writes=[Biot])
        P.op("dve", lambda e: e.tensor_tensor(rf, rf, A.f32(o_iot + 32, 1), ALU.add), reads=[Biot], writes=[Biot])
        for half, shift in ((0, 0.0), (1, math.pi / 2)):
            P.op("dve", lambda e: e.tensor_scalar(ang, qf, rf, None, ALU.mult), reads=[Btmp, Biot], writes=[Btmp])
            P.op("dve", lambda e, shift=shift: e.tensor_scalar(ang, ang, shift, None, ALU.add), reads=[Btmp], writes=[Btmp])
            sin_reduced(ang, kf, Etab[:, half * 512:(half + 1) * 512], Btmp, Bconst)

    def sin_reduced(ang, kf, out, Bin, Bout):
        Bin = list(Bin) if isinstance(Bin, (list, tuple)) else [Bin]
        P.op("dve", lambda e: e.tensor_scalar(kf, ang, 1.0 / TWO_PI, MAGIC, ALU.mult, ALU.add), reads=Bin, writes=Bin)
        P.op("dve", lambda e: e.tensor_scalar(kf, kf, MAGIC, None, ALU.subtract), reads=Bin, writes=Bin)
        P.op("dve", lambda e: e.scalar_tensor_tensor(ang, kf, -TWO_PI, ang, ALU.mult, ALU.add), reads=Bin, writes=Bin)
        P.op("dve", lambda e: e.tensor_scalar(ang, ang, -math.pi, math.pi, ALU.max, ALU.min), reads=Bin, writes=Bin)
        P.op("act", lambda e: e.activation(out, ang, AF.Sin), reads=Bin, writes=[Bout])

    def phase_mod():
        top = o_region
        o_sc = top; top += 32
        o_sg = top; top += 32
        o_bcol = top; top += 64
        o_gcol = top; top += 32
        Bsc, Bbcol, Bgcol = Buf("sc"), Buf("bcol"), Buf("gcol")
        sc = A.f32(o_sc, 32).rearrange("p (k m) -> p k m", k=16)
        sg = A.f32(o_sg, 32).rearrange("p (k m) -> p k m", k=16)
        for m in range(2):
            P.dma("sp", sc[:, :, m], cvec[m].rearrange("(k p) -> p k", p=128), writes=[Bsc],
                  allow_slow_non_contiguous=True)
        P.op("act", lambda e: e.activation(sg, sc, AF.Sigmoid), reads=[Bsc], writes=[Bsc])
        P.op("dve", lambda e: e.tensor_tensor(sc, sc, sg, ALU.mult), reads=[Bsc], writes=[Bsc])
        bcol = A.f32(o_bcol, 64).rearrange("p (k f) -> p k f", k=4)
        kind_off = [0, D, 3 * D, 4 * D]
        for k in range(4):
            P.dma("sp", bcol[:, k, :], b_ada[kind_off[k]:kind_off[k] + D].rearrange("(f p) -> p f", p=128),
                  writes=[Bbcol], allow_slow_non_contiguous=True)
        gcol = A.f32(o_gcol, 32).rearrange("p (k f) -> p k f", k=2)
        P.dma("sp", gcol[:, 0, :], g_mix.rearrange("(f p) -> p f", p=128), writes=[Bgcol], allow_slow_non_contiguous=True)
        P.dma("sp", gcol[:, 1, :], g_mlp.rearrange("(f p) -> p f", p=128), writes=[Bgcol], allow_slow_non_contiguous=True)
        bk = nextbank()
        first = True
        for k in range(4):
            for ft in range(16):
                wsb, Bw = wload32(w_ada3, 0, 16, kind_off[k] + ft * 128, 128)

                def fn(e, wsb=wsb, k=k, ft=ft, bk=bk):
                    ins = None
                    col = (k * 16 + ft) * 2
                    for kt in range(16):
                        ins = e.matmul(ps32(bk, 2, col), wsb[:, kt, :], sc[:, kt, :],
                                       start=(kt == 0), stop=(kt == 15))
                    return ins
                P.op("pe", fn, reads=[Bw, Bsc], writes=[pb[bk]])
        mraw = ps32(bk, 128).rearrange("p (k f m) -> p k f m", k=4, f=16)
        P.op("dve", lambda e: e.tensor_tensor(modc, mraw, bcol.unsqueeze(3).broadcast_to([128, 4, 16, 2]), ALU.add),
             reads=[pb[bk], Bbcol], writes=[Bmod])
        for k, gi in ((1, 0), (3, 1)):
            P.op("dve", lambda e, k=k, gi=gi: e.scalar_tensor_tensor(
                modc[:, k], modc[:, k], 1.0, gcol[:, gi, :].unsqueeze(2).broadcast_to([128, 16, 2]), ALU.add, ALU.mult),
                reads=[Bmod, Bgcol], writes=[Bmod])

    def compute_gates(ms):
        top = o_region
        o_sc = top; top += 16
        o_sg = top; top += 16
        o_rep = top; top += 2048
        o_brow = top; top += 2 * D
        o_one = top; top += 128
        Bsc, Brep, Bbrow = Buf("gsc"), Buf("grep"), Buf("gbrow")
        sc = A.f32(o_sc, 16)
        sg = A.f32(o_sg, 16)
        P.dma("sp", sc, cvec[ms].rearrange("(k p) -> p k", p=128), writes=[Bsc], allow_slow_non_contiguous=True)
        P.op("act", lambda e: e.activation(sg, sc, AF.Sigmoid), reads=[Bsc], writes=[Bsc])
        P.op("dve", lambda e: e.tensor_tensor(sc, sc, sg, ALU.mult), reads=[Bsc], writes=[Bsc])
        rep = A.f32(o_rep, 2048).rearrange("p (k m) -> p k m", k=16)
        P.op("dve", lambda e: e.tensor_copy(rep, sc.unsqueeze(2).broadcast_to([128, 16, 128])), reads=[Bsc], writes=[Brep])
        brow = A.t32[0:1, o_brow:o_brow + 2 * D]
        ones1 = A.t32[0:1, o_one:o_one + 128]
        P.dma("sp", brow[:, 0:D], b_ada[2 * D:3 * D].rearrange("(o n) -> o n", o=1), writes=[Bbrow])
        P.dma("sp", brow[:, D:2 * D], b_ada[5 * D:6 * D].rearrange("(o n) -> o n", o=1), writes=[Bbrow])
        P.op("dve", lambda e: e.memset(ones1, 1.0), writes=[Brep])
        for gi, coff in ((0, 2 * D), (1, 5 * D)):
            for ch in range(16):
                wsb, Bw = wload32(w_ada3, 0, 16, coff + ch * 128, 128)
                bk = nextbank()

                def fn(e, wsb=wsb, bk=bk, gi=gi, ch=ch):
                    for kt in range(16):
                        e.matmul(ps32(bk, 128), rep[:, kt, :], wsb[:, kt, :], start=(kt == 0), stop=False)
                    return e.matmul(ps32(bk, 128), ones1, brow[:, gi * D + ch * 128:gi * D + ch * 128 + 128],
                                    start=False, stop=True)
                P.op("pe", fn, reads=[Bw, Brep, Bbrow], writes=[pb[bk]])
                P.op("act", lambda e, bk=bk, gi=gi, ch=ch: e.activation(
                    gate_bc[:, gi * D + ch * 128:gi * D + ch * 128 + 128], ps32(bk, 128), AF.Identity),
                    reads=[pb[bk]], writes=[Bgate])

    def norm_to_hT(xs, Bxs, xn, Bxn, hT, BhT, ms, kA, kS, junk, Bjunk):
        P.op("dve", lambda e: e.memset(cols[:, 0:4], 0.0), writes=[Bcols])
        for t in range(4):
            ssc = cols[:, t:t + 1]
            P.op("act", lambda e, t=t, ssc=ssc: e.activation(junk, xs[t], AF.Square, accum_out=ssc),
                 reads=[Bxs[t]], writes=[Bjunk, Bcols])
        P.op("act", lambda e: e.activation(cols[:, 8:12], cols[:, 0:4], AF.Sqrt, bias=EPS, scale=1.0 / D),
             reads=[Bcols], writes=[Bcols])
        P.op("dve", lambda e: e.reciprocal(cols[:, 8:12], cols[:, 8:12]), reads=[Bcols], writes=[Bcols])
        for t in range(4):
            rsc = cols[:, 8 + t:9 + t]
            eng = evac_eng()
            if eng == "act":
                P.op("act", lambda e, t=t, rsc=rsc: e.activation(xn[t], xs[t], AF.Copy, scale=rsc),
                     reads=[Bxs[t], Bcols], writes=[Bxn[t]])
            else:
                P.op("dve", lambda e, t=t, rsc=rsc: e.tensor_scalar(xn[t], xs[t], rsc, None, ALU.mult),
                     reads=[Bxs[t], Bcols], writes=[Bxn[t]])
        transposes_to_hT(xn, Bxn, hT, BhT, ms, kA, kS)

    def transposes_to_hT(xn, Bxn, hT, BhT, ms, kA, kS):
        for ft in range(16):
            bk = nextbank()

            def fn(e, ft=ft, bk=bk):
                ins = None
                for t in range(4):
                    ins = e.transpose(ps16(bk, 128, t * 128), xn[t][:, ft * 128:(ft + 1) * 128], id16)
                return ins
            P.op("pe", fn, reads=list(Bxn) + [Bconst], writes=[pb[bk]])
            eng = evac_eng()
            Ac = modc[:, kA, ft, ms:ms + 1]
            Sc = modc[:, kS, ft, ms:ms + 1]
            if eng == "act":
                P.op("act", lambda e, ft=ft, bk=bk, Ac=Ac, Sc=Sc: e.activation(
                    hT[:, ft, :], ps16(bk, 512), AF.Identity, bias=Sc, scale=Ac),
                    reads=[pb[bk], Bmod], writes=[BhT[ft]])
            else:
                P.op("dve", lambda e, ft=ft, bk=bk, Ac=Ac, Sc=Sc: e.tensor_scalar(
                    hT[:, ft, :], ps16(bk, 512), Ac, Sc, ALU.mult, ALU.add),
                    reads=[pb[bk], Bmod], writes=[BhT[ft]])

    def add_emb(xt_ap, Bx, ohcol, ohsb, Boh):
        P.dma("sp", ohsb, oh[:, ohcol:ohcol + 128], writes=[Boh])
        q = nextquad()
        for hh in range(2):
            for c2 in range(2):
                bk = q + hh * 2 + c2
                P.op("pe", lambda e, hh=hh, c2=c2, bk=bk: e.matmul(
                    ps32(bk), ohsb[hh * 64:(hh + 1) * 64, :], Etab[hh * 64:(hh + 1) * 64, c2 * 512:(c2 + 1) * 512],
                    start=True, stop=True), reads=[Boh, Bconst], writes=[pb[bk]])
        P.op("dve", lambda e, q=q: e.tensor_tensor(xt_ap, xt_ap, psum[:, q * 512:q * 512 + 2048], ALU.add),
             reads=[pb[q], pb[q + 1], pb[q + 2], pb[q + 3], Bx], writes=[Bx])

    def phase2(S):
        top = o_region
        o_xt = top; top += 4 * D
        o_hT = top; top += 4096
        o_R = top; top += 10240
        o_tmp = top; top += 3 * 512
        o_wst = top; top += 4 * 256
        o_oh = top; top += 128
        assert top - o_region <= REGION_WORDS, (top - o_region, REGION_WORDS)
        xt = [A.f32(o_xt + t * D, D) for t in range(4)]
        Bxt = [Buf(f"xt{t}") for t in range(4)]
        hT = A.bf(o_hT, 8192).rearrange("p (k n) -> p k n", k=16)
        BhT = [Buf(f"hT{k}") for k in range(16)]
        uT = A.bf(o_R, 4096).rearrange("p (k n) -> p k n", k=8)
        BuT = [Buf(f"uT{k}") for k in range(8)]
        vtk = [A.bf(o_R + 2048 + t * 512, 1024) for t in range(4)]
        Bv = [Buf(f"v{t}") for t in range(4)]
        yaT = A.bf(o_R + 4096, 4096).rearrange("p (k n) -> p k n", k=8)
        ByaT = [Buf(f"yaT{k}") for k in range(8)]
        mT = A.bf(o_R + 6144, 8192).rearrange("p (k n) -> p k n", k=16)
        BmT = [Buf(f"mT{k}") for k in range(16)]
        aT = A.bf(o_R, 16384).rearrange("p (k n) -> p k n", k=32)
        BaT = [Buf(f"aT{k}") for k in range(32)]
        xn = [A.bf(o_R + t * 1024, D) for t in range(4)]
        Bxn = [Buf(f"xn{t}") for t in range(4)]
        junk = A.bf(o_R + 4096, D)
        Bjunk = Buf("junk")
        tmpA = A.f32(o_tmp, 512)
        tmpB = A.f32(o_tmp + 512, 512)
        tmpC = A.f32(o_tmp + 1024, 512)
        BtA, BtB, BtC = Buf("tA"), Buf("tB"), Buf("tC")
        wst = [A.bf(o_wst + t * 256, 512).rearrange("p (g i) -> p g i", g=4) for t in range(4)]
        Bwst = [Buf(f"wst{t}") for t in range(4)]
        ohsb = A.f32(o_oh, 128)
        Boh = Buf("oh")
        Rall = BuT + Bv + ByaT + BmT + BaT + Bxn + [Bjunk]
        ms = S["ms"]
        g1 = gate_bc[:, 0:D]
        g2 = gate_bc[:, D:2 * D]

        def alias_guard(new_bufs, old_bufs):
            for nb in new_bufs:
                for ob in old_bufs:
                    if ob.w is not None:
                        nb.r[("w", id(ob))] = ob.w
                    for k, t in ob.r.items():
                        nb.r[(k, id(ob))] = t

        for tile in range(S["ntok"] // 512):
            t0 = tile * 512
            alias_guard(Bxn + [Bjunk], BaT + BmT)
            for t in range(4):
                P.dma("sp", xt[t], S["x"][t0 + t * 128:t0 + (t + 1) * 128, :], writes=[Bxt[t]])
                if S["emb"]:
                    add_emb(xt[t], Bxt[t], S["ohoff"] + t0 + t * 128, ohsb, Boh)
            norm_to_hT(xt, Bxt, xn, Bxn, hT, BhT, ms, 1, 0, junk, Bjunk)
            if tile == 0 and S["name"] == "prompt":
                dbg("hT", hT, BhT)
            alias_guard(BuT + Bv + ByaT, Bxn + [Bjunk])
            for ch in range(4):
                wsb, Bw = wload(w_in3, 0, 16, ch * 256, 256)
                for m2 in range(2):
                    mt = ch * 2 + m2
                    bk = nextbank()

                    def fn(e, wsb=wsb, m2=m2, bk=bk):
                        ins = None
                        for kt in range(16):
                            ins = e.matmul(ps32(bk), wsb[:, kt, m2 * 128:(m2 + 1) * 128], hT[:, kt, :],
                                           start=(kt == 0), stop=(kt == 15))
                        return ins
                    P.op("pe", fn, reads=[Bw] + BhT, writes=[pb[bk]])
                    P.op("act", lambda e, mt=mt, bk=bk: e.activation(uT[:, mt, :], ps32(bk), AF.Gelu_apprx_tanh),
                         reads=[pb[bk]], writes=[BuT[mt]])
            for ncn in range(2):
                wA, BwA = wload(w_in3, 0, 8, DA + ncn * 512, 512)
                wB, BwB = wload(w_in3, 8, 8, DA + ncn * 512, 512)
                for t in range(4):
                    bk = nextbank()

                    def fn1(e, wA=wA, t=t, bk=bk):
                        ins = None
                        for kt in range(8):
                            ins = e.matmul(ps32(bk), hT[:, kt, t * 128:(t + 1) * 128], wA[:, kt, :],
                                           start=(kt == 0), stop=False)
                        return ins

                    def fn2(e, wB=wB, t=t, bk=bk):
                        ins = None
                        for kt in range(8):
                            ins = e.matmul(ps32(bk), hT[:, 8 + kt, t * 128:(t + 1) * 128], wB[:, kt, :],
                                           start=False, stop=(kt == 7))
                        return ins
                    P.op("pe", fn1, reads=[BwA] + BhT, writes=[pb[bk]])
                    P.op("pe", fn2, reads=[BwB] + BhT, writes=[pb[bk]])
                    P.op("act", lambda e, t=t, ncn=ncn, bk=bk: e.activation(
                        vtk[t][:, ncn * 512:(ncn + 1) * 512], ps32(bk), AF.Gelu_apprx_tanh),
                        reads=[pb[bk]], writes=[Bv[t]])
            P.op("dve", lambda e: e.memset(cols[:, 16:20], 0.0), writes=[Bcols])
            for t in range(4):
                ssc = cols[:, 16 + t:17 + t]
                P.op("act", lambda e, t=t, ssc=ssc: e.activation(A.bf(o_tmp, 1024), vtk[t], AF.Square, accum_out=ssc),
                     reads=[Bv[t]], writes=[BtA, BtB, Bcols])
            P.op("act", lambda e: e.activation(cols[:, 24:28], cols[:, 16:20], AF.Sqrt, bias=EPS, scale=1.0 / DA),
                 reads=[Bcols], writes=[Bcols])
            P.op("dve", lambda e: e.reciprocal(cols[:, 24:28], cols[:, 24:28]), reads=[Bcols], writes=[Bcols])
            for t in range(4):
                rsc = cols[:, 24 + t:25 + t]
                P.op("dve", lambda e, t=t, rsc=rsc: e.tensor_scalar(wst[t], wsT, rsc, None, ALU.mult),
                     reads=[Bcols, Bconst], writes=[Bwst[t]])
            for mt in range(8):
                ga = mt // 2
                bk = nextbank()

                def fn(e, mt=mt, ga=ga, bk=bk):
                    ins = None
                    for t in range(4):
                        ins = e.matmul(ps32(bk, 128, t * 128), vtk[t][:, mt * 128:(mt + 1) * 128], wst[t][:, ga, :],
                                       start=True, stop=True)
                    return ins
                P.op("pe", fn, reads=Bv + Bwst, writes=[pb[bk]])
                P.op("dve", lambda e, mt=mt, ga=ga, bk=bk: e.scalar_tensor_tensor(
                    tmpA.rearrange("p (t i) -> p t i", t=4), ps32(bk).rearrange("p (t i) -> p t i", t=4),
                    gsgu_c[:, mt:mt + 1], bs_bc[:, ga, :].unsqueeze(1).broadcast_to([128, 4, 128]), ALU.mult, ALU.add),
                    reads=[pb[bk], Bconst], writes=[BtA])
                P.op("dve", lambda e, mt=mt: e.tensor_tensor(yaT[:, mt, :], tmpA, uT[:, mt, :], ALU.mult),
                     reads=[BtA, BuT[mt]], writes=[ByaT[mt]])
            if tile == 0 and S["name"] == "prompt":
                dbg("uT", uT, BuT)
                dbg("v0", vtk[0], [Bv[0]])
                dbg("yaT", yaT, ByaT)
                dbg("wst0", wst[0], [Bwst[0]])
            for ft2 in range(8):
                pA, pB, Bpw = wload2(w_pa3, w_pb3, 0, 8, ft2 * 256, 256)
                for f in range(2):
                    ft = ft2 * 2 + f
                    gA, gB, Bgw = wload2(w_in3[:, :, 3 * DA:3 * DA + D], w_in3[:, :, 3 * DA + D:3 * DA + 2 * D],
                                         0, 16, ft * 128, 128)
                    q = nextquad()
                    bga, bpa, bgb, bpb = q, q + 1, q + 2, q + 3

                    def fga(e, gA=gA, bk=bga):
                        ins = None
                        for kt in range(16):
                            ins = e.matmul(ps32(bk), gA[:, kt, :], hT[:, kt, :], start=(kt == 0), stop=(kt == 15))
                        return ins

                    def fgb(e, gB=gB, bk=bgb):
                        ins = None
                        for kt in range(16):
                            ins = e.matmul(ps32(bk), gB[:, kt, :], hT[:, kt, :], start=(kt == 0), stop=(kt == 15))
                        return ins

                    def fpa(e, pA=pA, f=f, bk=bpa):
                        ins = None
                        for kt in range(8):
                            ins = e.matmul(ps32(bk), pA[:, kt, f * 128:(f + 1) * 128], yaT[:, kt, :],
                                           start=(kt == 0), stop=(kt == 7))
                        return ins

                    def fpb(e, pB=pB, f=f, bk=bpb, t0=t0):
                        ins = None
                        for kt in range(8):
                            ins = e.matmul(ps32(bk), pB[:, kt, f * 128:(f + 1) * 128], ybT[:, kt, S["yoff"] + t0:S["yoff"] + t0 + 512],
                                           start=(kt == 0), stop=(kt == 7))
                        return ins
                    P.op("pe", fga, reads=[Bgw] + BhT, writes=[pb[bga]])
                    P.op("pe", fpa, reads=[Bpw] + ByaT, writes=[pb[bpa]])
                    P.op("pe", fgb, reads=[Bgw] + BhT, writes=[pb[bgb]])
                    P.op("pe", fpb, reads=[Bpw] + BybT, writes=[pb[bpb]])
                    P.op("act", lambda e, bk=bga: e.activation(tmpA, ps32(bk), AF.Sigmoid), reads=[pb[bga]], writes=[BtA])
                    P.op("dve", lambda e, bk=bpa: e.tensor_tensor(tmpA, tmpA, ps32(bk), ALU.mult),
                         reads=[BtA, pb[bpa]], writes=[BtA])
                    P.op("act", lambda e, bk=bgb: e.activation(tmpB, ps32(bk), AF.Sigmoid), reads=[pb[bgb]], writes=[BtB])
                    P.op("dve", lambda e, bk=bpb: e.tensor_tensor(tmpB, tmpB, ps32(bk), ALU.mult),
                         reads=[BtB, pb[bpb]], writes=[BtB])
                    P.op("dve", lambda e, ft=ft: e.tensor_tensor(mT[:, ft, :], tmpA, tmpB, ALU.add),
                         reads=[BtA, BtB], writes=[BmT[ft]])
            if tile == 0 and S["name"] == "prompt":
                dbg("mT", mT, BmT)
            for c in range(4):
                wA, BwA = wload(w_out3, 0, 8, c * 512, 512)
                wB, BwB = wload(w_out3, 8, 8, c * 512, 512)
                for t in range(4):
                    bk = nextbank()

                    def fn1(e, wA=wA, t=t, bk=bk):
                        ins = None
                        for kt in range(8):
                            ins = e.matmul(ps32(bk), mT[:, kt, t * 128:(t + 1) * 128], wA[:, kt, :],
                                           start=(kt == 0), stop=False)
                        return ins

                    def fn2(e, wB=wB, t=t, bk=bk):
                        ins = None
                        for kt in range(8):
                            ins = e.matmul(ps32(bk), mT[:, 8 + kt, t * 128:(t + 1) * 128], wB[:, kt, :],
                                           start=False, stop=(kt == 7))
                        return ins
                    P.op("pe", fn1, reads=[BwA] + BmT, writes=[pb[bk]])
                    P.op("pe", fn2, reads=[BwB] + BmT, writes=[pb[bk]])
                    P.op("dve", lambda e, c=c, bk=bk: e.tensor_tensor(tmpC, ps32(bk), g1[:, c * 512:(c + 1) * 512], ALU.mult),
                         reads=[pb[bk], Bgate], writes=[BtC])
                    P.op("dve", lambda e, c=c, t=t: e.tensor_tensor(
                        xt[t][:, c * 512:(c + 1) * 512], xt[t][:, c * 512:(c + 1) * 512], tmpC, ALU.add),
                        reads=[BtC, Bxt[t]], writes=[Bxt[t]])
            if tile == 0 and S["name"] == "prompt":
                dbg("xmid0", xt[0], [Bxt[0]])
            alias_guard(Bxn + [Bjunk], BuT + Bv + ByaT + BmT)
            norm_to_hT(xt, Bxt, xn, Bxn, hT, BhT, ms, 3, 2, junk, Bjunk)
            if tile == 0 and S["name"] == "prompt":
                dbg("h2T", hT, BhT)
            alias_guard(BaT, Bxn + [Bjunk] + BuT + Bv + ByaT + BmT)
            for hh in range(2):
                for ch in range(16):
                    wsb, Bw = wload(w_mi3, 0, 16, hh * 4096 + ch * 256, 256)
                    for f2 in range(2):
                        fl = ch * 2 + f2
                        bk = nextbank()

                        def fn(e, wsb=wsb, f2=f2, bk=bk):
                            ins = None
                            for kt in range(16):
                                ins = e.matmul(ps32(bk), wsb[:, kt, f2 * 128:(f2 + 1) * 128], hT[:, kt, :],
                                               start=(kt == 0), stop=(kt == 15))
                            return ins
                        P.op("pe", fn, reads=[Bw] + BhT, writes=[pb[bk]])
                        tt, Bt = (tmpA, BtA) if fl % 2 == 0 else (tmpB, BtB)
                        P.op("act", lambda e, bk=bk, tt=tt: e.activation(tt, ps32(bk), AF.Relu), reads=[pb[bk]], writes=[Bt])
                        P.op("dve", lambda e, fl=fl, tt=tt: e.tensor_tensor(aT[:, fl, :], tt, tt, ALU.mult),
                             reads=[Bt], writes=[BaT[fl]])
                for c in range(4):
                    q = nextquad()
                    for f8 in range(4):
                        wsb, Bw = wload(w_mo3, hh * 32 + f8 * 8, 8, c * 512, 512)

                        def fn(e, wsb=wsb, f8=f8, q=q):
                            ins = None
                            for f in range(8):
                                for t in range(4):
                                    ins = e.matmul(ps32(q + t), aT[:, f8 * 8 + f, t * 128:(t + 1) * 128], wsb[:, f, :],
                                                   start=(f8 == 0 and f == 0), stop=(f8 == 3 and f == 7))
                            return ins
                        P.op("pe", fn, reads=[Bw] + BaT[f8 * 8:(f8 + 1) * 8], writes=[pb[q + t] for t in range(4)])
                    for t in range(4):
                        P.op("dve", lambda e, c=c, bk=q + t: e.tensor_tensor(tmpC, ps32(bk), g2[:, c * 512:(c + 1) * 512], ALU.mult),
                             reads=[pb[q + t], Bgate], writes=[BtC])
                        P.op("dve", lambda e, c=c, t=t: e.tensor_tensor(
                            xt[t][:, c * 512:(c + 1) * 512], xt[t][:, c * 512:(c + 1) * 512], tmpC, ALU.add),
                            reads=[BtC, Bxt[t]], writes=[Bxt[t]])
            P.op("dve", lambda e: e.memset(cols[:, 32:36], 0.0), writes=[Bcols])
            for t in range(4):
                ssc = cols[:, 32 + t:33 + t]
                P.op("act", lambda e, t=t, ssc=ssc: e.activation(A.bf(o_hT, D), xt[t], AF.Square, accum_out=ssc),
                     reads=[Bxt[t]], writes=BhT[0:4] + [Bcols])
            P.op("act", lambda e: e.activation(cols[:, 40:44], cols[:, 32:36], AF.Sqrt, bias=EPS, scale=1.0 / D),
                 reads=[Bcols], writes=[Bcols])
            P.op("dve", lambda e: e.reciprocal(cols[:, 40:44], cols[:, 40:44]), reads=[Bcols], writes=[Bcols])
            for t in range(4):
                rsc = cols[:, 40 + t:41 + t]
                P.op("dve", lambda e, t=t, rsc=rsc: e.scalar_tensor_tensor(xt[t], xt[t], rsc, gfin_bc, ALU.mult, ALU.mult),
                     reads=[Bxt[t], Bcols, Bconst], writes=[Bxt[t]])
                P.dma("sp", S["y"][t0 + t * 128:t0 + (t + 1) * 128, :], xt[t], reads=[Bxt[t]], is_output=True)
        return Rall + Bxt + BhT + [BtA, BtB, BtC] + Bwst + [Boh]

    o_s5p = s5p_off
    rho8 = A.f32(o_s5p, 64)
    phi8 = A.f32(o_s5p + 64, 64)
    h0r_s = A.f32(o_s5p + 128, 64)
    h0i_s = A.f32(o_s5p + 192, 64)
    car_r = A.f32(o_s5p + 256, 64)
    car_i = A.f32(o_s5p + 320, 64)
    ini_r = A.f32(o_s5p + 384, 64)
    ini_i = A.f32(o_s5p + 448, 64)
    Bs5p = Buf("s5p")

    def half(ap, d):
        return ap[d * 64:(d + 1) * 64]

    def phase_s5build():
        top = [o_region]

        def al(n):
            o = top[0]
            top[0] += n
            assert top[0] - o_region <= REGION_WORDS, (top[0] - o_region, REGION_WORDS)
            return o
        o_sm = al(1024)
        sm = [A.f32(o_sm + i * 64, 64) for i in range(16)]
        lamre, lamim, dtt, lrd, lid, mag, abre, abim, den, ur, fre, fim, s1, s2, s3, s4 = sm
        o_ex = al(32)
        EX = A.f32(o_ex, 32).rearrange("p (t k) -> p t k", t=4)
        o_k8 = al(16)
        o_big = al(6144)
        o_pwre = al(2048)
        o_pwim = al(2048)
        o_b = al(2048)
        o_fb = al(2048)
        o_ct = al(2048)
        o_cn = al(256)
        o_t = al(2048)
        o_w3 = al(1024)
        o_w2 = al(1024)
        o_m1 = al(512)
        o_m1t = al(256)
        o_msk = al(256)
        Bsm, Bex, Bbig, Bpw, Bb, Bfb, Bct, Bcn, Bt, Bw3, Bw2, Bm1, Bm1t, Bmsk = [
            Buf(n) for n in "sm ex big pw b fb ct cn t w3 w2 m1 m1t msk".split()]
        nat = A.f32(o_t, 128)
        Bnat = Buf("nat")
        for (src, dst, Bd) in ((a_re, lamre, Bsm), (a_im, lamim, Bsm), (h0re, h0r_s, Bs5p), (h0im, h0i_s, Bs5p)):
            P.dma("sp", nat[0:64].rearrange("p (d s) -> p d s", d=2), src.rearrange("d g s -> g d s"), writes=[Bnat])
            bk = nextbank()
            P.op("pe", lambda e, bk=bk: e.transpose(ps32(bk, 64), nat[0:64], id32[0:64, 0:64]), reads=[Bnat, Bconst], writes=[pb[bk]])
            P.op("dve", lambda e, dst=dst, bk=bk: e.tensor_copy(dst, ps32(bk, 64)), reads=[pb[bk]], writes=[Bd])
        for d in range(2):
            P.dma("sp", half(dtt, d), log_dt[d].partition_broadcast(64), writes=[Bsm])
            for g8 in range(8):
                gsl = slice(g8 * 8, (g8 + 1) * 8)
                P.dma("sp", half(A.f32(o_b, 1024), d).rearrange("p (g c) -> p g c", g=64)[:, gsl, :],
                      b_re[d, gsl].rearrange("g s c -> s g c"), writes=[Bb])
                P.dma("sp", half(A.f32(o_b + 1024, 1024), d).rearrange("p (g c) -> p g c", g=64)[:, gsl, :],
                      b_im[d, gsl].rearrange("g s c -> s g c"), writes=[Bb])
        Bre_ = A.f32(o_b, 1024).rearrange("p (g c) -> p g c", g=64)
        Bim_ = A.f32(o_b + 1024, 1024).rearrange("p (g c) -> p g c", g=64)
        Fbre = A.f32(o_fb, 1024).rearrange("p (g c) -> p g c", g=64)
        Fbim = A.f32(o_fb + 1024, 1024).rearrange("p (g c) -> p g c", g=64)
        CTre = A.f32(o_ct, 1024).rearrange("p (g c) -> p g c", g=64)
        CTim = A.f32(o_ct + 1024, 1024).rearrange("p (g c) -> p g c", g=64)
        V = lambda fn, r, w: P.op("dve", fn, reads=r, writes=w)
        P.op("act", lambda e: e.activation(dtt, dtt, AF.Exp), reads=[Bsm], writes=[Bsm])
        V(lambda e: e.tensor_tensor(lrd, lamre, dtt, ALU.mult), [Bsm], [Bsm])
        V(lambda e: e.tensor_tensor(lid, lamim, dtt, ALU.mult), [Bsm], [Bsm])
        P.op("act", lambda e: e.activation(mag, lrd, AF.Exp), reads=[Bsm], writes=[Bsm])
        P.op("act", lambda e: e.activation(rho8, lrd, AF.Exp, scale=8.0), reads=[Bsm], writes=[Bs5p])
        V(lambda e: e.tensor_scalar(s1, lid, 8.0 / TWO_PI, MAGIC, ALU.mult, ALU.add), [Bsm], [Bsm])
        V(lambda e: e.tensor_scalar(s1, s1, MAGIC, None, ALU.subtract), [Bsm], [Bsm])
        V(lambda e: e.tensor_scalar(s2, lid, 8.0, None, ALU.mult), [Bsm], [Bsm])
        V(lambda e: e.scalar_tensor_tensor(phi8, s1, -TWO_PI, s2, ALU.mult, ALU.add), [Bsm], [Bs5p])
        V(lambda e: e.tensor_copy(s3, lid), [Bsm], [Bsm])
        sin_reduced(s3, s4, abim, Bsm, Bsm)
        V(lambda e: e.tensor_scalar(s3, lid, math.pi / 2, None, ALU.add), [Bsm], [Bsm])
        sin_reduced(s3, s4, abre, Bsm, Bsm)
        V(lambda e: e.tensor_tensor(abre, abre, mag, ALU.mult), [Bsm], [Bsm])
        V(lambda e: e.tensor_tensor(abim, abim, mag, ALU.mult), [Bsm], [Bsm])
        V(lambda e: e.tensor_tensor(den, lamre, lamre, ALU.mult), [Bsm], [Bsm])
        V(lambda e: e.tensor_tensor(s1, lamim, lamim, ALU.mult), [Bsm], [Bsm])
        V(lambda e: e.tensor_tensor(den, den, s1, ALU.add), [Bsm], [Bsm])
        V(lambda e: e.reciprocal(den, den), [Bsm], [Bsm])
        V(lambda e: e.tensor_scalar(ur, abre, -1.0, None, ALU.add), [Bsm], [Bsm])
        V(lambda e: e.tensor_tensor(s1, ur, lamre, ALU.mult), [Bsm], [Bsm])
        V(lambda e: e.tensor_tensor(s2, abim, lamim, ALU.mult), [Bsm], [Bsm])
        V(lambda e: e.tensor_tensor(fre, s1, s2, ALU.add), [Bsm], [Bsm])
        V(lambda e: e.tensor_tensor(fre, fre, den, ALU.mult), [Bsm], [Bsm])
        V(lambda e: e.tensor_tensor(s1, abim, lamre, ALU.mult), [Bsm], [Bsm])
        V(lambda e: e.tensor_tensor(s2, ur, lamim, ALU.mult), [Bsm], [Bsm])
        V(lambda e: e.tensor_tensor(fim, s1, s2, ALU.subtract), [Bsm], [Bsm])
        V(lambda e: e.tensor_tensor(fim, fim, den, ALU.mult), [Bsm], [Bsm])
        SP_ = int(os.environ.get("MK_S5PART", "9"))
        if SP_ <= 1:
            return
        t1 = A.f32(o_t, 1024)
        t2 = A.f32(o_t + 1024, 1024)
        t1g = t1.rearrange("p (g c) -> p g c", g=64)
        t2g = t2.rearrange("p (g c) -> p g c", g=64)
        fre_b = fre.unsqueeze(2).broadcast_to([128, 64, 16])
        fim_b = fim.unsqueeze(2).broadcast_to([128, 64, 16])
        V(lambda e: e.tensor_tensor(t1g, Bre_, fre_b, ALU.mult), [Bsm, Bb], [Bt])
        V(lambda e: e.tensor_tensor(t2g, Bim_, fim_b, ALU.mult), [Bsm, Bb], [Bt])
        V(lambda e: e.tensor_tensor(Fbre, t1g, t2g, ALU.subtract), [Bt], [Bfb])
        V(lambda e: e.tensor_tensor(t1g, Bim_, fre_b, ALU.mult), [Bsm, Bb, Bfb], [Bt])
        V(lambda e: e.tensor_tensor(t2g, Bre_, fim_b, ALU.mult), [Bsm, Bb], [Bt])
        V(lambda e: e.tensor_tensor(Fbim, t1g, t2g, ALU.add), [Bt], [Bfb])
        k8i = A.i32(o_k8, 8)
        k8 = A.f32(o_k8 + 8, 8)
        P.op("pool", lambda e: e.iota(k8i, [[1, 8]], base=0, channel_multiplier=0), writes=[Bex])
        V(lambda e: e.tensor_copy(k8, k8i), [Bex], [Bex])
        spec = {0: [(1.0, 0.0), (-1.0, 0.0), (-1.0, 7.0), (1.0, 1.0)],
                1: [(-1.0, 0.0), (1.0, 0.0), (1.0, 0.0), (-1.0, 8.0)]}
        for d in range(2):
            for tb in range(4):
                a_, b_ = spec[d][tb]
                V(lambda e, d=d, tb=tb, a_=a_, b_=b_: e.tensor_scalar(half(EX[:, tb, :], d), half(k8, d), a_, b_, ALU.mult, ALU.add),
                  [Bex], [Bex])
        if SP_ <= 2:
            return
        PWmag = A.f32(o_big, 2048)
        ang = A.f32(o_big + 2048, 2048)
        kfs = A.f32(o_big + 4096, 2048)
        PWre = A.f32(o_pwre, 2048)
        PWim = A.f32(o_pwim, 2048)
        v3 = lambda ap: ap.rearrange("p (g x) -> p g x", g=64)
        EXf = A.f32(o_ex, 32)
        lrd_b = lrd.unsqueeze(2).broadcast_to([128, 64, 32])
        lid_b = lid.unsqueeze(2).broadcast_to([128, 64, 32])
        EX_b = EXf.unsqueeze(1).broadcast_to([128, 64, 32])
        V(lambda e: e.tensor_tensor(v3(PWmag), lrd_b, EX_b, ALU.mult), [Bsm, Bex], [Bbig])
        P.op("act", lambda e: e.activation(PWmag, PWmag, AF.Exp), reads=[Bbig], writes=[Bbig])
        V(lambda e: e.tensor_tensor(v3(ang), lid_b, EX_b, ALU.mult), [Bsm, Bex], [Bbig])
        sin_reduced(ang, kfs, PWim, Bbig, Bpw)
        V(lambda e: e.tensor_tensor(v3(ang), lid_b, EX_b, ALU.mult), [Bsm, Bex, Bbig], [Bbig])
        V(lambda e: e.tensor_scalar(ang, ang, math.pi / 2, None, ALU.add), [Bbig], [Bbig])
        sin_reduced(ang, kfs, PWre, Bbig, Bpw)
        V(lambda e: e.tensor_tensor(PWre, PWre, PWmag, ALU.mult), [Bpw, Bbig], [Bpw])
        V(lambda e: e.tensor_tensor(PWim, PWim, PWmag, ALU.mult), [Bpw, Bbig], [Bpw])
        if SP_ <= 3:
            return
        mski = A.i32(o_m1t, 256)
        msk2 = A.f32(o_msk, 256)
        pj_i = A.i32(o_k8, 1)
        pj = A.f32(o_k8 + 8, 1)
        P.op("pool", lambda e: e.iota(mski, [[0, 2], [1, 8], [0, 16]], base=0, channel_multiplier=0), reads=[Bm1t], writes=[Bm1t])
        V(lambda e: e.tensor_copy(msk2, mski), [Bm1t], [Bmsk])
        P.op("pool", lambda e: e.iota(pj_i, [[1, 1]], base=0, channel_multiplier=1), reads=[Bex], writes=[Bex])
        V(lambda e: e.tensor_single_scalar(pj_i, pj_i, 4, ALU.arith_shift_right), [Bex], [Bex])
        V(lambda e: e.tensor_copy(pj, pj_i), [Bex], [Bex])
        V(lambda e: e.tensor_scalar(msk2, msk2, pj, None, ALU.subtract), [Bmsk, Bex], [Bmsk])
        V(lambda e: e.tensor_scalar(msk2[:, 0:128], msk2[:, 0:128], 0.0, None, ALU.is_ge), [Bmsk], [Bmsk])
        V(lambda e: e.tensor_scalar(msk2[:, 128:256], msk2[:, 128:256], 0.0, None, ALU.is_le), [Bmsk], [Bmsk])
        if SP_ <= 4:
            return
        P.barrier()
        for gt in range(8):
            for ri, csrc in ((0, c_re), (1, c_im)):
                cn = A.f32(o_cn + ri * 128, 128)
                P.dma("sp", cn.rearrange("p (d s) -> p d s", d=2), csrc[:, gt * 8:(gt + 1) * 8].rearrange("d g c s -> (g c) d s"),
                      writes=[Bcn])
                bk = nextbank()
                P.op("pe", lambda e, cn=cn, bk=bk: e.transpose(ps32(bk, 128), cn, id32), reads=[Bcn, Bconst], writes=[pb[bk]])
                dst = (CTre if ri == 0 else CTim)[:, gt * 8:(gt + 1) * 8, :]
                V(lambda e, dst=dst, bk=bk: e.tensor_copy(dst, ps32(bk, 128).rearrange("p (g c) -> p g c", g=8)), [pb[bk]], [Bct])
        if SP_ <= 5:
            return
        bufs6 = [A.f32(o_big + i * 1024, 1024) for i in range(6)]
        Lre, Lim, Rmre, nRmim, W2r, W2i = bufs6
        v4 = lambda ap: ap.rearrange("p (g k c) -> p g k c", g=8, k=8)
        PW4r = PWre.rearrange("p (g t k) -> p g t k", g=64, t=4)
        PW4i = PWim.rearrange("p (g t k) -> p g t k", g=64, t=4)
        W3re16 = A.bf(o_w3, 1024)
        W3im16 = A.bf(o_w3 + 512, 1024)
        W2re16 = A.bf(o_w2, 1024)
        W2im16 = A.bf(o_w2 + 512, 1024)
        M116 = A.bf(o_m1, 1024)
        m1t = A.f32(o_m1t, 256)

        def cmul(out_re, out_im, ar, ai, br, bi, rd, wr, neg_im=False, out_is_bf=False):
            V(lambda e: e.tensor_tensor(v4(t1), ar, br, ALU.mult), rd, [Bt])
            V(lambda e: e.tensor_tensor(v4(t2), ai, bi, ALU.mult), rd, [Bt])
            V(lambda e: e.tensor_tensor(out_re, t1, t2, ALU.subtract), [Bt], wr)
            V(lambda e: e.tensor_tensor(v4(t1), ar, bi, ALU.mult), rd + wr, [Bt])
            V(lambda e: e.tensor_tensor(v4(t2), ai, br, ALU.mult), rd, [Bt])
            if neg_im:
                V(lambda e: e.scalar_tensor_tensor(out_im, t1, -1.0, t2, ALU.mult, ALU.subtract), [Bt], wr)
            else:
                V(lambda e: e.tensor_tensor(out_im, t1, t2, ALU.add), [Bt], wr)

        for gt in range(8):
            gs = slice(gt * 8, (gt + 1) * 8)
            ctr = CTre[:, gs, :].unsqueeze(2).broadcast_to([128, 8, 8, 16])
            cti = CTim[:, gs, :].unsqueeze(2).broadcast_to([128, 8, 8, 16])
            fbr = Fbre[:, gs, :].unsqueeze(2).broadcast_to([128, 8, 8, 16])
            fbi = Fbim[:, gs, :].unsqueeze(2).broadcast_to([128, 8, 8, 16])
            pw = lambda P4, tb: P4[:, gs, tb, :].unsqueeze(3).broadcast_to([128, 8, 8, 16])
            BL, BR, BW2 = Buf("L"), Buf("R"), Buf("W2p")
            for b_ in (BL, BR, BW2):
                b_.r[("g", 0)] = Bbig.w
            cmul(Lre, Lim, ctr, cti, pw(PW4r, 0), pw(PW4i, 0), [Bct, Bpw], [BL])
            cmul(Rmre, nRmim, fbr, fbi, pw(PW4r, 1), pw(PW4i, 1), [Bfb, Bpw], [BR], neg_im=True)
            cmul(W2r, W2i, fbr, fbi, pw(PW4r, 2), pw(PW4i, 2), [Bfb, Bpw], [BW2])
            cmul(W3re16, W3im16, ctr, cti, pw(PW4r, 3), pw(PW4i, 3), [Bct, Bpw], [Bw3], neg_im=True)
            if SP_ <= 6:
                continue
            P.dma("sp", smat[3, :, gs, :], W3re16.rearrange("p (g n) -> p g n", g=8), reads=[Bw3])
            P.dma("sp", smat[4, :, gs, :], W3im16.rearrange("p (g n) -> p g n", g=8), reads=[Bw3])
            if SP_ <= 7:
                continue
            for gl in range(8):
                for ri, src, dst in ((0, W2r, W2re16), (1, W2i, W2im16)):
                    bk = nextbank()
                    P.op("pe", lambda e, src=src, gl=gl, bk=bk: e.transpose(ps32(bk, 128), src[:, gl * 128:(gl + 1) * 128], id32),
                         reads=[BW2, Bconst], writes=[pb[bk]])
                    P.op("act", lambda e, dst=dst, gl=gl, bk=bk: e.activation(dst[:, gl * 128:(gl + 1) * 128], ps32(bk, 128), AF.Copy),
                         reads=[pb[bk]], writes=[Bw2])
                if SP_ <= 8:
                    continue
                bk = nextbank()
                bk2 = nextbank()

                def fm(e, gl=gl, bk=bk, bk2=bk2):
                    ins = None
                    for d in range(2):
                        cs = slice(gl * 128, (gl + 1) * 128)
                        ob = ps32(bk if d == 0 else bk2, 128)
                        e.matmul(ob, half(Rmre[:, cs], d), half(Lre[:, cs], d), start=True, stop=False)
                        ins = e.matmul(ob, half(nRmim[:, cs], d), half(Lim[:, cs], d), start=False, stop=True)
                    return ins
                P.op("pe", fm, reads=[BL, BR], writes=[pb[bk], pb[bk2]])
                V(lambda e, bk=bk: e.tensor_tensor(m1t[:, 0:128], ps32(bk, 128), msk2[:, 0:128], ALU.mult), [pb[bk], Bmsk], [Bm1t])
                V(lambda e, bk2=bk2: e.tensor_tensor(m1t[:, 128:256], ps32(bk2, 128), msk2[:, 128:256], ALU.mult), [pb[bk2], Bmsk], [Bm1t])
                V(lambda e: e.tensor_tensor(m1t[:, 0:128], m1t[:, 0:128], m1t[:, 128:256], ALU.add), [Bm1t], [Bm1t])
                V(lambda e, gl=gl, gt=gt: e.scalar_tensor_tensor(M116[:, gl * 128:(gl + 1) * 128], id32, dsk_c[:, gt * 8 + gl:gt * 8 + gl + 1],
                                                                m1t[:, 0:128], ALU.mult, ALU.add), [Bm1t, Bconst], [Bm1])
            P.dma("sp", smat[0, :, gs, :], M116.rearrange("p (g n) -> p g n", g=8), reads=[Bm1])
            P.dma("sp", smat[1, :, gs, :], W2re16.rearrange("p (g n) -> p g n", g=8), reads=[Bw2])
            P.dma("sp", smat[2, :, gs, :], W2im16.rearrange("p (g n) -> p g n", g=8), reads=[Bw2])

    def phase1(S):
        top = [o_region]

        def al(n):
            o = top[0]
            top[0] += n
            assert top[0] - o_region <= REGION_WORDS, (top[0] - o_region, REGION_WORDS)
            return o
        o_U = al(8192)
        o_sel = al(4096)
        o_work = top[0]
        NU = 256 if S["name"] == "sample" else 128
        U = A.bf(o_U, 64 * NU).rearrange("p (g n) -> p g n", g=64)
        BU = [Buf(f"U{g}") for g in range(8)]
        Sel = A.bf(o_sel, 8192).rearrange("p (m n) -> p m n", m=64)
        Bsel = Buf("sel")
        o_w = al(256)
        wi = A.i32(o_w, 128)
        wf = A.f32(o_w, 128)
        pi_ = A.i32(o_w + 128, 1)
        pf_ = A.f32(o_w + 144, 1)
        Bw_ = Buf("selw")
        P.op("pool", lambda e: e.iota(wi, [[1, 128]], base=0, channel_multiplier=-1), writes=[Bw_])
        P.op("pool", lambda e: e.iota(pi_, [[1, 1]], base=0, channel_multiplier=1), writes=[Bw_])
        P.op("dve", lambda e: e.tensor_single_scalar(pi_, pi_, 4, ALU.arith_shift_right), reads=[Bw_], writes=[Bw_])
        P.op("dve", lambda e: e.tensor_copy(pf_, pi_), reads=[Bw_], writes=[Bw_])
        P.op("dve", lambda e: e.tensor_copy(wf, wi), reads=[Bw_], writes=[Bw_])
        P.op("dve", lambda e: e.tensor_scalar(pf_, pf_, 1024.0, None, ALU.mult), reads=[Bw_], writes=[Bw_])
        P.op("dve", lambda e: e.tensor_scalar(wf, wf, pf_, None, ALU.add), reads=[Bw_], writes=[Bw_])
        for a in range(8):
            for b in range(8):
                P.op("dve", lambda e, a=a, b=b: e.tensor_scalar(Sel[:, a * 8 + b, :], wf, float(16 * (b - a) + 1024 * a), None, ALU.is_equal),
                     reads=[Bw_], writes=[Bsel])
        top[0] = o_work
        P.barrier()

        def build_U(xsrc, ntok, emb, ohoff):
            t_ = [o_work]

            def al2(n):
                o = t_[0]
                t_[0] += n
                assert t_[0] - o_region <= REGION_WORDS, (t_[0] - o_region, REGION_WORDS)
                return o
            o_xb = al2(2 * D)
            o_xn = al2(4096)
            o_hT = al2(4096)
            o_xbT = al2(2048)
            o_oh = al2(128)
            xblk = [A.f32(o_xb + (t % 2) * D, D) for t in range(4)]
            Bxb2 = [Buf("xb0"), Buf("xb1")]
            Bxb = [Bxb2[t % 2] for t in range(4)]
            xn = [A.bf(o_xn + t * 1024, D) for t in range(4)]
            Bxn = [Buf(f"p1xn{t}") for t in range(4)]
            hT = A.bf(o_hT, 8192).rearrange("p (k n) -> p k n", k=16)
            BhT = [Buf(f"p1hT{k}") for k in range(16)]
            xbT = A.bf(o_xbT, 4096).rearrange("p (k n) -> p k n", k=8)
            BxbT = [Buf(f"xbT{k}") for k in range(8)]
            junk = A.bf(o_xbT, D)
            Bjunk = Buf("p1junk")
            ohsb = A.f32(o_oh, 128)
            Boh = Buf("p1oh")
            ms = S["ms"]
            for tile in range(ntok // 512):
                t0 = tile * 512
                for b_ in BxbT:
                    Bjunk.r[("x", id(b_))] = b_.w
                    for k_, v_ in b_.r.items():
                        Bjunk.r[(k_, id(b_))] = v_
                for t in range(4):
                    P.dma("sp", xblk[t], xsrc[t0 + t * 128:t0 + (t + 1) * 128, :], writes=[Bxb[t]])
                    if emb:
                        add_emb(xblk[t], Bxb[t], ohoff + t0 + t * 128, ohsb, Boh)
                    ssc = cols[:, t:t + 1]
                    rsc = cols[:, 8 + t:9 + t]
                    P.op("dve", lambda e, ssc=ssc: e.memset(ssc, 0.0), writes=[Bcols])
                    P.op("act", lambda e, t=t, ssc=ssc: e.activation(junk, xblk[t], AF.Square, accum_out=ssc),
                         reads=[Bxb[t]], writes=[Bjunk, Bcols])
                    P.op("act", lambda e, ssc=ssc, rsc=rsc: e.activation(rsc, ssc, AF.Sqrt, bias=EPS, scale=1.0 / D),
                         reads=[Bcols], writes=[Bcols])
                    P.op("dve", lambda e, rsc=rsc: e.reciprocal(rsc, rsc), reads=[Bcols], writes=[Bcols])
                    P.op("dve", lambda e, t=t, rsc=rsc: e.tensor_scalar(xn[t], xblk[t], rsc, None, ALU.mult),
                         reads=[Bxb[t], Bcols], writes=[Bxn[t]])
                for b_ in BxbT:
                    b_.r[("j", 0)] = Bjunk.w
                transposes_to_hT(xn, Bxn, hT, BhT, ms, 1, 0)
                for ch in range(4):
                    wsb, Bw = wload(w_in3, 0, 16, 2 * DA + ch * 256, 256)
                    for m2 in range(2):
                        mt = ch * 2 + m2
                        bk = nextbank()

                        def fn(e, wsb=wsb, m2=m2, bk=bk):
                            ins = None
                            for kt in range(16):
                                ins = e.matmul(ps32(bk), wsb[:, kt, m2 * 128:(m2 + 1) * 128], hT[:, kt, :],
                                               start=(kt == 0), stop=(kt == 15))
                            return ins
                        P.op("pe", fn, reads=[Bw] + BhT, writes=[pb[bk]])
                        if evac_eng() == "act":
                            P.op("act", lambda e, mt=mt, bk=bk: e.activation(xbT[:, mt, :], ps32(bk), AF.Copy),
                                 reads=[pb[bk]], writes=[BxbT[mt]])
                        else:
                            P.op("dve", lambda e, mt=mt, bk=bk: e.tensor_copy(xbT[:, mt, :], ps32(bk)),
                                 reads=[pb[bk]], writes=[BxbT[mt]])
                for gt in range(8):
                    bk = nextbank()

                    def fr(e, gt=gt, bk=bk):
                        ins = None
                        for gl in range(8):
                            for j in range(8):
                                ins = e.matmul(ps32(bk, 64, gl * 64), Sel[:, gl * 8 + j, :], xbT[:, gt, j:512:8],
                                               start=(j == 0), stop=(j == 7))
                        return ins
                    P.op("pe", fr, reads=[Bsel, BxbT[gt]], writes=[pb[bk]])
                    dst = U[:, gt * 8:(gt + 1) * 8, tile * 64:(tile + 1) * 64]
                    if evac_eng() == "act":
                        P.op("act", lambda e, dst=dst, bk=bk: e.activation(dst, ps32(bk).rearrange("p (g n) -> p g n", g=8), AF.Copy),
                             reads=[pb[bk]], writes=[BU[gt]])
                    else:
                        P.op("dve", lambda e, dst=dst, bk=bk: e.tensor_copy(dst, ps32(bk).rearrange("p (g n) -> p g n", g=8)),
                             reads=[pb[bk]], writes=[BU[gt]])

        def scan_stream(kind, N, nseq, L):
            t_ = [o_work]

            def al2(n):
                o = t_[0]
                t_[0] += n
                assert t_[0] - o_region <= REGION_WORDS, (t_[0] - o_region, REGION_WORDS)
                return o
            bg = 1024 // N
            slots = [al2(1024) for _ in range(9)]
            cosT, sinT, p1, p2, tre, tim, D0, Tre, Tim = [A.f32(o, 1024) for o in slots]
            Hre, Him = tre, tim
            o_hsh = al2(1024)
            Hshr = A.bf(o_hsh, 1024)
            Hshi = A.bf(o_hsh + 512, 1024)
            o_smb = al2(2560)
            SMb = A.bf(o_smb, 5 * bg * 128).rearrange("p (k g n) -> p k g n", k=5, g=bg)
            o_ysb = al2(1024)
            Ysb = A.bf(o_ysb, 8 * N).rearrange("p (g n) -> p g n", g=8)
            o_fs = al2(512)
            FSr = A.f32(o_fs, 256).rearrange("p (s g) -> p s g", s=4)
            FSi = A.f32(o_fs + 256, 256).rearrange("p (s g) -> p s g", s=4)
            o_idx = al2(256)
            idxi = A.i32(o_idx, L)
            idxf = A.f32(o_idx, L)
            Bc, Bs, Bp1, Bp2, Btr, Bti, BD0, BTr, BTi, Bhsh, Bsmb, Bysb, Bfs, Bidx = [
                Buf(n) for n in "cos sin p1 p2 tre tim D0 Tre Tim hsh smb ysb fs idx".split()]
            V = lambda fn, r, w: P.op("dve", fn, reads=r, writes=w)
            G_ = lambda fn, r, w: P.op("pool", fn, reads=r, writes=w)
            v4 = lambda ap: ap.rearrange("p (g s l) -> p g s l", g=bg, s=nseq)
            v3 = lambda ap: ap.rearrange("p (g x) -> p g x", g=bg)
            P.op("pool", lambda e: e.iota(idxi, [[1, L]], base=1, channel_multiplier=0), writes=[Bidx])
            V(lambda e: e.tensor_copy(idxf, idxi), [Bidx], [Bidx])
            V(lambda e: e.tensor_scalar(half(idxf, 1), half(idxf, 1), -1.0, float(L + 1), ALU.mult, ALU.add), [Bidx], [Bidx])
            if kind != "prompt":
                for (ini, h0s, car) in ((ini_r, h0r_s, car_r), (ini_i, h0i_s, car_i)):
                    if kind == "other":
                        V(lambda e, ini=ini, h0s=h0s: e.tensor_tensor(ini, h0s, rho8, ALU.mult), [Bs5p], [Bs5p])
                    else:
                        V(lambda e, ini=ini, h0s=h0s: e.tensor_tensor(half(ini, 0), half(h0s, 0), half(rho8, 0), ALU.mult), [Bs5p], [Bs5p])
                        V(lambda e, ini=ini, car=car: e.tensor_tensor(half(ini, 1), half(car, 1), half(rho8, 1), ALU.mult), [Bs5p], [Bs5p])
            def batch(bi):
                g0 = bi * bg
                gs = slice(g0, g0 + bg)
                P.dma("sp", SMb, smat[:, :, gs, :].rearrange("k p g n -> p k g n"), writes=[Bsmb])
                q = nextquad()
                for ri in range(2):
                    def fg(e, ri=ri, q=q, g0=g0):
                        ins = None
                        for gl in range(bg):
                            ins = e.matmul(psum[:, (q + 2 * ri) * 512 + gl * N:(q + 2 * ri) * 512 + (gl + 1) * N],
                                           SMb[:, 1 + ri, gl, :], U[:, g0 + gl, 0:N], start=True, stop=True)
                        return ins
                    P.op("pe", fg, reads=[Bsmb, BU[g0 // 8]], writes=[pb[q + 2 * ri], pb[q + 2 * ri + 1]])
                Gre = psum[:, q * 512:q * 512 + 1024]
                Gim = psum[:, (q + 2) * 512:(q + 2) * 512 + 1024]
                BGr = [pb[q], pb[q + 1]]
                BGi = [pb[q + 2], pb[q + 3]]
                ph_b = phi8[:, gs].unsqueeze(2).broadcast_to([128, bg, L])
                ix_b = idxf.unsqueeze(1).broadcast_to([128, bg, L])
                a3 = lambda ap: ap[:, 0:bg * L].rearrange("p (g l) -> p g l", g=bg)
                V(lambda e: e.tensor_tensor(a3(p1), ph_b, ix_b, ALU.mult), [Bs5p, Bidx], [Bp1])
                sin_reduced(p1[:, 0:bg * L], p2[:, 0:bg * L], sinT[:, 0:bg * L], Bp1, Bs)
                V(lambda e: e.tensor_tensor(a3(p1), ph_b, ix_b, ALU.mult), [Bs5p, Bidx, Bs], [Bp1])
                V(lambda e: e.tensor_scalar(p1[:, 0:bg * L], p1[:, 0:bg * L], math.pi / 2, None, ALU.add), [Bp1], [Bp1])
                sin_reduced(p1[:, 0:bg * L], p2[:, 0:bg * L], cosT[:, 0:bg * L], Bp1, Bc)
                cb = a3(cosT).unsqueeze(2).broadcast_to([128, bg, nseq, L])
                sb = a3(sinT).unsqueeze(2).broadcast_to([128, bg, nseq, L])
                V(lambda e: e.tensor_tensor(v4(p1), v4(Gre), cb, ALU.mult), BGr + [Bc, Bp1], [Bp1])
                V(lambda e: e.tensor_tensor(v4(p2), v4(Gim), sb, ALU.mult), BGi + [Bs, Bp1], [Bp2])
                V(lambda e: e.tensor_tensor(tre, p1, p2, ALU.add), [Bp1, Bp2], [Btr])
                V(lambda e: e.tensor_tensor(v4(p1), v4(Gim), cb, ALU.mult), BGi + [Bc, Btr], [Bp1])
                V(lambda e: e.tensor_tensor(v4(p2), v4(Gre), sb, ALU.mult), BGr + [Bs, Btr], [Bp2])
                V(lambda e: e.tensor_tensor(tim, p1, p2, ALU.subtract), [Bp1, Bp2], [Bti])
                V(lambda e: e.tensor_copy(v3(D0), rho8[:, gs].unsqueeze(2).broadcast_to([128, bg, N])), [Bs5p], [BD0])
                V(lambda e: e.memset(half(v4(D0), 0)[:, :, :, 0:1], 0.0), [], [BD0])
                V(lambda e: e.memset(half(v4(D0), 1)[:, :, :, L - 1:L], 0.0), [], [BD0])
                if kind != "prompt":
                    for (tt, Bt_, ini) in ((tre, Btr, ini_r), (tim, Bti, ini_i)):
                        V(lambda e, tt=tt, ini=ini: e.tensor_tensor(half(v3(tt), 0)[:, :, 0:1], half(v3(tt), 0)[:, :, 0:1],
                                                                   half(ini, 0)[:, gs].unsqueeze(2), ALU.add), [Bs5p, Bt_], [Bt_])
                        V(lambda e, tt=tt, ini=ini: e.tensor_tensor(half(v3(tt), 1)[:, :, N - 1:N], half(v3(tt), 1)[:, :, N - 1:N],
                                                                   half(ini, 1)[:, gs].unsqueeze(2), ALU.add), [Bs5p, Bt_], [Bt_])
                for (src, Bsrc, dst, Bdst) in ((tre, Btr, Tre, BTr), (tim, Bti, Tim, BTi)):
                    V(lambda e, src=src, dst=dst: e.tensor_tensor_scan(half(dst, 0), half(D0, 0), half(src, 0), 0.0, ALU.mult, ALU.add),
                      [Bsrc, BD0], [Bdst])
                    V(lambda e, src=src, dst=dst: e.tensor_tensor_scan(half(dst, 1)[:, ::-1], half(D0, 1)[:, ::-1], half(src, 1)[:, ::-1],
                                                                      0.0, ALU.mult, ALU.add), [Bsrc, BD0], [Bdst])
                if kind == "other":
                    c0 = half(a3(cosT), 1)[:, :, 0:1]
                    s0 = half(a3(sinT), 1)[:, :, 0:1]
                    tr0 = half(v3(Tre), 1)[:, :, 0:1]
                    ti0 = half(v3(Tim), 1)[:, :, 0:1]
                    q1 = half(v3(p1), 1)[:, :, 0:1]
                    q2 = half(v3(p2), 1)[:, :, 0:1]
                    V(lambda e: e.tensor_tensor(q1, c0, tr0, ALU.mult), [Bc, BTr], [Bp1])
                    V(lambda e: e.tensor_tensor(q2, s0, ti0, ALU.mult), [Bs, BTi], [Bp2])
                    V(lambda e: e.tensor_tensor(half(car_r, 1)[:, gs].unsqueeze(2), q1, q2, ALU.subtract), [Bp1, Bp2], [Bs5p])
                    V(lambda e: e.tensor_tensor(q1, s0, tr0, ALU.mult), [Bs, BTr, Bs5p], [Bp1])
                    V(lambda e: e.tensor_tensor(q2, c0, ti0, ALU.mult), [Bc, BTi, Bs5p], [Bp2])
                    V(lambda e: e.tensor_tensor(half(car_i, 1)[:, gs].unsqueeze(2), q1, q2, ALU.add), [Bp1, Bp2], [Bs5p])
                    return
                if kind == "prompt" and bi == 0:
                    dbg("idxf", idxf, [Bidx])
                    dbg("cosT", cosT, [Bc]); dbg("sinT", sinT, [Bs]); dbg("tre", tre, [Btr]); dbg("tim", tim, [Bti])
                    dbg("D0", D0, [BD0]); dbg("Tre", Tre, [BTr]); dbg("Tim", Tim, [BTi])
                V(lambda e: e.tensor_tensor(v4(p1), v4(Tre), cb, ALU.mult), [BTr, Bc], [Bp1])
                V(lambda e: e.tensor_tensor(v4(p2), v4(Tim), sb, ALU.mult), [BTi, Bs], [Bp2])
                V(lambda e: e.tensor_tensor(Hre, p1, p2, ALU.subtract), [Bp1, Bp2], [Btr])
                V(lambda e: e.tensor_tensor(v4(p1), v4(Tre), sb, ALU.mult), [BTr, Bs, Btr], [Bp1])
                V(lambda e: e.tensor_tensor(v4(p2), v4(Tim), cb, ALU.mult), [BTi, Bc, Btr], [Bp2])
                V(lambda e: e.tensor_tensor(Him, p1, p2, ALU.add), [Bp1, Bp2], [Bti])
                for (Hs, Hh, BH, h0s, car) in ((Hshr, Hre, Btr, h0r_s, car_r), (Hshi, Him, Bti, h0i_s, car_i)):
                    V(lambda e, Hs=Hs, Hh=Hh: e.tensor_copy(half(v4(Hs), 0)[:, :, :, 1:L], half(v4(Hh), 0)[:, :, :, 0:L - 1]), [BH], [Bhsh])
                    V(lambda e, Hs=Hs, Hh=Hh: e.tensor_copy(half(v4(Hs), 1)[:, :, :, 0:L - 1], half(v4(Hh), 1)[:, :, :, 1:L]), [BH], [Bhsh])
                    if kind == "prompt":
                        V(lambda e, Hs=Hs: e.memset(half(v4(Hs), 0)[:, :, :, 0:1], 0.0), [], [Bhsh])
                        V(lambda e, Hs=Hs: e.memset(half(v4(Hs), 1)[:, :, :, L - 1:L], 0.0), [], [Bhsh])
                    else:
                        V(lambda e, Hs=Hs, h0s=h0s: e.tensor_copy(half(v3(Hs), 0)[:, :, 0:1], half(h0s, 0)[:, gs].unsqueeze(2)), [Bs5p], [Bhsh])
                        V(lambda e, Hs=Hs, car=car: e.tensor_copy(half(v3(Hs), 1)[:, :, N - 1:N], half(car, 1)[:, gs].unsqueeze(2)), [Bs5p], [Bhsh])
                if kind == "prompt":
                    for (FS, Hh, BH) in ((FSr, Hre, Btr), (FSi, Him, Bti)):
                        V(lambda e, FS=FS, Hh=Hh: e.tensor_copy(half(FS, 0)[:, :, gs].rearrange("p s g -> p g s"),
                                                                half(v4(Hh), 0)[:, :, :, L - 1]), [BH], [Bfs])
                        V(lambda e, FS=FS, Hh=Hh: e.tensor_copy(half(FS, 1)[:, :, gs].rearrange("p s g -> p g s"),
                                                                half(v4(Hh), 1)[:, :, :, 0]), [BH], [Bfs])
                if kind == "prompt" and bi == 0:
                    dbg("Hre", Hre, [Btr]); dbg("Him", Him, [Bti]); dbg("Hshr", Hshr, [Bhsh]); dbg("Hshi", Hshi, [Bhsh])
                for gl in range(bg):
                    g = g0 + gl
                    bk = nextbank()

                    def fy(e, gl=gl, g=g, bk=bk):
                        e.matmul(ps32(bk, N), SMb[:, 0, gl, :], U[:, g, 0:N], start=True, stop=False)
                        e.matmul(ps32(bk, N), SMb[:, 3, gl, :], v3(Hshr)[:, gl, :], start=False, stop=False)
                        return e.matmul(ps32(bk, N), SMb[:, 4, gl, :], v3(Hshi)[:, gl, :], start=False, stop=True)
                    P.op("pe", fy, reads=[Bsmb, BU[g // 8], Bhsh], writes=[pb[bk]])
                    P.op("act", lambda e, g=g, bk=bk: e.activation(Ysb[:, g % 8, :], ps32(bk, N), AF.Copy), reads=[pb[bk]], writes=[Bysb])
                if kind == "prompt" and bi == 0:
                    dbg("Ysb", Ysb, [Bysb])
                if (g0 + bg) % 8 == 0:
                    gt = g0 // 8
                    for nb in range(N // 64):
                        bk = nextbank()

                        def fb(e, nb=nb, bk=bk):
                            ins = None
                            for i in range(8):
                                for gl in range(8):
                                    ins = e.matmul(ps32(bk)[:, i:512:8], Sel[:, i * 8 + gl, :], Ysb[:, gl, nb * 64:(nb + 1) * 64],
                                                   start=(gl == 0), stop=(gl == 7))
                            return ins
                        P.op("pe", fb, reads=[Bsel, Bysb], writes=[pb[bk]])
                        P.op("act", lambda e, gt=gt, nb=nb, bk=bk: e.activation(ybT[:, gt, nb * 512:(nb + 1) * 512], ps32(bk), AF.Gelu_apprx_tanh),
                             reads=[pb[bk]], writes=[BybT[gt]])

            for bi in range(64 // bg):
                batch(bi)
            if kind == "prompt":
                fsT = A.f32(slots[0], 128)
                BfsT = Buf("fsT")
                for (FS, dst) in ((FSr, sre), (FSi, sim)):
                    for sq in range(4):
                        bk = nextbank()
                        P.op("pe", lambda e, FS=FS, sq=sq, bk=bk: e.transpose(ps32(bk, 128)[0:64], FS[:, sq, :], id32),
                             reads=[Bfs, Bconst], writes=[pb[bk]])
                        V(lambda e, bk=bk: e.tensor_copy(fsT[0:64], ps32(bk, 128)[0:64]), [pb[bk], Bc], [BfsT])
                        P.dma("sp", dst[sq].rearrange("d g s -> g d s"), fsT[0:64].rearrange("p (d s) -> p d s", d=2),
                              reads=[BfsT], is_output=True)

        def glu(ntok):
            t_ = [o_work]

            def al2(n):
                o = t_[0]
                t_[0] += n
                return o
            o_wg = al2(4096)
            o_yt = al2(2048)
            o_sg = al2(1024)
            wg = A.bf(o_wg, 8192).rearrange("p (k n) -> p k n", k=8)
            Bwg = Buf("wglu")
            ytmp = A.bf(o_yt, 4096).rearrange("p (k n) -> p k n", k=8)
            Byt = Buf("ytmp")
            sgs = [A.f32(o_sg, 512), A.f32(o_sg + 512, 512)]
            Bsg = [Buf("sg0"), Buf("sg1")]
            for kk in range(2):
                P.dma("pool", wg[:, kk * 4:(kk + 1) * 4, :], w_glu3[:, kk * 4:(kk + 1) * 4, :], writes=[Bwg])
            for tile in range(ntok // 512):
                ts = slice(tile * 512, (tile + 1) * 512)
                for mt in range(8):
                    bk = nextbank()

                    def fgl(e, mt=mt, bk=bk, ts=ts):
                        ins = None
                        for kt in range(8):
                            ins = e.matmul(ps32(bk), wg[:, kt, mt * 128:(mt + 1) * 128], ybT[:, kt, ts], start=(kt == 0), stop=(kt == 7))
                        return ins
                    P.op("pe", fgl, reads=[Bwg] + BybT, writes=[pb[bk]])
                    sg, Bs_ = sgs[mt % 2], Bsg[mt % 2]
                    P.op("act", lambda e, sg=sg, mt=mt, bk=bk: e.activation(sg, ps32(bk), AF.Sigmoid, bias=bglu_c[:, mt:mt + 1]),
                         reads=[pb[bk], Bconst], writes=[Bs_])
                    P.op("dve", lambda e, sg=sg, mt=mt, ts=ts: e.tensor_tensor(ytmp[:, mt, :], sg, ybT[:, mt, ts], ALU.mult),
                         reads=[Bs_, BybT[mt]], writes=[Byt])
                P.op("dve", lambda e, ts=ts: e.tensor_copy(ybT[:, :, ts], ytmp), reads=[Byt], writes=BybT)

        if S["name"] == "prompt":
            build_U(xp, 1024, False, 0)
            P.barrier()
            dbg("U", U, BU)
            dbg("s5p", A.f32(o_s5p, 512), [Bs5p])
            scan_stream("prompt", 128, 4, 32)
            P.barrier()
            dbg("yg", ybT[:, :, 0:1024], BybT)
        else:
            build_U(xs_oth, 2048, True, 2048)
            P.barrier()
            scan_stream("other", 256, 1, 256)
            P.barrier()
            build_U(xs_own, 2048, True, 0)
            P.barrier()
            scan_stream("own", 256, 1, 256)
        P.barrier()
        glu(S["ntok"])

    phase0()
    P.barrier()
    phase_mod()
    P.barrier()
    sets = [
        dict(name="prompt", ms=0, emb=False, x=xp, y=yp, ntok=1024, yoff=0, ohoff=0),
        dict(name="sample", ms=1, emb=True, x=xs_own, y=ys, ntok=2048, yoff=0, ohoff=0),
    ]
    if STAGE >= 2:
        phase_s5build()
        P.barrier()
    else:
        P.op("dve", lambda e: e.memset(ybT, 0.0), writes=BybT)
    for S in sets:
        if STAGE >= 3:
            P.barrier()
            phase1(S)
        P.barrier()
        compute_gates(S["ms"])
        P.barrier()
        phase2(S)
    P.emit()
    nc._dbg_names = dbg_names
    return nc


def _core_inputs(r, inp):
    b, hf = r // 2, r % 2
    f = np.ascontiguousarray
    xs = inp["x_sample"][b]
    if hf == 0:
        own, oth = xs[0:2048], xs[2048:4096]
        pos = np.arange(4096)
    else:
        own, oth = xs[4095:2047:-1], xs[2047::-1]
        pos = 4095 - np.arange(4096)
    xpr = inp["x_prompt"][4 * r:4 * r + 4]
    if hf == 1:
        xpr = xpr[:, ::-1]
    ohm = np.zeros((128, 4096), np.float32)
    ohm[pos // 64, np.arange(4096)] = 1.0
    ohm[64 + pos % 64, np.arange(4096)] = 1.0
    dsel = slice(None) if hf == 0 else slice(None, None, -1)
    w_s = inp["w_spatial"][0]
    b_s = inp["b_spatial"][0]
    if hf == 1:
        w_s = w_s[:, ::-1, ::-1]
        b_s = b_s[:, ::-1]
    m = {
        "xs_own": f(own), "xs_oth": f(oth), "xp": f(xpr.reshape(1024, D)), "oh": ohm,
        "cvec": f(np.stack([inp["c_ctx"], inp["c"][b]])),
        "h0re": f(inp["state_ssm_re"][b, 0][dsel]), "h0im": f(inp["state_ssm_im"][b, 0][dsel]),
        "w_ada": inp["w_ada"][0], "b_ada": inp["b_ada"][0], "g_mix": inp["g_norm_mix"][0],
        "w_in": inp["w_in"][0], "g_sgu": inp["g_sgu"][0], "w_s": f(w_s), "b_s": f(b_s),
        "a_re": f(inp["ssm_a_re"][0][dsel]), "a_im": f(inp["ssm_a_im"][0][dsel]),
        "log_dt": f(inp["ssm_log_dt"][0][dsel]),
        "b_re": f(inp["ssm_b_re"][0][dsel]), "b_im": f(inp["ssm_b_im"][0][dsel]),
        "c_re": f(inp["ssm_c_re"][0][dsel]), "c_im": f(inp["ssm_c_im"][0][dsel]),
        "ssm_d": inp["ssm_d"][0], "w_glu": inp["w_glu"][0], "b_glu": inp["b_glu"][0],
        "w_pa": inp["w_proj_a"][0], "w_pb": inp["w_proj_b"][0], "w_out": inp["w_out"][0],
        "g_mlp": inp["g_norm_mlp"][0], "w_mi": inp["w_mlp_in"][0], "w_mo": inp["w_mlp_out"][0],
        "g_fin": inp["g_final"],
    }
    return {k: np.ascontiguousarray(np.asarray(v, dtype=np.float32)) for k, v in m.items()}


_NC_CACHE = {}


def kernel(**inputs):
    inp = {k: np.asarray(v) for k, v in inputs.items()}
    if "nc" not in _NC_CACHE:
        _NC_CACHE["nc"] = build_program()
    nc = _NC_CACHE["nc"]
    ncores = int(os.environ.get("MK_NCORES", "8"))
    in_maps = [_core_inputs(r, inp) for r in range(ncores)]
    res = run_bass_kernel_spmd(nc, in_maps, core_ids=list(range(ncores)))
    if os.environ.get("MK_DBG", "0") == "1":
        _NC_CACHE["dbg"] = {k: np.asarray(res.results[0][k]).astype(np.float32) for k in list(nc._dbg_names) + ["smat"]}
    y_prompt = np.zeros((32, 256, D), np.float32)
    y_sample = np.zeros((4, 4096, D), np.float32)
    new_re = np.zeros((32, 1, 2, G, NP), np.float32)
    new_im = np.zeros((32, 1, 2, G, NP), np.float32)
    for r in range(ncores):
        o = res.results[r]
        b, hf = r // 2, r % 2
        ypr = o["yp"].reshape(4, 256, D)
        ysr = o["ys"]
        s_re = o["sre"]
        s_im = o["sim"]
        if hf == 1:
            ypr = ypr[:, ::-1]
            ysr = ysr[::-1]
            s_re = s_re[:, ::-1]
            s_im = s_im[:, ::-1]
            y_sample[b, 2048:4096] = ysr
        else:
            y_sample[b, 0:2048] = ysr
        y_prompt[4 * r:4 * r + 4] = ypr
        new_re[4 * r:4 * r + 4, 0] = s_re
        new_im[4 * r:4 * r + 4, 0] = s_im
    return (y_prompt, y_sample, new_re, new_im)
```

```python
import os
import math
import numpy as np
import concourse.bass as bass
import concourse.mybir as mybir
from concourse.bass_utils import run_bass_kernel_spmd

F32 = mybir.dt.float32
BF16 = mybir.dt.bfloat16
I32 = mybir.dt.int32
ALU = mybir.AluOpType
AF = mybir.ActivationFunctionType

D = 2048
DA = 1024
DB = 1024
DFF = 8192
DIN = 7168
G = 64
NP = 64
EPS = 1e-6
ENGS = ("pe", "act", "dve", "pool", "sp")
SEM_ROT = 12000
MAGIC = 12582912.0
TWO_PI = 2.0 * math.pi

STAGE = int(os.environ.get("MK_STAGE", "9"))


class Buf:
    __slots__ = ("name", "w", "r")

    def __init__(self, name):
        self.name = name
        self.w = None
        self.r = {}


class Prog:
    def __init__(self, nc, n_dma_sems=(("sp", 8), ("pool", 6), ("act", 4))):
        self.nc = nc
        self.q = {e: [] for e in ENGS}
        self.cnt = {e: 0 for e in ENGS}
        self.sems = {e: [nc.alloc_semaphore(f"s_{e}_0")] for e in ENGS}
        self.seen = {e: {} for e in ENGS}
        self.dma_ring = {}
        for e, n in n_dma_sems:
            self.dma_ring[e] = dict(sems=[nc.alloc_semaphore(f"d_{e}_{i}") for i in range(n)],
                                    val=[0] * n, pos=0)
        self.out_tokens = []
        self.own = {e: set() for e in ENGS}

    def _need(self, eng, tok):
        if tok is None:
            return
        sem, val = tok
        if eng == "pe" and sem.num in self.own["pe"]:
            return
        if self.seen[eng].get(sem.num, 0) >= val:
            return
        self.seen[eng][sem.num] = val
        self.q[eng].append(lambda e, s=sem, v=val: e.wait_ge(s, v))

    def _next_token(self, eng):
        if self.cnt[eng] >= SEM_ROT:
            self.sems[eng].append(self.nc.alloc_semaphore(f"s_{eng}_{len(self.sems[eng])}"))
            self.cnt[eng] = 0
        self.cnt[eng] += 1
        sem = self.sems[eng][-1]
        self.own[eng].add(sem.num)
        return (sem, self.cnt[eng])

    def _deps(self, eng, reads, writes):
        for b in reads:
            self._need(eng, b.w)
        for b in writes:
            self._need(eng, b.w)
            for t in b.r.values():
                self._need(eng, t)

    def _commit(self, tok, reads, writes):
        for b in reads:
            b.r[tok[0].num] = tok
        for b in writes:
            b.w = tok
            b.r = {}

    def op(self, eng, fn, reads=(), writes=()):
        self._deps(eng, reads, writes)
        tok = self._next_token(eng)
        sem = tok[0]
        self.q[eng].append(lambda e, f=fn, s=sem: f(e).then_inc(s, 1))
        self._commit(tok, reads, writes)
        return tok

    def dma(self, eng, out_ap, in_ap, reads=(), writes=(), is_output=False, **kw):
        ring = self.dma_ring[eng]
        i = ring["pos"]
        ring["pos"] = (i + 1) % len(ring["sems"])
        sem = ring["sems"][i]
        if ring["val"][i] > 0:
            self._need(eng, (sem, ring["val"][i]))
        self._deps(eng, reads, writes)
        ring["val"][i] += 16
        tok = (sem, ring["val"][i])
        self.q[eng].append(
            lambda e, o=out_ap, a=in_ap, s=sem, k=kw: e.dma_start(out=o, in_=a, **k).then_inc(s, 16))
        self._commit(tok, reads, writes)
        if is_output:
            self.out_tokens.append(tok)
        return tok

    def barrier(self):
        last = []
        for e in ENGS:
            if self.cnt[e] > 0:
                last.append((self.sems[e][-1], self.cnt[e]))
        for ring in self.dma_ring.values():
            for sem, v in zip(ring["sems"], ring["val"]):
                if v > 0:
                    last.append((sem, v))
        for e in ENGS:
            for tok in last:
                self._need(e, tok)

    def emit(self):
        nc = self.nc
        for tok in self.out_tokens:
            self._need("sp", tok)
        for e in ENGS:
            if e != "sp" and self.cnt[e] > 0:
                self._need("sp", (self.sems[e][-1], self.cnt[e]))
        with nc.Block() as block:
            @block.tensor
            def _(e):
                for f in self.q["pe"]:
                    f(e)

            @block.scalar
            def _(e):
                for f in self.q["act"]:
                    f(e)

            @block.vector
            def _(e):
                for f in self.q["dve"]:
                    f(e)

            @block.gpsimd
            def _(e):
                for f in self.q["pool"]:
                    f(e)

            @block.sync
            def _(e):
                for f in self.q["sp"]:
                    f(e)


class Arena:
    def __init__(self, nc, words):
        self.t32 = nc.alloc_sbuf_tensor("arena", [128, words], F32)
        self.t16 = self.t32.bitcast(BF16)
        self.ti = self.t32.bitcast(I32)
        self.words = words
        self.top = 0

    def alloc(self, words):
        o = self.top
        self.top += (words + 15) // 16 * 16
        assert self.top <= self.words, (self.top, self.words)
        return o

    def f32(self, off, n):
        return self.t32[:, off:off + n]

    def bf(self, off, n):
        return self.t16[:, 2 * off:2 * off + n]

    def i32(self, off, n):
        return self.ti[:, off:off + n]


def build_program():
    nc = bass.Bass("TRN2", target_bir_lowering=False)

    def din(name, shape):
        return nc.dram_tensor(name, list(shape), F32, kind="ExternalInput").ap()

    def dout(name, shape):
        return nc.dram_tensor(name, list(shape), F32, kind="ExternalOutput").ap()

    xs_own = din("xs_own", [2048, D])
    xs_oth = din("xs_oth", [2048, D])
    xp = din("xp", [1024, D])
    oh = din("oh", [128, 4096])
    cvec = din("cvec", [2, D])
    h0re = din("h0re", [2, G, NP])
    h0im = din("h0im", [2, G, NP])
    w_ada = din("w_ada", [D, 6 * D])
    b_ada = din("b_ada", [6 * D])
    g_mix = din("g_mix", [D])
    w_in = din("w_in", [D, DIN])
    g_sgu = din("g_sgu", [DA])
    w_s = din("w_s", [4, 128, 128])
    b_s = din("b_s", [4, 128])
    a_re = din("a_re", [2, G, NP])
    a_im = din("a_im", [2, G, NP])
    log_dt = din("log_dt", [2, G])
    b_re = din("b_re", [2, G, NP, 16])
    b_im = din("b_im", [2, G, NP, 16])
    c_re = din("c_re", [2, G, 16, NP])
    c_im = din("c_im", [2, G, 16, NP])
    ssm_d = din("ssm_d", [DB])
    w_glu = din("w_glu", [DB, DB])
    b_glu = din("b_glu", [DB])
    w_pa = din("w_pa", [DA, D])
    w_pb = din("w_pb", [DB, D])
    w_out = din("w_out", [D, D])
    g_mlp = din("g_mlp", [D])
    w_mi = din("w_mi", [D, DFF])
    w_mo = din("w_mo", [DFF, D])
    g_fin = din("g_fin", [D])
    yp = dout("yp", [1024, D])
    ys = dout("ys", [2048, D])
    sre = dout("sre", [4, 2, G, NP])
    sim = dout("sim", [4, 2, G, NP])
    smat = nc.dram_tensor("smat", [5, 128, G, 128], BF16,
                          kind="ExternalOutput" if os.environ.get("MK_DBG", "0") == "1" else "Internal").ap()

    P = Prog(nc)
    A = Arena(nc, 53000)
    DBG = os.environ.get("MK_DBG", "0") == "1"
    dbg_names = []

    def dbg(name, ap, bufs):
        if not DBG:
            return
        shape = [int(x) for x in ap.shape]
        dt_ = ap.dtype
        t = nc.dram_tensor("dbg_" + name, shape, dt_, kind="ExternalOutput").ap()
        P.dma("sp", t, ap, reads=bufs, is_output=True)
        dbg_names.append("dbg_" + name)
    psum = nc.alloc_psum_tensor("psum", [128, 4096], F32)
    psum16 = psum.bitcast(BF16)
    pb = [Buf(f"bank{i}") for i in range(8)]

    def ps32(i, n=512, off=0):
        return psum[:, i * 512 + off:i * 512 + off + n]

    def ps16(i, n=1024, off=0):
        return psum16[:, i * 1024 + off:i * 1024 + off + n]

    bank_rr = [0]

    def nextbank():
        b = bank_rr[0]
        bank_rr[0] = (b + 1) % 8
        return b

    quad_rr = [0]

    def nextquad():
        q = quad_rr[0]
        quad_rr[0] = 1 - q
        return q * 4

    alt = [0]

    def evac_eng():
        alt[0] ^= 1
        return "act" if alt[0] else "dve"

    o_id16 = A.alloc(64)
    o_id32 = A.alloc(128)
    o_mod = A.alloc(128)
    o_gate = A.alloc(2 * D)
    o_gfin = A.alloc(D)
    o_bglu = A.alloc(8)
    o_gsgu = A.alloc(8)
    o_dsk = A.alloc(64)
    o_bsbc = A.alloc(512)
    o_wsT = A.alloc(256)
    o_E = A.alloc(1024)
    o_cols = A.alloc(64)
    o_ybT = A.alloc(8192)
    s5p_off = A.alloc(512)
    NRING = 4
    o_ring = [A.alloc(2048) for _ in range(NRING)]
    o_region = A.top
    REGION_WORDS = A.words - o_region

    id16 = A.bf(o_id16, 128)
    id32 = A.f32(o_id32, 128)
    Bconst = Buf("const")
    modc = A.f32(o_mod, 128).rearrange("p (k f m) -> p k f m", k=4, f=16)
    Bmod = Buf("mod")
    gate_bc = A.f32(o_gate, 2 * D)
    Bgate = Buf("gate")
    gfin_bc = A.f32(o_gfin, D)
    bglu_c = A.f32(o_bglu, 8)
    gsgu_c = A.f32(o_gsgu, 8)
    dsk_c = A.f32(o_dsk, 64)
    bs_bc = A.f32(o_bsbc, 512).rearrange("p (g i) -> p g i", g=4)
    wsT = A.bf(o_wsT, 512).rearrange("p (g i) -> p g i", g=4)
    Etab = A.f32(o_E, 1024)
    cols = A.f32(o_cols, 64)
    Bcols = Buf("cols")
    ybT = A.bf(o_ybT, 16384).rearrange("p (k n) -> p k n", k=8)
    BybT = [Buf(f"ybT{k}") for k in range(8)]

    ring_pos = [0]
    ringB = [Buf(f"ring{i}") for i in range(NRING)]

    def ring_next():
        i = ring_pos[0]
        ring_pos[0] = (i + 1) % NRING
        return o_ring[i], ringB[i]

    def wload(dram3, k0, nk, c0, ncol, q="pool"):
        off, B = ring_next()
        dst = A.bf(off, nk * ncol).rearrange("p (k n) -> p k n", k=nk)
        P.dma(q, dst, dram3[:, k0:k0 + nk, c0:c0 + ncol], writes=[B])
        return dst, B

    def wload2(dA, dB_, k0, nk, c0, ncol):
        off, B = ring_next()
        d1 = A.bf(off, nk * ncol).rearrange("p (k n) -> p k n", k=nk)
        d2 = A.bf(off + nk * ncol // 2, nk * ncol).rearrange("p (k n) -> p k n", k=nk)
        P.dma("pool", d1, dA[:, k0:k0 + nk, c0:c0 + ncol], writes=[B])
        P.dma("pool", d2, dB_[:, k0:k0 + nk, c0:c0 + ncol], writes=[B])
        return d1, d2, B

    def wload32(dram3, k0, nk, c0, ncol, q="sp"):
        off, B = ring_next()
        dst = A.f32(off, nk * ncol).rearrange("p (k n) -> p k n", k=nk)
        P.dma(q, dst, dram3[:, k0:k0 + nk, c0:c0 + ncol], writes=[B])
        return dst, B

    w_in3 = w_in.rearrange("(k p) n -> p k n", p=128)
    w_ada3 = w_ada.rearrange("(k p) n -> p k n", p=128)
    w_pa3 = w_pa.rearrange("(k p) n -> p k n", p=128)
    w_pb3 = w_pb.rearrange("(k p) n -> p k n", p=128)
    w_out3 = w_out.rearrange("(k p) n -> p k n", p=128)
    w_mi3 = w_mi.rearrange("(k p) n -> p k n", p=128)
    w_mo3 = w_mo.rearrange("(k p) n -> p k n", p=128)
    w_glu3 = w_glu.rearrange("(k p) n -> p k n", p=128)

    def phase0():
        top = o_region
        o_iot = top; top += 128
        o_tmp = top; top += 2048
        o_sc = top; top += 32
        o_sg = top; top += 32
        o_screp = top; top += 2 * 2048
        o_bcol = top; top += 64
        o_gcol = top; top += 32
        o_ws = top; top += 512
        assert top - o_region <= REGION_WORDS
        Biot, Btmp, Bsc, Bscr, Bbcol, Bgcol, Bws = [Buf(n) for n in "iot tmp sc scr bcol gcol ws".split()]
        iot = A.i32(o_iot, 128)
        iotf = A.f32(o_tmp, 128)
        P.op("pool", lambda e: e.iota(iot, [[1, 128]], base=0, channel_multiplier=-1), writes=[Biot])
        P.op("dve", lambda e: e.tensor_copy(iotf, iot), reads=[Biot], writes=[Btmp])
        P.op("dve", lambda e: e.tensor_scalar(id32, iotf, 0.0, None, ALU.is_equal), reads=[Btmp], writes=[Bconst])
        P.op("dve", lambda e: e.tensor_copy(id16, id32), reads=[Bconst], writes=[Bconst])
        P.dma("sp", gfin_bc, g_fin.partition_broadcast(128), writes=[Bconst])
        P.dma("sp", A.f32(o_bsbc, 512), b_s.rearrange("g i -> (g i)").partition_broadcast(128), writes=[Bconst])
        P.dma("sp", bglu_c, b_glu.rearrange("(k p) -> p k", p=128), writes=[Bconst], allow_slow_non_contiguous=True)
        P.dma("sp", gsgu_c, g_sgu.rearrange("(k p) -> p k", p=128), writes=[Bconst], allow_slow_non_contiguous=True)
        for j in range(8):
            P.dma("sp", A.t32[j * 16:(j + 1) * 16, o_dsk:o_dsk + 64], ssm_d.rearrange("(g c) -> c g", c=16),
                  writes=[Bconst], allow_slow_non_contiguous=True)
        ws32 = A.f32(o_ws, 512).rearrange("p (g j) -> p g j", g=4)
        P.dma("sp", ws32, w_s.rearrange("g i j -> i g j"), writes=[Bws])
        for g4 in range(4):
            bk = nextbank()
            P.op("pe", lambda e, g4=g4, bk=bk: e.transpose(ps32(bk, 128), ws32[:, g4, :], id32),
                 reads=[Bws, Bconst], writes=[pb[bk]])
            P.op("dve", lambda e, g4=g4, bk=bk: e.tensor_copy(wsT[:, g4, :], ps32(bk, 128)),
                 reads=[pb[bk]], writes=[Bconst])
        qi = A.i32(o_tmp, 512)
        qf = A.f32(o_tmp + 512, 512)
        ang = A.f32(o_tmp + 1024, 512)
        kf = A.f32(o_tmp + 1536, 512)
        ri = A.i32(o_iot, 1)
        rf = A.f32(o_iot + 16, 1)
        P.op("pool", lambda e: e.iota(qi, [[1, 512]], base=0, channel_multiplier=0), reads=[Btmp], writes=[Btmp])
        P.op("dve", lambda e: e.tensor_copy(qf, qi), reads=[Btmp], writes=[Btmp])
        P.op("act", lambda e: e.activation(qf, qf, AF.Exp, scale=-math.log(10000.0) / 512.0), reads=[Btmp], writes=[Btmp])
        P.op("pool", lambda e: e.iota(ri, [[1, 1]], base=0, channel_multiplier=1), reads=[Biot], writes=[Biot])
        P.op("dve", lambda e: e.tensor_copy(rf, ri), reads=[Biot], writes=[Biot])
        P.op("dve", lambda e: e.tensor_scalar(A.f32(o_iot + 32, 1), rf, 64.0, -64.0, ALU.is_ge, ALU.mult),
             reads=[Biot], writes=[Biot])
        P.op("dve", lambda e: e.tensor_tensor(rf, rf, A.f32(o_iot + 32, 1), ALU.add), reads=[Biot], writes=[Biot])
        for half, shift in ((0, 0.0), (1, math.pi / 2)):
            P.op("dve", lambda e: e.tensor_scalar(ang, qf, rf, None, ALU.mult), reads=[Btmp, Biot], writes=[Btmp])
            P.op("dve", lambda e, shift=shift: e.tensor_scalar(ang, ang, shift, None, ALU.add), reads=[Btmp], writes=[Btmp])
            sin_reduced(ang, kf, Etab[:, half * 512:(half + 1) * 512], Btmp, Bconst)

    def sin_reduced(ang, kf, out, Bin, Bout, eng="dve"):
        Bin = list(Bin) if isinstance(Bin, (list, tuple)) else [Bin]
        P.op("act", lambda e: e.activation(kf, ang, AF.Identity, bias=MAGIC, scale=1.0 / TWO_PI), reads=Bin, writes=Bin)
        P.op("act", lambda e: e.activation(kf, kf, AF.Identity, bias=-MAGIC, scale=1.0), reads=Bin, writes=Bin)
        if eng == "dve":
            P.op(eng, lambda e: e.scalar_tensor_tensor(ang, kf, -TWO_PI, ang, ALU.mult, ALU.add), reads=Bin, writes=Bin)
        else:
            P.op(eng, lambda e: e.tensor_scalar(kf, kf, -TWO_PI, None, ALU.mult), reads=Bin, writes=Bin)
            P.op(eng, lambda e: e.tensor_tensor(ang, ang, kf, ALU.add), reads=Bin, writes=Bin)
        P.op(eng, lambda e: e.tensor_scalar(ang, ang, -math.pi, math.pi, ALU.max, ALU.min), reads=Bin, writes=Bin)
        P.op("act", lambda e: e.activation(out, ang, AF.Sin), reads=Bin, writes=[Bout])

    def phase_mod():
        top = o_region
        o_sc = top; top += 32
        o_sg = top; top += 32
        o_bcol = top; top += 64
        o_gcol = top; top += 32
        Bsc, Bbcol, Bgcol = Buf("sc"), Buf("bcol"), Buf("gcol")
        sc = A.f32(o_sc, 32).rearrange("p (k m) -> p k m", k=16)
        sg = A.f32(o_sg, 32).rearrange("p (k m) -> p k m", k=16)
        for m in range(2):
            P.dma("sp", sc[:, :, m], cvec[m].rearrange("(k p) -> p k", p=128), writes=[Bsc],
                  allow_slow_non_contiguous=True)
        P.op("act", lambda e: e.activation(sg, sc, AF.Sigmoid), reads=[Bsc], writes=[Bsc])
        P.op("dve", lambda e: e.tensor_tensor(sc, sc, sg, ALU.mult), reads=[Bsc], writes=[Bsc])
        bcol = A.f32(o_bcol, 64).rearrange("p (k f) -> p k f", k=4)
        kind_off = [0, D, 3 * D, 4 * D]
        for k in range(4):
            P.dma("sp", bcol[:, k, :], b_ada[kind_off[k]:kind_off[k] + D].rearrange("(f p) -> p f", p=128),
                  writes=[Bbcol], allow_slow_non_contiguous=True)
        gcol = A.f32(o_gcol, 32).rearrange("p (k f) -> p k f", k=2)
        P.dma("sp", gcol[:, 0, :], g_mix.rearrange("(f p) -> p f", p=128), writes=[Bgcol], allow_slow_non_contiguous=True)
        P.dma("sp", gcol[:, 1, :], g_mlp.rearrange("(f p) -> p f", p=128), writes=[Bgcol], allow_slow_non_contiguous=True)
        bk = nextbank()
        sc16 = A.bf(o_sg, 32).rearrange("p (k m) -> p k m", k=16)
        P.op("dve", lambda e: e.tensor_copy(sc16, sc), reads=[Bsc], writes=[Bsc])
        for k in range(4):
            for ch in range(8):
                wsb, Bw = wload(w_ada3, 0, 16, kind_off[k] + ch * 256, 256)

                def fn(e, wsb=wsb, k=k, ch=ch, bk=bk):
                    ins = None
                    for f2 in range(2):
                        col = (k * 16 + ch * 2 + f2) * 2
                        for kt in range(16):
                            ins = e.matmul(ps32(bk, 2, col), wsb[:, kt, f2 * 128:(f2 + 1) * 128], sc16[:, kt, :],
                                           start=(kt == 0), stop=(kt == 15))
                    return ins
                P.op("pe", fn, reads=[Bw, Bsc], writes=[pb[bk]])
        mraw = ps32(bk, 128).rearrange("p (k f m) -> p k f m", k=4, f=16)
        P.op("dve", lambda e: e.tensor_tensor(modc, mraw, bcol.unsqueeze(3).broadcast_to([128, 4, 16, 2]), ALU.add),
             reads=[pb[bk], Bbcol], writes=[Bmod])
        for k, gi in ((1, 0), (3, 1)):
            P.op("dve", lambda e, k=k, gi=gi: e.scalar_tensor_tensor(
                modc[:, k], modc[:, k], 1.0, gcol[:, gi, :].unsqueeze(2).broadcast_to([128, 16, 2]), ALU.add, ALU.mult),
                reads=[Bmod, Bgcol], writes=[Bmod])

    def compute_gates(ms):
        top = o_region
        o_sc = top; top += 16
        o_sg = top; top += 16
        o_rep = top; top += 2048
        o_brow = top; top += 2 * D
        o_one = top; top += 128
        Bsc, Brep, Bbrow = Buf("gsc"), Buf("grep"), Buf("gbrow")
        sc = A.f32(o_sc, 16)
        sg = A.f32(o_sg, 16)
        P.dma("sp", sc, cvec[ms].rearrange("(k p) -> p k", p=128), writes=[Bsc], allow_slow_non_contiguous=True)
        P.op("act", lambda e: e.activation(sg, sc, AF.Sigmoid), reads=[Bsc], writes=[Bsc])
        P.op("dve", lambda e: e.tensor_tensor(sc, sc, sg, ALU.mult), reads=[Bsc], writes=[Bsc])
        rep = A.bf(o_rep, 2048).rearrange("p (k m) -> p k m", k=16)
        P.op("dve", lambda e: e.tensor_copy(rep, sc.unsqueeze(2).broadcast_to([128, 16, 128])), reads=[Bsc], writes=[Brep])
        P.dma("sp", gate_bc[:, 0:D], b_ada[2 * D:3 * D].partition_broadcast(128), writes=[Bgate])
        P.dma("sp", gate_bc[:, D:2 * D], b_ada[5 * D:6 * D].partition_broadcast(128), writes=[Bgate])
        for gi, coff in ((0, 2 * D), (1, 5 * D)):
            for ch in range(8):
                wsb, Bw = wload(w_ada3, 0, 16, coff + ch * 256, 256)
                bk = nextbank()

                def fn(e, wsb=wsb, bk=bk):
                    ins = None
                    for kt in range(16):
                        ins = e.matmul(ps32(bk, 256), rep[:, kt, :], wsb[:, kt, :], start=(kt == 0), stop=(kt == 15))
                    return ins
                P.op("pe", fn, reads=[Bw, Brep], writes=[pb[bk]])
                gsl = gate_bc[:, gi * D + ch * 256:gi * D + ch * 256 + 256]
                P.op("dve", lambda e, bk=bk, gsl=gsl: e.tensor_tensor(gsl, gsl, ps32(bk, 256), ALU.add),
                     reads=[pb[bk], Bgate], writes=[Bgate])

    def norm_to_hT(xs, Bxs, xn, Bxn, hT, BhT, ms, kA, kS, junk, Bjunk):
        P.op("dve", lambda e: e.memset(cols[:, 0:4], 0.0), writes=[Bcols])
        for t in range(4):
            ssc = cols[:, t:t + 1]
            P.op("act", lambda e, t=t, ssc=ssc: e.activation(junk, xs[t], AF.Square, accum_out=ssc),
                 reads=[Bxs[t]], writes=[Bjunk, Bcols])
        P.op("act", lambda e: e.activation(cols[:, 8:12], cols[:, 0:4], AF.Sqrt, bias=EPS, scale=1.0 / D),
             reads=[Bcols], writes=[Bcols])
        P.op("dve", lambda e: e.reciprocal(cols[:, 8:12], cols[:, 8:12]), reads=[Bcols], writes=[Bcols])
        for t in range(4):
            rsc = cols[:, 8 + t:9 + t]
            eng = evac_eng()
            if eng == "act":
                P.op("act", lambda e, t=t, rsc=rsc: e.activation(xn[t], xs[t], AF.Copy, scale=rsc),
                     reads=[Bxs[t], Bcols], writes=[Bxn[t]])
            else:
                P.op("dve", lambda e, t=t, rsc=rsc: e.tensor_scalar(xn[t], xs[t], rsc, None, ALU.mult),
                     reads=[Bxs[t], Bcols], writes=[Bxn[t]])
        transposes_to_hT(xn, Bxn, hT, BhT, ms, kA, kS)

    def transposes_to_hT(xn, Bxn, hT, BhT, ms, kA, kS):
        for ft in range(16):
            bk = nextbank()

            def fn(e, ft=ft, bk=bk):
                ins = None
                for t in range(4):
                    ins = e.transpose(ps16(bk, 128, t * 128), xn[t][:, ft * 128:(ft + 1) * 128], id16)
                return ins
            P.op("pe", fn, reads=list(Bxn) + [Bconst], writes=[pb[bk]])
            eng = evac_eng()
            Ac = modc[:, kA, ft, ms:ms + 1]
            Sc = modc[:, kS, ft, ms:ms + 1]
            if eng == "act":
                P.op("act", lambda e, ft=ft, bk=bk, Ac=Ac, Sc=Sc: e.activation(
                    hT[:, ft, :], ps16(bk, 512), AF.Identity, bias=Sc, scale=Ac),
                    reads=[pb[bk], Bmod], writes=[BhT[ft]])
            else:
                P.op("dve", lambda e, ft=ft, bk=bk, Ac=Ac, Sc=Sc: e.tensor_scalar(
                    hT[:, ft, :], ps16(bk, 512), Ac, Sc, ALU.mult, ALU.add),
                    reads=[pb[bk], Bmod], writes=[BhT[ft]])

    def add_emb(xt_ap, Bx, ohcol, ohsb, Boh):
        P.dma("sp", ohsb, oh[:, ohcol:ohcol + 128], writes=[Boh])
        q = nextquad()
        for hh in range(2):
            for c2 in range(2):
                bk = q + hh * 2 + c2
                P.op("pe", lambda e, hh=hh, c2=c2, bk=bk: e.matmul(
                    ps32(bk), ohsb[hh * 64:(hh + 1) * 64, :], Etab[hh * 64:(hh + 1) * 64, c2 * 512:(c2 + 1) * 512],
                    start=True, stop=True), reads=[Boh, Bconst], writes=[pb[bk]])
        P.op("dve", lambda e, q=q: e.tensor_tensor(xt_ap, xt_ap, psum[:, q * 512:q * 512 + 2048], ALU.add),
             reads=[pb[q], pb[q + 1], pb[q + 2], pb[q + 3], Bx], writes=[Bx])

    def phase2(S):
        top = o_region
        o_xt = top; top += 4 * D
        o_hT = top; top += 4096
        o_R = top; top += 10240
        o_tmp = top; top += 3 * 512
        o_wst = top; top += 4 * 256
        o_oh = top; top += 128
        assert top - o_region <= REGION_WORDS, (top - o_region, REGION_WORDS)
        xt = [A.f32(o_xt + t * D, D) for t in range(4)]
        Bxt = [Buf(f"xt{t}") for t in range(4)]
        hT = A.bf(o_hT, 8192).rearrange("p (k n) -> p k n", k=16)
        BhT = [Buf(f"hT{k}") for k in range(16)]
        uT = A.bf(o_R, 4096).rearrange("p (k n) -> p k n", k=8)
        BuT = [Buf(f"uT{k}") for k in range(8)]
        vtk = [A.bf(o_R + 2048 + t * 512, 1024) for t in range(4)]
        Bv = [Buf(f"v{t}") for t in range(4)]
        yaT = A.bf(o_R + 4096, 4096).rearrange("p (k n) -> p k n", k=8)
        ByaT = [Buf(f"yaT{k}") for k in range(8)]
        mT = A.bf(o_R + 6144, 8192).rearrange("p (k n) -> p k n", k=16)
        BmT = [Buf(f"mT{k}") for k in range(16)]
        aT = A.bf(o_R, 16384).rearrange("p (k n) -> p k n", k=32)
        BaT = [Buf(f"aT{k}") for k in range(32)]
        xn = [A.bf(o_R + t * 1024, D) for t in range(4)]
        Bxn = [Buf(f"xn{t}") for t in range(4)]
        junk = A.bf(o_R + 4096, D)
        Bjunk = Buf("junk")
        tmpA = A.f32(o_tmp, 512)
        tmpB = A.f32(o_tmp + 512, 512)
        tmpC = A.f32(o_tmp + 1024, 512)
        BtA, BtB, BtC = Buf("tA"), Buf("tB"), Buf("tC")
        wst = [A.bf(o_wst + t * 256, 512).rearrange("p (g i) -> p g i", g=4) for t in range(4)]
        Bwst = [Buf(f"wst{t}") for t in range(4)]
        ohsb = A.f32(o_oh, 128)
        Boh = Buf("oh")
        Rall = BuT + Bv + ByaT + BmT + BaT + Bxn + [Bjunk]
        ms = S["ms"]
        g1 = gate_bc[:, 0:D]
        g2 = gate_bc[:, D:2 * D]

        def alias_guard(new_bufs, old_bufs):
            for nb in new_bufs:
                for ob in old_bufs:
                    if ob.w is not None:
                        nb.r[("w", id(ob))] = ob.w
                    for k, t in ob.r.items():
                        nb.r[(k, id(ob))] = t

        for tile in range(S["ntok"] // 512):
            t0 = tile * 512
            alias_guard(Bxn + [Bjunk], BaT + BmT)
            for t in range(4):
                P.dma("sp", xt[t], S["x"][t0 + t * 128:t0 + (t + 1) * 128, :], writes=[Bxt[t]])
                if S["emb"]:
                    add_emb(xt[t], Bxt[t], S["ohoff"] + t0 + t * 128, ohsb, Boh)
            norm_to_hT(xt, Bxt, xn, Bxn, hT, BhT, ms, 1, 0, junk, Bjunk)
            if tile == 0 and S["name"] == "prompt":
                dbg("hT", hT, BhT)
            alias_guard(BuT + Bv + ByaT, Bxn + [Bjunk])
            for ch in range(4):
                wsb, Bw = wload(w_in3, 0, 16, ch * 256, 256)
                for m2 in range(2):
                    mt = ch * 2 + m2
                    bk = nextbank()

                    def fn(e, wsb=wsb, m2=m2, bk=bk):
                        ins = None
                        for kt in range(16):
                            ins = e.matmul(ps32(bk), wsb[:, kt, m2 * 128:(m2 + 1) * 128], hT[:, kt, :],
                                           start=(kt == 0), stop=(kt == 15))
                        return ins
                    P.op("pe", fn, reads=[Bw] + BhT, writes=[pb[bk]])
                    P.op("act", lambda e, mt=mt, bk=bk: e.activation(uT[:, mt, :], ps32(bk), AF.Gelu_apprx_tanh),
                         reads=[pb[bk]], writes=[BuT[mt]])
            for ncn in range(2):
                wA, BwA = wload(w_in3, 0, 8, DA + ncn * 512, 512)
                wB, BwB = wload(w_in3, 8, 8, DA + ncn * 512, 512)
                for t in range(4):
                    bk = nextbank()

                    def fn1(e, wA=wA, t=t, bk=bk):
                        ins = None
                        for kt in range(8):
                            ins = e.matmul(ps32(bk), hT[:, kt, t * 128:(t + 1) * 128], wA[:, kt, :],
                                           start=(kt == 0), stop=False)
                        return ins

                    def fn2(e, wB=wB, t=t, bk=bk):
                        ins = None
                        for kt in range(8):
                            ins = e.matmul(ps32(bk), hT[:, 8 + kt, t * 128:(t + 1) * 128], wB[:, kt, :],
                                           start=False, stop=(kt == 7))
                        return ins
                    P.op("pe", fn1, reads=[BwA] + BhT, writes=[pb[bk]])
                    P.op("pe", fn2, reads=[BwB] + BhT, writes=[pb[bk]])
                    P.op("act", lambda e, t=t, ncn=ncn, bk=bk: e.activation(
                        vtk[t][:, ncn * 512:(ncn + 1) * 512], ps32(bk), AF.Gelu_apprx_tanh),
                        reads=[pb[bk]], writes=[Bv[t]])
            P.op("dve", lambda e: e.memset(cols[:, 16:20], 0.0), writes=[Bcols])
            for t in range(4):
                ssc = cols[:, 16 + t:17 + t]
                P.op("act", lambda e, t=t, ssc=ssc: e.activation(A.bf(o_tmp, 1024), vtk[t], AF.Square, accum_out=ssc),
                     reads=[Bv[t]], writes=[BtA, BtB, Bcols])
            P.op("act", lambda e: e.activation(cols[:, 24:28], cols[:, 16:20], AF.Sqrt, bias=EPS, scale=1.0 / DA),
                 reads=[Bcols], writes=[Bcols])
            P.op("dve", lambda e: e.reciprocal(cols[:, 24:28], cols[:, 24:28]), reads=[Bcols], writes=[Bcols])
            for t in range(4):
                rsc = cols[:, 24 + t:25 + t]
                P.op("dve", lambda e, t=t, rsc=rsc: e.tensor_scalar(wst[t], wsT, rsc, None, ALU.mult),
                     reads=[Bcols, Bconst], writes=[Bwst[t]])
            for mt in range(8):
                ga = mt // 2
                bk = nextbank()

                def fn(e, mt=mt, ga=ga, bk=bk):
                    ins = None
                    for t in range(4):
                        ins = e.matmul(ps32(bk, 128, t * 128), vtk[t][:, mt * 128:(mt + 1) * 128], wst[t][:, ga, :],
                                       start=True, stop=True)
                    return ins
                P.op("pe", fn, reads=Bv + Bwst, writes=[pb[bk]])
                P.op("dve", lambda e, mt=mt, ga=ga, bk=bk: e.scalar_tensor_tensor(
                    tmpA.rearrange("p (t i) -> p t i", t=4), ps32(bk).rearrange("p (t i) -> p t i", t=4),
                    gsgu_c[:, mt:mt + 1], bs_bc[:, ga, :].unsqueeze(1).broadcast_to([128, 4, 128]), ALU.mult, ALU.add),
                    reads=[pb[bk], Bconst], writes=[BtA])
                P.op("dve", lambda e, mt=mt: e.tensor_tensor(yaT[:, mt, :], tmpA, uT[:, mt, :], ALU.mult),
                     reads=[BtA, BuT[mt]], writes=[ByaT[mt]])
            if tile == 0 and S["name"] == "prompt":
                dbg("uT", uT, BuT)
                dbg("v0", vtk[0], [Bv[0]])
                dbg("yaT", yaT, ByaT)
                dbg("wst0", wst[0], [Bwst[0]])
            for ft2 in range(8):
                pA, pB, Bpw = wload2(w_pa3, w_pb3, 0, 8, ft2 * 256, 256)
                for f in range(2):
                    ft = ft2 * 2 + f
                    gA, gB, Bgw = wload2(w_in3[:, :, 3 * DA:3 * DA + D], w_in3[:, :, 3 * DA + D:3 * DA + 2 * D],
                                         0, 16, ft * 128, 128)
                    q = nextquad()
                    bga, bpa, bgb, bpb = q, q + 1, q + 2, q + 3

                    def fga(e, gA=gA, bk=bga):
                        ins = None
                        for kt in range(16):
                            ins = e.matmul(ps32(bk), gA[:, kt, :], hT[:, kt, :], start=(kt == 0), stop=(kt == 15))
                        return ins

                    def fgb(e, gB=gB, bk=bgb):
                        ins = None
                        for kt in range(16):
                            ins = e.matmul(ps32(bk), gB[:, kt, :], hT[:, kt, :], start=(kt == 0), stop=(kt == 15))
                        return ins

                    def fpa(e, pA=pA, f=f, bk=bpa):
                        ins = None
                        for kt in range(8):
                            ins = e.matmul(ps32(bk), pA[:, kt, f * 128:(f + 1) * 128], yaT[:, kt, :],
                                           start=(kt == 0), stop=(kt == 7))
                        return ins

                    def fpb(e, pB=pB, f=f, bk=bpb, t0=t0):
                        ins = None
                        for kt in range(8):
                            ins = e.matmul(ps32(bk), pB[:, kt, f * 128:(f + 1) * 128], ybT[:, kt, S["yoff"] + t0:S["yoff"] + t0 + 512],
                                           start=(kt == 0), stop=(kt == 7))
                        return ins
                    P.op("pe", fga, reads=[Bgw] + BhT, writes=[pb[bga]])
                    P.op("pe", fpa, reads=[Bpw] + ByaT, writes=[pb[bpa]])
                    P.op("pe", fgb, reads=[Bgw] + BhT, writes=[pb[bgb]])
                    P.op("pe", fpb, reads=[Bpw] + BybT, writes=[pb[bpb]])
                    P.op("act", lambda e, bk=bga: e.activation(tmpA, ps32(bk), AF.Sigmoid), reads=[pb[bga]], writes=[BtA])
                    P.op("dve", lambda e, bk=bpa: e.tensor_tensor(tmpA, tmpA, ps32(bk), ALU.mult),
                         reads=[BtA, pb[bpa]], writes=[BtA])
                    P.op("act", lambda e, bk=bgb: e.activation(tmpB, ps32(bk), AF.Sigmoid), reads=[pb[bgb]], writes=[BtB])
                    P.op("dve", lambda e, bk=bpb: e.tensor_tensor(tmpB, tmpB, ps32(bk), ALU.mult),
                         reads=[BtB, pb[bpb]], writes=[BtB])
                    P.op("dve", lambda e, ft=ft: e.tensor_tensor(mT[:, ft, :], tmpA, tmpB, ALU.add),
                         reads=[BtA, BtB], writes=[BmT[ft]])
            if tile == 0 and S["name"] == "prompt":
                dbg("mT", mT, BmT)
            for c in range(4):
                wA, BwA = wload(w_out3, 0, 8, c * 512, 512)
                wB, BwB = wload(w_out3, 8, 8, c * 512, 512)
                for t in range(4):
                    bk = nextbank()

                    def fn1(e, wA=wA, t=t, bk=bk):
                        ins = None
                        for kt in range(8):
                            ins = e.matmul(ps32(bk), mT[:, kt, t * 128:(t + 1) * 128], wA[:, kt, :],
                                           start=(kt == 0), stop=False)
                        return ins

                    def fn2(e, wB=wB, t=t, bk=bk):
                        ins = None
                        for kt in range(8):
                            ins = e.matmul(ps32(bk), mT[:, 8 + kt, t * 128:(t + 1) * 128], wB[:, kt, :],
                                           start=False, stop=(kt == 7))
                        return ins
                    P.op("pe", fn1, reads=[BwA] + BmT, writes=[pb[bk]])
                    P.op("pe", fn2, reads=[BwB] + BmT, writes=[pb[bk]])
                    P.op("dve", lambda e, c=c, bk=bk: e.tensor_tensor(tmpC, ps32(bk), g1[:, c * 512:(c + 1) * 512], ALU.mult),
                         reads=[pb[bk], Bgate], writes=[BtC])
                    P.op("dve", lambda e, c=c, t=t: e.tensor_tensor(
                        xt[t][:, c * 512:(c + 1) * 512], xt[t][:, c * 512:(c + 1) * 512], tmpC, ALU.add),
                        reads=[BtC, Bxt[t]], writes=[Bxt[t]])
            if tile == 0 and S["name"] == "prompt":
                dbg("xmid0", xt[0], [Bxt[0]])
            alias_guard(Bxn + [Bjunk], BuT + Bv + ByaT + BmT)
            norm_to_hT(xt, Bxt, xn, Bxn, hT, BhT, ms, 3, 2, junk, Bjunk)
            if tile == 0 and S["name"] == "prompt":
                dbg("h2T", hT, BhT)
            alias_guard(BaT, Bxn + [Bjunk] + BuT + Bv + ByaT + BmT)
            for hh in range(2):
                for ch in range(16):
                    wsb, Bw = wload(w_mi3, 0, 16, hh * 4096 + ch * 256, 256)
                    for f2 in range(2):
                        fl = ch * 2 + f2
                        bk = nextbank()

                        def fn(e, wsb=wsb, f2=f2, bk=bk):
                            ins = None
                            for kt in range(16):
                                ins = e.matmul(ps32(bk), wsb[:, kt, f2 * 128:(f2 + 1) * 128], hT[:, kt, :],
                                               start=(kt == 0), stop=(kt == 15))
                            return ins
                        P.op("pe", fn, reads=[Bw] + BhT, writes=[pb[bk]])
                        tt, Bt = (tmpA, BtA) if fl % 2 == 0 else (tmpB, BtB)
                        P.op("act", lambda e, bk=bk, tt=tt: e.activation(tt, ps32(bk), AF.Relu), reads=[pb[bk]], writes=[Bt])
                        P.op("dve", lambda e, fl=fl, tt=tt: e.tensor_tensor(aT[:, fl, :], tt, tt, ALU.mult),
                             reads=[Bt], writes=[BaT[fl]])
                for c in range(4):
                    q = nextquad()
                    for f8 in range(4):
                        wsb, Bw = wload(w_mo3, hh * 32 + f8 * 8, 8, c * 512, 512)

                        def fn(e, wsb=wsb, f8=f8, q=q):
                            ins = None
                            for f in range(8):
                                for t in range(4):
                                    ins = e.matmul(ps32(q + t), aT[:, f8 * 8 + f, t * 128:(t + 1) * 128], wsb[:, f, :],
                                                   start=(f8 == 0 and f == 0), stop=(f8 == 3 and f == 7))
                            return ins
                        P.op("pe", fn, reads=[Bw] + BaT[f8 * 8:(f8 + 1) * 8], writes=[pb[q + t] for t in range(4)])
                    for t in range(4):
                        P.op("dve", lambda e, c=c, bk=q + t: e.tensor_tensor(tmpC, ps32(bk), g2[:, c * 512:(c + 1) * 512], ALU.mult),
                             reads=[pb[q + t], Bgate], writes=[BtC])
                        P.op("dve", lambda e, c=c, t=t: e.tensor_tensor(
                            xt[t][:, c * 512:(c + 1) * 512], xt[t][:, c * 512:(c + 1) * 512], tmpC, ALU.add),
                            reads=[BtC, Bxt[t]], writes=[Bxt[t]])
            P.op("dve", lambda e: e.memset(cols[:, 32:36], 0.0), writes=[Bcols])
            for t in range(4):
                ssc = cols[:, 32 + t:33 + t]
                P.op("act", lambda e, t=t, ssc=ssc: e.activation(A.bf(o_hT, D), xt[t], AF.Square, accum_out=ssc),
                     reads=[Bxt[t]], writes=BhT[0:4] + [Bcols])
            P.op("act", lambda e: e.activation(cols[:, 40:44], cols[:, 32:36], AF.Sqrt, bias=EPS, scale=1.0 / D),
                 reads=[Bcols], writes=[Bcols])
            P.op("dve", lambda e: e.reciprocal(cols[:, 40:44], cols[:, 40:44]), reads=[Bcols], writes=[Bcols])
            for t in range(4):
                rsc = cols[:, 40 + t:41 + t]
                P.op("dve", lambda e, t=t, rsc=rsc: e.scalar_tensor_tensor(xt[t], xt[t], rsc, gfin_bc, ALU.mult, ALU.mult),
                     reads=[Bxt[t], Bcols, Bconst], writes=[Bxt[t]])
                P.dma("sp", S["y"][t0 + t * 128:t0 + (t + 1) * 128, :], xt[t], reads=[Bxt[t]], is_output=True)
        return Rall + Bxt + BhT + [BtA, BtB, BtC] + Bwst + [Boh]

    o_s5p = s5p_off
    rho8 = A.f32(o_s5p, 64)
    phi8 = A.f32(o_s5p + 64, 64)
    h0r_s = A.f32(o_s5p + 128, 64)
    h0i_s = A.f32(o_s5p + 192, 64)
    car_r = A.f32(o_s5p + 256, 64)
    car_i = A.f32(o_s5p + 320, 64)
    ini_r = A.f32(o_s5p + 384, 64)
    ini_i = A.f32(o_s5p + 448, 64)
    Bs5p = Buf("s5p")

    def half(ap, d):
        return ap[d * 64:(d + 1) * 64]

    def phase_s5build():
        top = [o_region]

        def al(n):
            o = top[0]
            top[0] += n
            assert top[0] - o_region <= REGION_WORDS, (top[0] - o_region, REGION_WORDS)
            return o
        o_sm = al(1024)
        sm = [A.f32(o_sm + i * 64, 64) for i in range(16)]
        lamre, lamim, dtt, lrd, lid, mag, abre, abim, den, ur, fre, fim, s1, s2, s3, s4 = sm
        o_ex = al(32)
        EX = A.f32(o_ex, 32).rearrange("p (t k) -> p t k", t=4)
        o_k8 = al(16)
        o_big = al(6144)
        o_pwre = al(2048)
        o_pwim = al(2048)
        o_b = al(2048)
        o_fb = al(2048)
        o_ct = al(2048)
        o_cn = al(256)
        o_t = al(2048)
        o_w3 = al(1024)
        o_w2 = al(1024)
        o_m1 = al(512)
        o_m1t = al(256)
        o_msk = al(256)
        Bsm, Bex, Bbig, Bpw, Bb, Bfb, Bct, Bcn, Bt, Bw3, Bw2, Bm1, Bm1t, Bmsk = [
            Buf(n) for n in "sm ex big pw b fb ct cn t w3 w2 m1 m1t msk".split()]
        nat = A.f32(o_t, 128)
        Bnat = Buf("nat")
        for (src, dst, Bd) in ((a_re, lamre, Bsm), (a_im, lamim, Bsm), (h0re, h0r_s, Bs5p), (h0im, h0i_s, Bs5p)):
            P.dma("sp", nat[0:64].rearrange("p (d s) -> p d s", d=2), src.rearrange("d g s -> g d s"), writes=[Bnat])
            bk = nextbank()
            P.op("pe", lambda e, bk=bk: e.transpose(ps32(bk, 64), nat[0:64], id32[0:64, 0:64]), reads=[Bnat, Bconst], writes=[pb[bk]])
            P.op("dve", lambda e, dst=dst, bk=bk: e.tensor_copy(dst, ps32(bk, 64)), reads=[pb[bk]], writes=[Bd])
        for d in range(2):
            P.dma("sp", half(dtt, d), log_dt[d].partition_broadcast(64), writes=[Bsm])
            for g8 in range(8):
                gsl = slice(g8 * 8, (g8 + 1) * 8)
                P.dma("sp", half(A.f32(o_b, 1024), d).rearrange("p (g c) -> p g c", g=64)[:, gsl, :],
                      b_re[d, gsl].rearrange("g s c -> s g c"), writes=[Bb])
                P.dma("sp", half(A.f32(o_b + 1024, 1024), d).rearrange("p (g c) -> p g c", g=64)[:, gsl, :],
                      b_im[d, gsl].rearrange("g s c -> s g c"), writes=[Bb])
        Bre_ = A.f32(o_b, 1024).rearrange("p (g c) -> p g c", g=64)
        Bim_ = A.f32(o_b + 1024, 1024).rearrange("p (g c) -> p g c", g=64)
        Fbre = A.f32(o_fb, 1024).rearrange("p (g c) -> p g c", g=64)
        Fbim = A.f32(o_fb + 1024, 1024).rearrange("p (g c) -> p g c", g=64)
        CTre = A.f32(o_ct, 1024).rearrange("p (g c) -> p g c", g=64)
        CTim = A.f32(o_ct + 1024, 1024).rearrange("p (g c) -> p g c", g=64)
        V = lambda fn, r, w: P.op("dve", fn, reads=r, writes=w)
        P.op("act", lambda e: e.activation(dtt, dtt, AF.Exp), reads=[Bsm], writes=[Bsm])
        V(lambda e: e.tensor_tensor(lrd, lamre, dtt, ALU.mult), [Bsm], [Bsm])
        V(lambda e: e.tensor_tensor(lid, lamim, dtt, ALU.mult), [Bsm], [Bsm])
        P.op("act", lambda e: e.activation(mag, lrd, AF.Exp), reads=[Bsm], writes=[Bsm])
        P.op("act", lambda e: e.activation(rho8, lrd, AF.Exp, scale=8.0), reads=[Bsm], writes=[Bs5p])
        V(lambda e: e.tensor_scalar(s1, lid, 8.0 / TWO_PI, MAGIC, ALU.mult, ALU.add), [Bsm], [Bsm])
        V(lambda e: e.tensor_scalar(s1, s1, MAGIC, None, ALU.subtract), [Bsm], [Bsm])
        V(lambda e: e.tensor_scalar(s2, lid, 8.0, None, ALU.mult), [Bsm], [Bsm])
        V(lambda e: e.scalar_tensor_tensor(phi8, s1, -TWO_PI, s2, ALU.mult, ALU.add), [Bsm], [Bs5p])
        V(lambda e: e.tensor_copy(s3, lid), [Bsm], [Bsm])
        sin_reduced(s3, s4, abim, Bsm, Bsm)
        V(lambda e: e.tensor_scalar(s3, lid, math.pi / 2, None, ALU.add), [Bsm], [Bsm])
        sin_reduced(s3, s4, abre, Bsm, Bsm)
        V(lambda e: e.tensor_tensor(abre, abre, mag, ALU.mult), [Bsm], [Bsm])
        V(lambda e: e.tensor_tensor(abim, abim, mag, ALU.mult), [Bsm], [Bsm])
        V(lambda e: e.tensor_tensor(den, lamre, lamre, ALU.mult), [Bsm], [Bsm])
        V(lambda e: e.tensor_tensor(s1, lamim, lamim, ALU.mult), [Bsm], [Bsm])
        V(lambda e: e.tensor_tensor(den, den, s1, ALU.add), [Bsm], [Bsm])
        V(lambda e: e.reciprocal(den, den), [Bsm], [Bsm])
        V(lambda e: e.tensor_scalar(ur, abre, -1.0, None, ALU.add), [Bsm], [Bsm])
        V(lambda e: e.tensor_tensor(s1, ur, lamre, ALU.mult), [Bsm], [Bsm])
        V(lambda e: e.tensor_tensor(s2, abim, lamim, ALU.mult), [Bsm], [Bsm])
        V(lambda e: e.tensor_tensor(fre, s1, s2, ALU.add), [Bsm], [Bsm])
        V(lambda e: e.tensor_tensor(fre, fre, den, ALU.mult), [Bsm], [Bsm])
        V(lambda e: e.tensor_tensor(s1, abim, lamre, ALU.mult), [Bsm], [Bsm])
        V(lambda e: e.tensor_tensor(s2, ur, lamim, ALU.mult), [Bsm], [Bsm])
        V(lambda e: e.tensor_tensor(fim, s1, s2, ALU.subtract), [Bsm], [Bsm])
        V(lambda e: e.tensor_tensor(fim, fim, den, ALU.mult), [Bsm], [Bsm])
        SP_ = int(os.environ.get("MK_S5PART", "9"))
        if SP_ <= 1:
            return
        t1 = A.f32(o_t, 1024)
        t2 = A.f32(o_t + 1024, 1024)
        t1g = t1.rearrange("p (g c) -> p g c", g=64)
        t2g = t2.rearrange("p (g c) -> p g c", g=64)
        fre_b = fre.unsqueeze(2).broadcast_to([128, 64, 16])
        fim_b = fim.unsqueeze(2).broadcast_to([128, 64, 16])
        V(lambda e: e.tensor_tensor(t1g, Bre_, fre_b, ALU.mult), [Bsm, Bb], [Bt])
        V(lambda e: e.tensor_tensor(t2g, Bim_, fim_b, ALU.mult), [Bsm, Bb], [Bt])
        V(lambda e: e.tensor_tensor(Fbre, t1g, t2g, ALU.subtract), [Bt], [Bfb])
        V(lambda e: e.tensor_tensor(t1g, Bim_, fre_b, ALU.mult), [Bsm, Bb, Bfb], [Bt])
        V(lambda e: e.tensor_tensor(t2g, Bre_, fim_b, ALU.mult), [Bsm, Bb], [Bt])
        V(lambda e: e.tensor_tensor(Fbim, t1g, t2g, ALU.add), [Bt], [Bfb])
        k8i = A.i32(o_k8, 8)
        k8 = A.f32(o_k8 + 8, 8)
        P.op("pool", lambda e: e.iota(k8i, [[1, 8]], base=0, channel_multiplier=0), writes=[Bex])
        V(lambda e: e.tensor_copy(k8, k8i), [Bex], [Bex])
        spec = {0: [(1.0, 0.0), (-1.0, 0.0), (-1.0, 7.0), (1.0, 1.0)],
                1: [(-1.0, 0.0), (1.0, 0.0), (1.0, 0.0), (-1.0, 8.0)]}
        for d in range(2):
            for tb in range(4):
                a_, b_ = spec[d][tb]
                V(lambda e, d=d, tb=tb, a_=a_, b_=b_: e.tensor_scalar(half(EX[:, tb, :], d), half(k8, d), a_, b_, ALU.mult, ALU.add),
                  [Bex], [Bex])
        if SP_ <= 2:
            return
        PWmag = A.f32(o_big, 2048)
        ang = A.f32(o_big + 2048, 2048)
        kfs = A.f32(o_big + 4096, 2048)
        PWre = A.f32(o_pwre, 2048)
        PWim = A.f32(o_pwim, 2048)
        v3 = lambda ap: ap.rearrange("p (g x) -> p g x", g=64)
        EXf = A.f32(o_ex, 32)
        lrd_b = lrd.unsqueeze(2).broadcast_to([128, 64, 32])
        lid_b = lid.unsqueeze(2).broadcast_to([128, 64, 32])
        EX_b = EXf.unsqueeze(1).broadcast_to([128, 64, 32])
        V(lambda e: e.tensor_tensor(v3(PWmag), lrd_b, EX_b, ALU.mult), [Bsm, Bex], [Bbig])
        P.op("act", lambda e: e.activation(PWmag, PWmag, AF.Exp), reads=[Bbig], writes=[Bbig])
        V(lambda e: e.tensor_tensor(v3(ang), lid_b, EX_b, ALU.mult), [Bsm, Bex], [Bbig])
        sin_reduced(ang, kfs, PWim, Bbig, Bpw)
        V(lambda e: e.tensor_tensor(v3(ang), lid_b, EX_b, ALU.mult), [Bsm, Bex, Bbig], [Bbig])
        V(lambda e: e.tensor_scalar(ang, ang, math.pi / 2, None, ALU.add), [Bbig], [Bbig])
        sin_reduced(ang, kfs, PWre, Bbig, Bpw)
        V(lambda e: e.tensor_tensor(PWre, PWre, PWmag, ALU.mult), [Bpw, Bbig], [Bpw])
        V(lambda e: e.tensor_tensor(PWim, PWim, PWmag, ALU.mult), [Bpw, Bbig], [Bpw])
        if SP_ <= 3:
            return
        mski = A.i32(o_m1t, 256)
        msk2 = A.f32(o_msk, 256)
        pj_i = A.i32(o_k8, 1)
        pj = A.f32(o_k8 + 8, 1)
        P.op("pool", lambda e: e.iota(mski, [[0, 2], [1, 8], [0, 16]], base=0, channel_multiplier=0), reads=[Bm1t], writes=[Bm1t])
        V(lambda e: e.tensor_copy(msk2, mski), [Bm1t], [Bmsk])
        P.op("pool", lambda e: e.iota(pj_i, [[1, 1]], base=0, channel_multiplier=1), reads=[Bex], writes=[Bex])
        V(lambda e: e.tensor_single_scalar(pj_i, pj_i, 4, ALU.arith_shift_right), [Bex], [Bex])
        V(lambda e: e.tensor_copy(pj, pj_i), [Bex], [Bex])
        V(lambda e: e.tensor_scalar(msk2, msk2, pj, None, ALU.subtract), [Bmsk, Bex], [Bmsk])
        V(lambda e: e.tensor_scalar(msk2[:, 0:128], msk2[:, 0:128], 0.0, None, ALU.is_ge), [Bmsk], [Bmsk])
        V(lambda e: e.tensor_scalar(msk2[:, 128:256], msk2[:, 128:256], 0.0, None, ALU.is_le), [Bmsk], [Bmsk])
        if SP_ <= 4:
            return
        P.barrier()
        for gt in range(8):
            for ri, csrc in ((0, c_re), (1, c_im)):
                cn = A.f32(o_cn + ri * 128, 128)
                P.dma("sp", cn.rearrange("p (d s) -> p d s", d=2), csrc[:, gt * 8:(gt + 1) * 8].rearrange("d g c s -> (g c) d s"),
                      writes=[Bcn])
                bk = nextbank()
                P.op("pe", lambda e, cn=cn, bk=bk: e.transpose(ps32(bk, 128), cn, id32), reads=[Bcn, Bconst], writes=[pb[bk]])
                dst = (CTre if ri == 0 else CTim)[:, gt * 8:(gt + 1) * 8, :]
                V(lambda e, dst=dst, bk=bk: e.tensor_copy(dst, ps32(bk, 128).rearrange("p (g c) -> p g c", g=8)), [pb[bk]], [Bct])
        if SP_ <= 5:
            return
        bufs6 = [A.f32(o_big + i * 1024, 1024) for i in range(6)]
        Lre, Lim, Rmre, nRmim, W2r, W2i = bufs6
        v4 = lambda ap: ap.rearrange("p (g k c) -> p g k c", g=8, k=8)
        PW4r = PWre.rearrange("p (g t k) -> p g t k", g=64, t=4)
        PW4i = PWim.rearrange("p (g t k) -> p g t k", g=64, t=4)
        W3re16 = A.bf(o_w3, 1024)
        W3im16 = A.bf(o_w3 + 512, 1024)
        W2re16 = A.bf(o_w2, 1024)
        W2im16 = A.bf(o_w2 + 512, 1024)
        M116 = A.bf(o_m1, 1024)
        m1t = A.f32(o_m1t, 256)

        def cmul(out_re, out_im, ar, ai, br, bi, rd, wr, neg_im=False, out_is_bf=False):
            V(lambda e: e.tensor_tensor(v4(t1), ar, br, ALU.mult), rd, [Bt])
            V(lambda e: e.tensor_tensor(v4(t2), ai, bi, ALU.mult), rd, [Bt])
            V(lambda e: e.tensor_tensor(out_re, t1, t2, ALU.subtract), [Bt], wr)
            V(lambda e: e.tensor_tensor(v4(t1), ar, bi, ALU.mult), rd + wr, [Bt])
            V(lambda e: e.tensor_tensor(v4(t2), ai, br, ALU.mult), rd, [Bt])
            if neg_im:
                V(lambda e: e.scalar_tensor_tensor(out_im, t1, -1.0, t2, ALU.mult, ALU.subtract), [Bt], wr)
            else:
                V(lambda e: e.tensor_tensor(out_im, t1, t2, ALU.add), [Bt], wr)

        for gt in range(8):
            gs = slice(gt * 8, (gt + 1) * 8)
            ctr = CTre[:, gs, :].unsqueeze(2).broadcast_to([128, 8, 8, 16])
            cti = CTim[:, gs, :].unsqueeze(2).broadcast_to([128, 8, 8, 16])
            fbr = Fbre[:, gs, :].unsqueeze(2).broadcast_to([128, 8, 8, 16])
            fbi = Fbim[:, gs, :].unsqueeze(2).broadcast_to([128, 8, 8, 16])
            pw = lambda P4, tb: P4[:, gs, tb, :].unsqueeze(3).broadcast_to([128, 8, 8, 16])
            BL, BR, BW2 = Buf("L"), Buf("R"), Buf("W2p")
            for b_ in (BL, BR, BW2):
                b_.r[("g", 0)] = Bbig.w
            cmul(Lre, Lim, ctr, cti, pw(PW4r, 0), pw(PW4i, 0), [Bct, Bpw], [BL])
            cmul(Rmre, nRmim, fbr, fbi, pw(PW4r, 1), pw(PW4i, 1), [Bfb, Bpw], [BR], neg_im=True)
            cmul(W2r, W2i, fbr, fbi, pw(PW4r, 2), pw(PW4i, 2), [Bfb, Bpw], [BW2])
            cmul(W3re16, W3im16, ctr, cti, pw(PW4r, 3), pw(PW4i, 3), [Bct, Bpw], [Bw3], neg_im=True)
            if SP_ <= 6:
                continue
            P.dma("sp", smat[3, :, gs, :], W3re16.rearrange("p (g n) -> p g n", g=8), reads=[Bw3])
            P.dma("sp", smat[4, :, gs, :], W3im16.rearrange("p (g n) -> p g n", g=8), reads=[Bw3])
            if SP_ <= 7:
                continue
            for gl in range(8):
                for ri, src, dst in ((0, W2r, W2re16), (1, W2i, W2im16)):
                    bk = nextbank()
                    P.op("pe", lambda e, src=src, gl=gl, bk=bk: e.transpose(ps32(bk, 128), src[:, gl * 128:(gl + 1) * 128], id32),
                         reads=[BW2, Bconst], writes=[pb[bk]])
                    P.op("act", lambda e, dst=dst, gl=gl, bk=bk: e.activation(dst[:, gl * 128:(gl + 1) * 128], ps32(bk, 128), AF.Copy),
                         reads=[pb[bk]], writes=[Bw2])
                if SP_ <= 8:
                    continue
                bk = nextbank()
                bk2 = nextbank()

                def fm(e, gl=gl, bk=bk, bk2=bk2):
                    ins = None
                    for d in range(2):
                        cs = slice(gl * 128, (gl + 1) * 128)
                        ob = ps32(bk if d == 0 else bk2, 128)
                        e.matmul(ob, half(Rmre[:, cs], d), half(Lre[:, cs], d), start=True, stop=False)
                        ins = e.matmul(ob, half(nRmim[:, cs], d), half(Lim[:, cs], d), start=False, stop=True)
                    return ins
                P.op("pe", fm, reads=[BL, BR], writes=[pb[bk], pb[bk2]])
                V(lambda e, bk=bk: e.tensor_tensor(m1t[:, 0:128], ps32(bk, 128), msk2[:, 0:128], ALU.mult), [pb[bk], Bmsk], [Bm1t])
                V(lambda e, bk2=bk2: e.tensor_tensor(m1t[:, 128:256], ps32(bk2, 128), msk2[:, 128:256], ALU.mult), [pb[bk2], Bmsk], [Bm1t])
                V(lambda e: e.tensor_tensor(m1t[:, 0:128], m1t[:, 0:128], m1t[:, 128:256], ALU.add), [Bm1t], [Bm1t])
                V(lambda e, gl=gl, gt=gt: e.scalar_tensor_tensor(M116[:, gl * 128:(gl + 1) * 128], id32, dsk_c[:, gt * 8 + gl:gt * 8 + gl + 1],
                                                                m1t[:, 0:128], ALU.mult, ALU.add), [Bm1t, Bconst], [Bm1])
            P.dma("sp", smat[0, :, gs, :], M116.rearrange("p (g n) -> p g n", g=8), reads=[Bm1])
            P.dma("sp", smat[1, :, gs, :], W2re16.rearrange("p (g n) -> p g n", g=8), reads=[Bw2])
            P.dma("sp", smat[2, :, gs, :], W2im16.rearrange("p (g n) -> p g n", g=8), reads=[Bw2])

    def phase1(S):
        top = [o_region]

        def al(n):
            o = top[0]
            top[0] += n
            assert top[0] - o_region <= REGION_WORDS, (top[0] - o_region, REGION_WORDS)
            return o
        o_U = al(8192)
        o_sel = al(4096)
        o_work = top[0]
        NU = 256 if S["name"] == "sample" else 128
        U = A.bf(o_U, 64 * NU).rearrange("p (g n) -> p g n", g=64)
        BU = [Buf(f"U{g}") for g in range(8)]
        Sel = A.bf(o_sel, 8192).rearrange("p (m n) -> p m n", m=64)
        Bsel = Buf("sel")
        o_w = al(256)
        wi = A.i32(o_w, 128)
        wf = A.f32(o_w, 128)
        pi_ = A.i32(o_w + 128, 1)
        pf_ = A.f32(o_w + 144, 1)
        Bw_ = Buf("selw")
        P.op("pool", lambda e: e.iota(wi, [[1, 128]], base=0, channel_multiplier=-1), writes=[Bw_])
        P.op("pool", lambda e: e.iota(pi_, [[1, 1]], base=0, channel_multiplier=1), writes=[Bw_])
        P.op("dve", lambda e: e.tensor_single_scalar(pi_, pi_, 4, ALU.arith_shift_right), reads=[Bw_], writes=[Bw_])
        P.op("dve", lambda e: e.tensor_copy(pf_, pi_), reads=[Bw_], writes=[Bw_])
        P.op("dve", lambda e: e.tensor_copy(wf, wi), reads=[Bw_], writes=[Bw_])
        P.op("dve", lambda e: e.tensor_scalar(pf_, pf_, 1024.0, None, ALU.mult), reads=[Bw_], writes=[Bw_])
        P.op("dve", lambda e: e.tensor_scalar(wf, wf, pf_, None, ALU.add), reads=[Bw_], writes=[Bw_])
        for a in range(8):
            for b in range(8):
                P.op("dve", lambda e, a=a, b=b: e.tensor_scalar(Sel[:, a * 8 + b, :], wf, float(16 * (b - a) + 1024 * a), None, ALU.is_equal),
                     reads=[Bw_], writes=[Bsel])
        top[0] = o_work
        P.barrier()

        def build_U(xsrc, ntok, emb, ohoff):
            t_ = [o_work]

            def al2(n):
                o = t_[0]
                t_[0] += n
                assert t_[0] - o_region <= REGION_WORDS, (t_[0] - o_region, REGION_WORDS)
                return o
            o_xb = al2(2 * D)
            o_xn = al2(4096)
            o_hT = al2(4096)
            o_xbT = al2(2048)
            o_oh = al2(128)
            xblk = [A.f32(o_xb + (t % 2) * D, D) for t in range(4)]
            Bxb2 = [Buf("xb0"), Buf("xb1")]
            Bxb = [Bxb2[t % 2] for t in range(4)]
            xn = [A.bf(o_xn + t * 1024, D) for t in range(4)]
            Bxn = [Buf(f"p1xn{t}") for t in range(4)]
            hT = A.bf(o_hT, 8192).rearrange("p (k n) -> p k n", k=16)
            BhT = [Buf(f"p1hT{k}") for k in range(16)]
            xbT = A.bf(o_xbT, 4096).rearrange("p (k n) -> p k n", k=8)
            BxbT = [Buf(f"xbT{k}") for k in range(8)]
            junk = A.bf(o_xbT, D)
            Bjunk = Buf("p1junk")
            ohsb = A.f32(o_oh, 128)
            Boh = Buf("p1oh")
            ms = S["ms"]
            for tile in range(ntok // 512):
                t0 = tile * 512
                for b_ in BxbT:
                    Bjunk.r[("x", id(b_))] = b_.w
                    for k_, v_ in b_.r.items():
                        Bjunk.r[(k_, id(b_))] = v_
                for t in range(4):
                    P.dma("sp", xblk[t], xsrc[t0 + t * 128:t0 + (t + 1) * 128, :], writes=[Bxb[t]])
                    if emb:
                        add_emb(xblk[t], Bxb[t], ohoff + t0 + t * 128, ohsb, Boh)
                    ssc = cols[:, t:t + 1]
                    rsc = cols[:, 8 + t:9 + t]
                    P.op("dve", lambda e, ssc=ssc: e.memset(ssc, 0.0), writes=[Bcols])
                    P.op("act", lambda e, t=t, ssc=ssc: e.activation(junk, xblk[t], AF.Square, accum_out=ssc),
                         reads=[Bxb[t]], writes=[Bjunk, Bcols])
                    P.op("act", lambda e, ssc=ssc, rsc=rsc: e.activation(rsc, ssc, AF.Sqrt, bias=EPS, scale=1.0 / D),
                         reads=[Bcols], writes=[Bcols])
                    P.op("dve", lambda e, rsc=rsc: e.reciprocal(rsc, rsc), reads=[Bcols], writes=[Bcols])
                    P.op("dve", lambda e, t=t, rsc=rsc: e.tensor_scalar(xn[t], xblk[t], rsc, None, ALU.mult),
                         reads=[Bxb[t], Bcols], writes=[Bxn[t]])
                for b_ in BxbT:
                    b_.r[("j", 0)] = Bjunk.w
                transposes_to_hT(xn, Bxn, hT, BhT, ms, 1, 0)
                for ch in range(4):
                    wsb, Bw = wload(w_in3, 0, 16, 2 * DA + ch * 256, 256)
                    for m2 in range(2):
                        mt = ch * 2 + m2
                        bk = nextbank()

                        def fn(e, wsb=wsb, m2=m2, bk=bk):
                            ins = None
                            for kt in range(16):
                                ins = e.matmul(ps32(bk), wsb[:, kt, m2 * 128:(m2 + 1) * 128], hT[:, kt, :],
                                               start=(kt == 0), stop=(kt == 15))
                            return ins
                        P.op("pe", fn, reads=[Bw] + BhT, writes=[pb[bk]])
                        if evac_eng() == "act":
                            P.op("act", lambda e, mt=mt, bk=bk: e.activation(xbT[:, mt, :], ps32(bk), AF.Copy),
                                 reads=[pb[bk]], writes=[BxbT[mt]])
                        else:
                            P.op("dve", lambda e, mt=mt, bk=bk: e.tensor_copy(xbT[:, mt, :], ps32(bk)),
                                 reads=[pb[bk]], writes=[BxbT[mt]])
                U4 = U.rearrange("p (a b) n -> p a b n", b=8)
                for gl in range(8):
                    bk = nextbank()

                    def fr(e, gl=gl, bk=bk):
                        ins = None
                        for j in range(8):
                            ins = e.matmul(ps32(bk).rearrange("p (a n) -> p a n", a=8), Sel[:, gl * 8 + j, :], xbT[:, :, j:512:8],
                                           start=(j == 0), stop=(j == 7))
                        return ins
                    P.op("pe", fr, reads=[Bsel] + BxbT, writes=[pb[bk]])
                    dst = U4[:, :, gl, tile * 64:(tile + 1) * 64]
                    if evac_eng() == "act":
                        P.op("act", lambda e, dst=dst, bk=bk: e.activation(dst, ps32(bk).rearrange("p (a n) -> p a n", a=8), AF.Copy),
                             reads=[pb[bk]], writes=BU)
                    else:
                        P.op("dve", lambda e, dst=dst, bk=bk: e.tensor_copy(dst, ps32(bk).rearrange("p (a n) -> p a n", a=8)),
                             reads=[pb[bk]], writes=BU)

        def scan_stream(kind, N, nseq, L):
            t_ = [o_work]

            def al2(n):
                o = t_[0]
                t_[0] += n
                assert t_[0] - o_region <= REGION_WORDS, (t_[0] - o_region, REGION_WORDS)
                return o
            bg = 1024 // N
            slots = [al2(1024) for _ in range(9)]
            cosT, sinT, p1, p2, tre, tim, D0, Tre, Tim = [A.f32(o, 1024) for o in slots]
            Hre, Him = tre, tim
            o_hsh = al2(1024)
            Hshr = A.bf(o_hsh, 1024)
            Hshi = A.bf(o_hsh + 512, 1024)
            o_smb = al2(2560)
            SMb = A.bf(o_smb, 5 * bg * 128).rearrange("p (k g n) -> p k g n", k=5, g=bg)
            o_ysb = al2(1024)
            Ysb = A.bf(o_ysb, 8 * N).rearrange("p (g n) -> p g n", g=8)
            o_fs = al2(512)
            FSr = A.f32(o_fs, 256).rearrange("p (s g) -> p s g", s=4)
            FSi = A.f32(o_fs + 256, 256).rearrange("p (s g) -> p s g", s=4)
            o_idx = al2(256)
            idxi = A.i32(o_idx, L)
            idxf = A.f32(o_idx, L)
            Bc, Bs, Bp1, Bp2, Btr, Bti, BD0, BTr, BTi, Bhshr, Bsmb, Bysb, Bfs, Bidx = [
                Buf(n) for n in "cos sin p1 p2 tre tim D0 Tre Tim hshr smb ysb fs idx".split()]
            Bhshi, Bp3, Bp4, BGre, BGim = [Buf(n) for n in "hshi p3 p4 Gre Gim".split()]
            p3, p4, Gre, Gim = [A.f32(o_gate + k_ * 1024, 1024) for k_ in range(4)]
            V = lambda fn, r, w: P.op("dve", fn, reads=r, writes=w)
            G_ = lambda fn, r, w: P.op("dve", fn, reads=r, writes=w)
            v4 = lambda ap: ap.rearrange("p (g s l) -> p g s l", g=bg, s=nseq)
            v3 = lambda ap: ap.rearrange("p (g x) -> p g x", g=bg)
            P.op("pool", lambda e: e.iota(idxi, [[1, L]], base=1, channel_multiplier=0), writes=[Bidx])
            V(lambda e: e.tensor_copy(idxf, idxi), [Bidx], [Bidx])
            V(lambda e: e.tensor_scalar(half(idxf, 1), half(idxf, 1), -1.0, float(L + 1), ALU.mult, ALU.add), [Bidx], [Bidx])
            if kind != "prompt":
                for (ini, h0s, car) in ((ini_r, h0r_s, car_r), (ini_i, h0i_s, car_i)):
                    if kind == "other":
                        V(lambda e, ini=ini, h0s=h0s: e.tensor_tensor(ini, h0s, rho8, ALU.mult), [Bs5p], [Bs5p])
                    else:
                        V(lambda e, ini=ini, h0s=h0s: e.tensor_tensor(half(ini, 0), half(h0s, 0), half(rho8, 0), ALU.mult), [Bs5p], [Bs5p])
                        V(lambda e, ini=ini, car=car: e.tensor_tensor(half(ini, 1), half(car, 1), half(rho8, 1), ALU.mult), [Bs5p], [Bs5p])
            def batch(bi):
                g0 = bi * bg
                gs = slice(g0, g0 + bg)
                P.dma("sp", SMb, smat[:, :, gs, :].rearrange("k p g n -> p k g n"), writes=[Bsmb])
                q = nextquad()
                for ri in range(2):
                    def fg(e, ri=ri, q=q, g0=g0):
                        ins = None
                        for gl in range(bg):
                            ins = e.matmul(psum[:, (q + 2 * ri) * 512 + gl * N:(q + 2 * ri) * 512 + (gl + 1) * N],
                                           SMb[:, 1 + ri, gl, :], U[:, g0 + gl, 0:N], start=True, stop=True)
                        return ins
                    P.op("pe", fg, reads=[Bsmb, BU[g0 // 8]], writes=[pb[q + 2 * ri], pb[q + 2 * ri + 1]])
                Gre_p = psum[:, q * 512:q * 512 + 1024]
                Gim_p = psum[:, (q + 2) * 512:(q + 2) * 512 + 1024]
                Gre, Gim = Gre_p, Gim_p
                BGr = [pb[q], pb[q + 1]]
                BGi = [pb[q + 2], pb[q + 3]]
                ph_b = phi8[:, gs].unsqueeze(2).broadcast_to([128, bg, L])
                ix_b = idxf.unsqueeze(1).broadcast_to([128, bg, L])
                a3 = lambda ap: ap[:, 0:bg * L].rearrange("p (g l) -> p g l", g=bg)
                V(lambda e: e.tensor_tensor(a3(p1), ph_b, ix_b, ALU.mult), [Bs5p, Bidx], [Bp1])
                sin_reduced(p1[:, 0:bg * L], p2[:, 0:bg * L], sinT[:, 0:bg * L], [Bp1, Bp2], Bs)
                G_(lambda e: e.tensor_tensor(a3(p3), ph_b, ix_b, ALU.mult), [Bs5p, Bidx], [Bp3])
                G_(lambda e: e.tensor_scalar(p3[:, 0:bg * L], p3[:, 0:bg * L], math.pi / 2, None, ALU.add), [Bp3], [Bp3])
                sin_reduced(p3[:, 0:bg * L], p4[:, 0:bg * L], cosT[:, 0:bg * L], [Bp3, Bp4], Bc)
                cb = a3(cosT).unsqueeze(2).broadcast_to([128, bg, nseq, L])
                sb = a3(sinT).unsqueeze(2).broadcast_to([128, bg, nseq, L])
                V(lambda e: e.tensor_tensor(v4(p1), v4(Gre), cb, ALU.mult), BGr + [Bc], [Bp1])
                V(lambda e: e.tensor_tensor(v4(p2), v4(Gim), sb, ALU.mult), BGi + [Bs], [Bp2])
                V(lambda e: e.tensor_tensor(tre, p1, p2, ALU.add), [Bp1, Bp2], [Btr])
                G_(lambda e: e.tensor_tensor(v4(p3), v4(Gim), cb, ALU.mult), BGi + [Bc], [Bp3])
                G_(lambda e: e.tensor_tensor(v4(p4), v4(Gre), sb, ALU.mult), BGr + [Bs], [Bp4])
                G_(lambda e: e.tensor_tensor(tim, p3, p4, ALU.subtract), [Bp3, Bp4], [Bti])
                V(lambda e: e.tensor_copy(v3(D0), rho8[:, gs].unsqueeze(2).broadcast_to([128, bg, N])), [Bs5p], [BD0])
                V(lambda e: e.memset(half(v4(D0), 0)[:, :, :, 0:1], 0.0), [], [BD0])
                V(lambda e: e.memset(half(v4(D0), 1)[:, :, :, L - 1:L], 0.0), [], [BD0])
                if kind != "prompt":
                    for (tt, Bt_, ini, E_) in ((tre, Btr, ini_r, V), (tim, Bti, ini_i, G_)):
                        E_(lambda e, tt=tt, ini=ini: e.tensor_tensor(half(v3(tt), 0)[:, :, 0:1], half(v3(tt), 0)[:, :, 0:1],
                                                                    half(ini, 0)[:, gs].unsqueeze(2), ALU.add), [Bs5p, Bt_], [Bt_])
                        E_(lambda e, tt=tt, ini=ini: e.tensor_tensor(half(v3(tt), 1)[:, :, N - 1:N], half(v3(tt), 1)[:, :, N - 1:N],
                                                                    half(ini, 1)[:, gs].unsqueeze(2), ALU.add), [Bs5p, Bt_], [Bt_])
                for (src, Bsrc, dst, Bdst, E_) in ((tre, Btr, Tre, BTr, V), (tim, Bti, Tim, BTi, V)):
                    E_(lambda e, src=src, dst=dst: e.tensor_tensor_scan(half(dst, 0), half(D0, 0), half(src, 0), 0.0, ALU.mult, ALU.add),
                       [Bsrc, BD0], [Bdst])
                    E_(lambda e, src=src, dst=dst: e.tensor_tensor_scan(half(dst, 1)[:, ::-1], half(D0, 1)[:, ::-1], half(src, 1)[:, ::-1],
                                                                       0.0, ALU.mult, ALU.add), [Bsrc, BD0], [Bdst])
                if kind == "other":
                    c0 = half(a3(cosT), 1)[:, :, 0:1]
                    s0 = half(a3(sinT), 1)[:, :, 0:1]
                    tr0 = half(v3(Tre), 1)[:, :, 0:1]
                    ti0 = half(v3(Tim), 1)[:, :, 0:1]
                    q1 = half(v3(p1), 1)[:, :, 0:1]
                    q2 = half(v3(p2), 1)[:, :, 0:1]
                    V(lambda e: e.tensor_tensor(q1, c0, tr0, ALU.mult), [Bc, BTr], [Bp1])
                    V(lambda e: e.tensor_tensor(q2, s0, ti0, ALU.mult), [Bs, BTi], [Bp2])
                    V(lambda e: e.tensor_tensor(half(car_r, 1)[:, gs].unsqueeze(2), q1, q2, ALU.subtract), [Bp1, Bp2], [Bs5p])
                    V(lambda e: e.tensor_tensor(q1, s0, tr0, ALU.mult), [Bs, BTr, Bs5p], [Bp1])
                    V(lambda e: e.tensor_tensor(q2, c0, ti0, ALU.mult), [Bc, BTi, Bs5p], [Bp2])
                    V(lambda e: e.tensor_tensor(half(car_i, 1)[:, gs].unsqueeze(2), q1, q2, ALU.add), [Bp1, Bp2], [Bs5p])
                    return
                if kind == "prompt" and bi == 0:
                    dbg("idxf", idxf, [Bidx])
                    dbg("cosT", cosT, [Bc]); dbg("sinT", sinT, [Bs]); dbg("tre", tre, [Btr]); dbg("tim", tim, [Bti])
                    dbg("D0", D0, [BD0]); dbg("Tre", Tre, [BTr]); dbg("Tim", Tim, [BTi])
                V(lambda e: e.tensor_tensor(v4(p1), v4(Tre), cb, ALU.mult), [BTr, Bc], [Bp1])
                V(lambda e: e.tensor_tensor(v4(p2), v4(Tim), sb, ALU.mult), [BTi, Bs], [Bp2])
                V(lambda e: e.tensor_tensor(Hre, p1, p2, ALU.subtract), [Bp1, Bp2], [Btr])
                G_(lambda e: e.tensor_tensor(v4(p3), v4(Tre), sb, ALU.mult), [BTr, Bs], [Bp3])
                G_(lambda e: e.tensor_tensor(v4(p4), v4(Tim), cb, ALU.mult), [BTi, Bc], [Bp4])
                G_(lambda e: e.tensor_tensor(Him, p3, p4, ALU.add), [Bp3, Bp4], [Bti])
                for (Hs, Hh, BH, h0s, car, E_, Bh_) in ((Hshr, Hre, Btr, h0r_s, car_r, V, Bhshr), (Hshi, Him, Bti, h0i_s, car_i, G_, Bhshi)):
                    P.op("act", lambda e, Hs=Hs, Hh=Hh: e.activation(half(v4(Hs), 0)[:, :, :, 1:L], half(v4(Hh), 0)[:, :, :, 0:L - 1], AF.Copy),
                         reads=[BH], writes=[Bh_])
                    P.op("act", lambda e, Hs=Hs, Hh=Hh: e.activation(half(v4(Hs), 1)[:, :, :, 0:L - 1], half(v4(Hh), 1)[:, :, :, 1:L], AF.Copy),
                         reads=[BH], writes=[Bh_])
                    if kind == "prompt":
                        E_(lambda e, Hs=Hs: e.memset(half(v4(Hs), 0)[:, :, :, 0:1], 0.0), [], [Bh_])
                        E_(lambda e, Hs=Hs: e.memset(half(v4(Hs), 1)[:, :, :, L - 1:L], 0.0), [], [Bh_])
                    else:
                        E_(lambda e, Hs=Hs, h0s=h0s: e.tensor_copy(half(v3(Hs), 0)[:, :, 0:1], half(h0s, 0)[:, gs].unsqueeze(2)), [Bs5p], [Bh_])
                        E_(lambda e, Hs=Hs, car=car: e.tensor_copy(half(v3(Hs), 1)[:, :, N - 1:N], half(car, 1)[:, gs].unsqueeze(2)), [Bs5p], [Bh_])
                if kind == "prompt":
                    for (FS, Hh, BH, E_) in ((FSr, Hre, Btr, V), (FSi, Him, Bti, G_)):
                        E_(lambda e, FS=FS, Hh=Hh: e.tensor_copy(half(FS, 0)[:, :, gs].rearrange("p s g -> p g s"),
                                                                 half(v4(Hh), 0)[:, :, :, L - 1]), [BH], [Bfs])
                        E_(lambda e, FS=FS, Hh=Hh: e.tensor_copy(half(FS, 1)[:, :, gs].rearrange("p s g -> p g s"),
                                                                 half(v4(Hh), 1)[:, :, :, 0]), [BH], [Bfs])
                if kind == "prompt" and bi == 0:
                    dbg("Hre", Hre, [Btr]); dbg("Him", Him, [Bti]); dbg("Hshr", Hshr, [Bhshr]); dbg("Hshi", Hshi, [Bhshi])
                for gl in range(bg):
                    g = g0 + gl
                    bk = nextbank()

                    def fy(e, gl=gl, g=g, bk=bk):
                        e.matmul(ps32(bk, N), SMb[:, 0, gl, :], U[:, g, 0:N], start=True, stop=False)
                        e.matmul(ps32(bk, N), SMb[:, 3, gl, :], v3(Hshr)[:, gl, :], start=False, stop=False)
                        return e.matmul(ps32(bk, N), SMb[:, 4, gl, :], v3(Hshi)[:, gl, :], start=False, stop=True)
                    P.op("pe", fy, reads=[Bsmb, BU[g // 8], Bhshr, Bhshi], writes=[pb[bk]])
                    P.op("act", lambda e, g=g, bk=bk: e.activation(Ysb[:, g % 8, :], ps32(bk, N), AF.Copy), reads=[pb[bk]], writes=[Bysb])
                if kind == "prompt" and bi == 0:
                    dbg("Ysb", Ysb, [Bysb])
                if (g0 + bg) % 8 == 0:
                    gt = g0 // 8
                    for nb in range(N // 64):
                        bk = nextbank()

                        def fb(e, nb=nb, bk=bk):
                            ins = None
                            for i in range(8):
                                for gl in range(8):
                                    ins = e.matmul(ps32(bk)[:, i:512:8], Sel[:, i * 8 + gl, :], Ysb[:, gl, nb * 64:(nb + 1) * 64],
                                                   start=(gl == 0), stop=(gl == 7))
                            return ins
                        P.op("pe", fb, reads=[Bsel, Bysb], writes=[pb[bk]])
                        P.op("act", lambda e, gt=gt, nb=nb, bk=bk: e.activation(ybT[:, gt, nb * 512:(nb + 1) * 512], ps32(bk), AF.Gelu_apprx_tanh),
                             reads=[pb[bk]], writes=[BybT[gt]])

            for bi in range(64 // bg):
                batch(bi)
            if kind == "prompt":
                fsT = A.f32(slots[0], 128)
                BfsT = Buf("fsT")
                for (FS, dst) in ((FSr, sre), (FSi, sim)):
                    for sq in range(4):
                        bk = nextbank()
                        P.op("pe", lambda e, FS=FS, sq=sq, bk=bk: e.transpose(ps32(bk, 128)[0:64], FS[:, sq, :], id32),
                             reads=[Bfs, Bconst], writes=[pb[bk]])
                        V(lambda e, bk=bk: e.tensor_copy(fsT[0:64], ps32(bk, 128)[0:64]), [pb[bk], Bc], [BfsT])
                        P.dma("sp", dst[sq].rearrange("d g s -> g d s"), fsT[0:64].rearrange("p (d s) -> p d s", d=2),
                              reads=[BfsT], is_output=True)

        def glu(ntok):
            t_ = [o_work]

            def al2(n):
                o = t_[0]
                t_[0] += n
                return o
            o_wg = al2(4096)
            o_yt = al2(2048)
            o_sg = al2(1024)
            wg = A.bf(o_wg, 8192).rearrange("p (k n) -> p k n", k=8)
            Bwg = Buf("wglu")
            ytmp = A.bf(o_yt, 4096).rearrange("p (k n) -> p k n", k=8)
            Byt = Buf("ytmp")
            sgs = [A.f32(o_sg, 512), A.f32(o_sg + 512, 512)]
            Bsg = [Buf("sg0"), Buf("sg1")]
            for kk in range(2):
                P.dma("pool", wg[:, kk * 4:(kk + 1) * 4, :], w_glu3[:, kk * 4:(kk + 1) * 4, :], writes=[Bwg])
            for tile in range(ntok // 512):
                ts = slice(tile * 512, (tile + 1) * 512)
                for mt in range(8):
                    bk = nextbank()

                    def fgl(e, mt=mt, bk=bk, ts=ts):
                        ins = None
                        for kt in range(8):
                            ins = e.matmul(ps32(bk), wg[:, kt, mt * 128:(mt + 1) * 128], ybT[:, kt, ts], start=(kt == 0), stop=(kt == 7))
                        return ins
                    P.op("pe", fgl, reads=[Bwg] + BybT, writes=[pb[bk]])
                    sg, Bs_ = sgs[mt % 2], Bsg[mt % 2]
                    P.op("act", lambda e, sg=sg, mt=mt, bk=bk: e.activation(sg, ps32(bk), AF.Sigmoid, bias=bglu_c[:, mt:mt + 1]),
                         reads=[pb[bk], Bconst], writes=[Bs_])
                    P.op("dve", lambda e, sg=sg, mt=mt, ts=ts: e.tensor_tensor(ytmp[:, mt, :], sg, ybT[:, mt, ts], ALU.mult),
                         reads=[Bs_, BybT[mt]], writes=[Byt])
                P.op("dve", lambda e, ts=ts: e.tensor_copy(ybT[:, :, ts], ytmp), reads=[Byt], writes=BybT)

        if S["name"] == "prompt":
            build_U(xp, 1024, False, 0)
            P.barrier()
            dbg("U", U, BU)
            dbg("s5p", A.f32(o_s5p, 512), [Bs5p])
            scan_stream("prompt", 128, 4, 32)
            P.barrier()
            dbg("yg", ybT[:, :, 0:1024], BybT)
        else:
            build_U(xs_oth, 2048, True, 2048)
            P.barrier()
            scan_stream("other", 256, 1, 256)
            P.barrier()
            build_U(xs_own, 2048, True, 0)
            P.barrier()
            scan_stream("own", 256, 1, 256)
        P.barrier()
        glu(S["ntok"])

    phase0()
    P.barrier()
    phase_mod()
    P.barrier()
    sets = [
        dict(name="prompt", ms=0, emb=False, x=xp, y=yp, ntok=1024, yoff=0, ohoff=0),
        dict(name="sample", ms=1, emb=True, x=xs_own, y=ys, ntok=2048, yoff=0, ohoff=0),
    ]
    if STAGE >= 2:
        phase_s5build()
        P.barrier()
    else:
        P.op("dve", lambda e: e.memset(ybT, 0.0), writes=BybT)
    for S in sets:
        if STAGE >= 3:
            P.barrier()
            phase1(S)
        P.barrier()
        compute_gates(S["ms"])
        P.barrier()
        phase2(S)
    P.emit()
    nc._dbg_names = dbg_names
    return nc


def _core_inputs(r, inp):
    b, hf = r // 2, r % 2
    f = np.ascontiguousarray
    xs = inp["x_sample"][b]
    if hf == 0:
        own, oth = xs[0:2048], xs[2048:4096]
        pos = np.arange(4096)
    else:
        own, oth = xs[4095:2047:-1], xs[2047::-1]
        pos = 4095 - np.arange(4096)
    xpr = inp["x_prompt"][4 * r:4 * r + 4]
    if hf == 1:
        xpr = xpr[:, ::-1]
    ohm = np.zeros((128, 4096), np.float32)
    ohm[pos // 64, np.arange(4096)] = 1.0
    ohm[64 + pos % 64, np.arange(4096)] = 1.0
    dsel = slice(None) if hf == 0 else slice(None, None, -1)
    w_s = inp["w_spatial"][0]
    b_s = inp["b_spatial"][0]
    if hf == 1:
        w_s = w_s[:, ::-1, ::-1]
        b_s = b_s[:, ::-1]
    m = {
        "xs_own": f(own), "xs_oth": f(oth), "xp": f(xpr.reshape(1024, D)), "oh": ohm,
        "cvec": f(np.stack([inp["c_ctx"], inp["c"][b]])),
        "h0re": f(inp["state_ssm_re"][b, 0][dsel]), "h0im": f(inp["state_ssm_im"][b, 0][dsel]),
        "w_ada": inp["w_ada"][0], "b_ada": inp["b_ada"][0], "g_mix": inp["g_norm_mix"][0],
        "w_in": inp["w_in"][0], "g_sgu": inp["g_sgu"][0], "w_s": f(w_s), "b_s": f(b_s),
        "a_re": f(inp["ssm_a_re"][0][dsel]), "a_im": f(inp["ssm_a_im"][0][dsel]),
        "log_dt": f(inp["ssm_log_dt"][0][dsel]),
        "b_re": f(inp["ssm_b_re"][0][dsel]), "b_im": f(inp["ssm_b_im"][0][dsel]),
        "c_re": f(inp["ssm_c_re"][0][dsel]), "c_im": f(inp["ssm_c_im"][0][dsel]),
        "ssm_d": inp["ssm_d"][0], "w_glu": inp["w_glu"][0], "b_glu": inp["b_glu"][0],
        "w_pa": inp["w_proj_a"][0], "w_pb": inp["w_proj_b"][0], "w_out": inp["w_out"][0],
        "g_mlp": inp["g_norm_mlp"][0], "w_mi": inp["w_mlp_in"][0], "w_mo": inp["w_mlp_out"][0],
        "g_fin": inp["g_final"],
    }
    return {k: np.ascontiguousarray(np.asarray(v, dtype=np.float32)) for k, v in m.items()}


_NC_CACHE = {}


def kernel(**inputs):
    inp = {k: np.asarray(v) for k, v in inputs.items()}
    if "nc" not in _NC_CACHE:
        _NC_CACHE["nc"] = build_program()
    nc = _NC_CACHE["nc"]
    ncores = int(os.environ.get("MK_NCORES", "8"))
    in_maps = [_core_inputs(r, inp) for r in range(ncores)]
    if os.environ.get("MK_TRACE", "0") == "1":
        res = run_bass_kernel_spmd(nc, in_maps, core_ids=list(range(ncores)), trace=True)
        print("MK exec_time_ns", res.exec_time_ns)
    else:
        res = run_bass_kernel_spmd(nc, in_maps, core_ids=list(range(ncores)))
    if os.environ.get("MK_DBG", "0") == "1":
        _NC_CACHE["dbg"] = {k: np.asarray(res.results[0][k]).astype(np.float32) for k in list(nc._dbg_names) + ["smat"]}
    y_prompt = np.zeros((32, 256, D), np.float32)
    y_sample = np.zeros((4, 4096, D), np.float32)
    new_re = np.zeros((32, 1, 2, G, NP), np.float32)
    new_im = np.zeros((32, 1, 2, G, NP), np.float32)
    for r in range(ncores):
        o = res.results[r]
        b, hf = r // 2, r % 2
        ypr = o["yp"].reshape(4, 256, D)
        ysr = o["ys"]
        s_re = o["sre"]
        s_im = o["sim"]
        if hf == 1:
            ypr = ypr[:, ::-1]
            ysr = ysr[::-1]
            s_re = s_re[:, ::-1]
            s_im = s_im[:, ::-1]
            y_sample[b, 2048:4096] = ysr
        else:
            y_sample[b, 0:2048] = ysr
        y_prompt[4 * r:4 * r + 4] = ypr
        new_re[4 * r:4 * r + 4, 0] = s_re
        new_im[4 * r:4 * r + 4, 0] = s_im
    return (y_prompt, y_sample, new_re, new_im)
```

```python
import os
import math
import numpy as np
import concourse.bass as bass
import concourse.mybir as mybir
from concourse.bass_utils import run_bass_kernel_spmd

F32 = mybir.dt.float32
BF16 = mybir.dt.bfloat16
I32 = mybir.dt.int32
ALU = mybir.AluOpType
AF = mybir.ActivationFunctionType

D = 2048
DA = 1024
DB = 1024
DFF = 8192
DIN = 7168
G = 64
NP = 64
EPS = 1e-6
ENGS = ("pe", "act", "dve", "pool", "sp")
SEM_ROT = 12000
MAGIC = 12582912.0
TWO_PI = 2.0 * math.pi

STAGE = int(os.environ.get("MK_STAGE", "9"))


class Buf:
    __slots__ = ("name", "w", "r")

    def __init__(self, name):
        self.name = name
        self.w = None
        self.r = {}


class Prog:
    def __init__(self, nc, n_dma_sems=(("sp", 8), ("pool", 6), ("act", 4))):
        self.nc = nc
        self.q = {e: [] for e in ENGS}
        self.cnt = {e: 0 for e in ENGS}
        self.sems = {e: [nc.alloc_semaphore(f"s_{e}_0")] for e in ENGS}
        self.seen = {e: {} for e in ENGS}
        self.dma_ring = {}
        for e, n in n_dma_sems:
            self.dma_ring[e] = dict(sems=[nc.alloc_semaphore(f"d_{e}_{i}") for i in range(n)],
                                    val=[0] * n, pos=0)
        self.out_tokens = []
        self.own = {e: set() for e in ENGS}

    def _need(self, eng, tok):
        if tok is None:
            return
        sem, val = tok
        if eng == "pe" and sem.num in self.own["pe"]:
            return
        if self.seen[eng].get(sem.num, 0) >= val:
            return
        self.seen[eng][sem.num] = val
        self.q[eng].append(lambda e, s=sem, v=val: e.wait_ge(s, v))

    def _next_token(self, eng):
        if self.cnt[eng] >= SEM_ROT:
            self.sems[eng].append(self.nc.alloc_semaphore(f"s_{eng}_{len(self.sems[eng])}"))
            self.cnt[eng] = 0
        self.cnt[eng] += 1
        sem = self.sems[eng][-1]
        self.own[eng].add(sem.num)
        return (sem, self.cnt[eng])

    def _deps(self, eng, reads, writes):
        for b in reads:
            self._need(eng, b.w)
        for b in writes:
            self._need(eng, b.w)
            for t in b.r.values():
                self._need(eng, t)

    def _commit(self, tok, reads, writes):
        for b in reads:
            b.r[tok[0].num] = tok
        for b in writes:
            b.w = tok
            b.r = {}

    def op(self, eng, fn, reads=(), writes=()):
        self._deps(eng, reads, writes)
        tok = self._next_token(eng)
        sem = tok[0]
        self.q[eng].append(lambda e, f=fn, s=sem: f(e).then_inc(s, 1))
        self._commit(tok, reads, writes)
        return tok

    def dma(self, eng, out_ap, in_ap, reads=(), writes=(), is_output=False, **kw):
        ring = self.dma_ring[eng]
        i = ring["pos"]
        ring["pos"] = (i + 1) % len(ring["sems"])
        sem = ring["sems"][i]
        if ring["val"][i] > 0:
            self._need(eng, (sem, ring["val"][i]))
        self._deps(eng, reads, writes)
        ring["val"][i] += 16
        tok = (sem, ring["val"][i])
        self.q[eng].append(
            lambda e, o=out_ap, a=in_ap, s=sem, k=kw: e.dma_start(out=o, in_=a, **k).then_inc(s, 16))
        self._commit(tok, reads, writes)
        if is_output:
            self.out_tokens.append(tok)
        return tok

    def barrier(self):
        last = []
        for e in ENGS:
            if self.cnt[e] > 0:
                last.append((self.sems[e][-1], self.cnt[e]))
        for qn, ring in self.dma_ring.items():
            if qn == "pool":
                continue
            for sem, v in zip(ring["sems"], ring["val"]):
                if v > 0:
                    last.append((sem, v))
        for e in ENGS:
            for tok in last:
                self._need(e, tok)

    def emit(self):
        nc = self.nc
        for tok in self.out_tokens:
            self._need("sp", tok)
        for e in ENGS:
            if e != "sp" and self.cnt[e] > 0:
                self._need("sp", (self.sems[e][-1], self.cnt[e]))
        with nc.Block() as block:
            @block.tensor
            def _(e):
                for f in self.q["pe"]:
                    f(e)

            @block.scalar
            def _(e):
                for f in self.q["act"]:
                    f(e)

            @block.vector
            def _(e):
                for f in self.q["dve"]:
                    f(e)

            @block.gpsimd
            def _(e):
                for f in self.q["pool"]:
                    f(e)

            @block.sync
            def _(e):
                for f in self.q["sp"]:
                    f(e)


class Arena:
    def __init__(self, nc, words):
        self.t32 = nc.alloc_sbuf_tensor("arena", [128, words], F32)
        self.t16 = self.t32.bitcast(BF16)
        self.ti = self.t32.bitcast(I32)
        self.words = words
        self.top = 0

    def alloc(self, words):
        o = self.top
        self.top += (words + 15) // 16 * 16
        assert self.top <= self.words, (self.top, self.words)
        return o

    def f32(self, off, n):
        return self.t32[:, off:off + n]

    def bf(self, off, n):
        return self.t16[:, 2 * off:2 * off + n]

    def i32(self, off, n):
        return self.ti[:, off:off + n]


def build_program():
    nc = bass.Bass("TRN2", target_bir_lowering=False)

    def din(name, shape):
        return nc.dram_tensor(name, list(shape), F32, kind="ExternalInput").ap()

    def dout(name, shape):
        return nc.dram_tensor(name, list(shape), F32, kind="ExternalOutput").ap()

    xs_own = din("xs_own", [2048, D])
    xs_oth = din("xs_oth", [2048, D])
    xp = din("xp", [1024, D])
    oh = din("oh", [128, 4096])
    cvec = din("cvec", [2, D])
    h0re = din("h0re", [2, G, NP])
    h0im = din("h0im", [2, G, NP])
    w_ada = din("w_ada", [D, 6 * D])
    b_ada = din("b_ada", [6 * D])
    g_mix = din("g_mix", [D])
    w_in = din("w_in", [D, DIN])
    g_sgu = din("g_sgu", [DA])
    w_s = din("w_s", [4, 128, 128])
    b_s = din("b_s", [4, 128])
    a_re = din("a_re", [2, G, NP])
    a_im = din("a_im", [2, G, NP])
    log_dt = din("log_dt", [2, G])
    b_re = din("b_re", [2, G, NP, 16])
    b_im = din("b_im", [2, G, NP, 16])
    c_re = din("c_re", [2, G, 16, NP])
    c_im = din("c_im", [2, G, 16, NP])
    ssm_d = din("ssm_d", [DB])
    w_glu = din("w_glu", [DB, DB])
    b_glu = din("b_glu", [DB])
    w_pa = din("w_pa", [DA, D])
    w_pb = din("w_pb", [DB, D])
    w_out = din("w_out", [D, D])
    g_mlp = din("g_mlp", [D])
    w_mi = din("w_mi", [D, DFF])
    w_mo = din("w_mo", [DFF, D])
    g_fin = din("g_fin", [D])
    yp = dout("yp", [1024, D])
    ys = dout("ys", [2048, D])
    sre = dout("sre", [4, 2, G, NP])
    sim = dout("sim", [4, 2, G, NP])
    smat = nc.dram_tensor("smat", [5, 128, G, 128], BF16,
                          kind="ExternalOutput" if os.environ.get("MK_DBG", "0") == "1" else "Internal").ap()

    P = Prog(nc)
    A = Arena(nc, 53000)
    DBG = os.environ.get("MK_DBG", "0") == "1"
    dbg_names = []

    def dbg(name, ap, bufs):
        if not DBG:
            return
        shape = [int(x) for x in ap.shape]
        dt_ = ap.dtype
        t = nc.dram_tensor("dbg_" + name, shape, dt_, kind="ExternalOutput").ap()
        P.dma("sp", t, ap, reads=bufs, is_output=True)
        dbg_names.append("dbg_" + name)
    psum = nc.alloc_psum_tensor("psum", [128, 4096], F32)
    psum16 = psum.bitcast(BF16)
    pb = [Buf(f"bank{i}") for i in range(8)]

    def ps32(i, n=512, off=0):
        return psum[:, i * 512 + off:i * 512 + off + n]

    def ps16(i, n=1024, off=0):
        return psum16[:, i * 1024 + off:i * 1024 + off + n]

    bank_rr = [0]

    def nextbank():
        b = bank_rr[0]
        bank_rr[0] = (b + 1) % 8
        return b

    quad_rr = [0]

    def nextquad():
        q = quad_rr[0]
        quad_rr[0] = 1 - q
        return q * 4

    alt = [0]

    def evac_eng():
        alt[0] ^= 1
        return "act" if alt[0] else "dve"

    o_id16 = A.alloc(64)
    o_id32 = A.alloc(128)
    o_mod = A.alloc(128)
    o_gate = A.alloc(2 * D)
    o_gfin = A.alloc(D)
    o_bglu = A.alloc(8)
    o_gsgu = A.alloc(8)
    o_dsk = A.alloc(64)
    o_bsbc = A.alloc(512)
    o_wsT = A.alloc(256)
    o_E = A.alloc(1024)
    o_cols = A.alloc(64)
    o_ybT = A.alloc(8192)
    s5p_off = A.alloc(512)
    NRING = 4
    o_ring = [A.alloc(2048) for _ in range(NRING)]
    o_region = A.top
    REGION_WORDS = A.words - o_region

    id16 = A.bf(o_id16, 128)
    id32 = A.f32(o_id32, 128)
    Bconst = Buf("const")
    modc = A.f32(o_mod, 128).rearrange("p (k f m) -> p k f m", k=4, f=16)
    Bmod = Buf("mod")
    gate_bc = A.f32(o_gate, 2 * D)
    Bgate = Buf("gate")
    gfin_bc = A.f32(o_gfin, D)
    bglu_c = A.f32(o_bglu, 8)
    gsgu_c = A.f32(o_gsgu, 8)
    dsk_c = A.f32(o_dsk, 64)
    bs_bc = A.f32(o_bsbc, 512).rearrange("p (g i) -> p g i", g=4)
    wsT = A.bf(o_wsT, 512).rearrange("p (g i) -> p g i", g=4)
    Etab = A.f32(o_E, 1024)
    cols = A.f32(o_cols, 64)
    Bcols = Buf("cols")
    ybT = A.bf(o_ybT, 16384).rearrange("p (k n) -> p k n", k=8)
    BybT = [Buf(f"ybT{k}") for k in range(8)]

    ring_pos = [0]
    ringB = [Buf(f"ring{i}") for i in range(NRING)]

    def ring_next():
        i = ring_pos[0]
        ring_pos[0] = (i + 1) % NRING
        return o_ring[i], ringB[i]

    def wload(dram3, k0, nk, c0, ncol, q="pool"):
        off, B = ring_next()
        dst = A.bf(off, nk * ncol).rearrange("p (k n) -> p k n", k=nk)
        P.dma(q, dst, dram3[:, k0:k0 + nk, c0:c0 + ncol], writes=[B])
        return dst, B

    def wload2(dA, dB_, k0, nk, c0, ncol):
        off, B = ring_next()
        d1 = A.bf(off, nk * ncol).rearrange("p (k n) -> p k n", k=nk)
        d2 = A.bf(off + nk * ncol // 2, nk * ncol).rearrange("p (k n) -> p k n", k=nk)
        P.dma("pool", d1, dA[:, k0:k0 + nk, c0:c0 + ncol], writes=[B])
        P.dma("pool", d2, dB_[:, k0:k0 + nk, c0:c0 + ncol], writes=[B])
        return d1, d2, B

    def wload32(dram3, k0, nk, c0, ncol, q="sp"):
        off, B = ring_next()
        dst = A.f32(off, nk * ncol).rearrange("p (k n) -> p k n", k=nk)
        P.dma(q, dst, dram3[:, k0:k0 + nk, c0:c0 + ncol], writes=[B])
        return dst, B

    NSLOT = 108
    wsc = nc.dram_tensor("wsc", [NSLOT, 128, 4096], BF16, kind="Internal").ap()
    slot_of = {}
    slotB = {}

    def conv(key, parts):
        i = len(slot_of)
        assert i < NSLOT
        slot_of[key] = i
        B = Buf(f"slot{i}")
        slotB[key] = B
        off = 0
        for (src, nk, ncol) in parts:
            P.dma("pool", wsc[i, :, off:off + nk * ncol].rearrange("p (k n) -> p k n", k=nk), src, writes=[B])
            off += nk * ncol

    def wl(key, nk, ncol, two=False):
        off, B = ring_next()
        n = nk * ncol * (2 if two else 1)
        P.dma("sp", A.bf(off, n), wsc[slot_of[key], :, 0:n], reads=[slotB[key]], writes=[B])
        d1 = A.bf(off, nk * ncol).rearrange("p (k n) -> p k n", k=nk)
        if not two:
            return d1, B
        d2 = A.bf(off + nk * ncol // 2, nk * ncol).rearrange("p (k n) -> p k n", k=nk)
        return d1, d2, B

    def convert_weights(first):
        if first:
            for ch in range(4):
                conv(("xb", ch), [(w_in3[:, :, 2 * DA + ch * 256:2 * DA + (ch + 1) * 256], 16, 256)])
            return
        for ch in range(4):
            conv(("u", ch), [(w_in3[:, :, ch * 256:(ch + 1) * 256], 16, 256)])
        for ncn in range(2):
            for hk in range(2):
                conv(("v", ncn, hk), [(w_in3[:, hk * 8:(hk + 1) * 8, DA + ncn * 512:DA + (ncn + 1) * 512], 8, 512)])
        for ft in range(16):
            conv(("g", ft), [(w_in3[:, :, 3 * DA + ft * 128:3 * DA + (ft + 1) * 128], 16, 128),
                             (w_in3[:, :, 3 * DA + D + ft * 128:3 * DA + D + (ft + 1) * 128], 16, 128)])
        for ft2 in range(8):
            conv(("p", ft2), [(w_pa3[:, :, ft2 * 256:(ft2 + 1) * 256], 8, 256), (w_pb3[:, :, ft2 * 256:(ft2 + 1) * 256], 8, 256)])
        for c in range(4):
            for hk in range(2):
                conv(("o", c, hk), [(w_out3[:, hk * 8:(hk + 1) * 8, c * 512:(c + 1) * 512], 8, 512)])
        for hh in range(2):
            for ch in range(16):
                conv(("mi", hh, ch), [(w_mi3[:, :, hh * 4096 + ch * 256:hh * 4096 + (ch + 1) * 256], 16, 256)])
        for hh in range(2):
            for c in range(4):
                for f8 in range(4):
                    conv(("mo", hh, c, f8), [(w_mo3[:, hh * 32 + f8 * 8:hh * 32 + (f8 + 1) * 8, c * 512:(c + 1) * 512], 8, 512)])

    w_in3 = w_in.rearrange("(k p) n -> p k n", p=128)
    w_ada3 = w_ada.rearrange("(k p) n -> p k n", p=128)
    w_pa3 = w_pa.rearrange("(k p) n -> p k n", p=128)
    w_pb3 = w_pb.rearrange("(k p) n -> p k n", p=128)
    w_out3 = w_out.rearrange("(k p) n -> p k n", p=128)
    w_mi3 = w_mi.rearrange("(k p) n -> p k n", p=128)
    w_mo3 = w_mo.rearrange("(k p) n -> p k n", p=128)
    w_glu3 = w_glu.rearrange("(k p) n -> p k n", p=128)

    def phase0():
        top = o_region
        o_iot = top; top += 128
        o_tmp = top; top += 2048
        o_sc = top; top += 32
        o_sg = top; top += 32
        o_screp = top; top += 2 * 2048
        o_bcol = top; top += 64
        o_gcol = top; top += 32
        o_ws = top; top += 512
        assert top - o_region <= REGION_WORDS
        Biot, Btmp, Bsc, Bscr, Bbcol, Bgcol, Bws = [Buf(n) for n in "iot tmp sc scr bcol gcol ws".split()]
        iot = A.i32(o_iot, 128)
        iotf = A.f32(o_tmp, 128)
        P.op("pool", lambda e: e.iota(iot, [[1, 128]], base=0, channel_multiplier=-1), writes=[Biot])
        P.op("dve", lambda e: e.tensor_copy(iotf, iot), reads=[Biot], writes=[Btmp])
        P.op("dve", lambda e: e.tensor_scalar(id32, iotf, 0.0, None, ALU.is_equal), reads=[Btmp], writes=[Bconst])
        P.op("dve", lambda e: e.tensor_copy(id16, id32), reads=[Bconst], writes=[Bconst])
        P.dma("sp", gfin_bc, g_fin.partition_broadcast(128), writes=[Bconst])
        P.dma("sp", A.f32(o_bsbc, 512), b_s.rearrange("g i -> (g i)").partition_broadcast(128), writes=[Bconst])
        P.dma("sp", bglu_c, b_glu.rearrange("(k p) -> p k", p=128), writes=[Bconst], allow_slow_non_contiguous=True)
        P.dma("sp", gsgu_c, g_sgu.rearrange("(k p) -> p k", p=128), writes=[Bconst], allow_slow_non_contiguous=True)
        for j in range(8):
            P.dma("sp", A.t32[j * 16:(j + 1) * 16, o_dsk:o_dsk + 64], ssm_d.rearrange("(g c) -> c g", c=16),
                  writes=[Bconst], allow_slow_non_contiguous=True)
        ws32 = A.f32(o_ws, 512).rearrange("p (g j) -> p g j", g=4)
        P.dma("sp", ws32, w_s.rearrange("g i j -> i g j"), writes=[Bws])
        for g4 in range(4):
            bk = nextbank()
            P.op("pe", lambda e, g4=g4, bk=bk: e.transpose(ps32(bk, 128), ws32[:, g4, :], id32),
                 reads=[Bws, Bconst], writes=[pb[bk]])
            P.op("dve", lambda e, g4=g4, bk=bk: e.tensor_copy(wsT[:, g4, :], ps32(bk, 128)),
                 reads=[pb[bk]], writes=[Bconst])
        qi = A.i32(o_tmp, 512)
        qf = A.f32(o_tmp + 512, 512)
        ang = A.f32(o_tmp + 1024, 512)
        kf = A.f32(o_tmp + 1536, 512)
        ri = A.i32(o_iot, 1)
        rf = A.f32(o_iot + 16, 1)
        P.op("pool", lambda e: e.iota(qi, [[1, 512]], base=0, channel_multiplier=0), reads=[Btmp], writes=[Btmp])
        P.op("dve", lambda e: e.tensor_copy(qf, qi), reads=[Btmp], writes=[Btmp])
        P.op("act", lambda e: e.activation(qf, qf, AF.Exp, scale=-math.log(10000.0) / 512.0), reads=[Btmp], writes=[Btmp])
        P.op("pool", lambda e: e.iota(ri, [[1, 1]], base=0, channel_multiplier=1), reads=[Biot], writes=[Biot])
        P.op("dve", lambda e: e.tensor_copy(rf, ri), reads=[Biot], writes=[Biot])
        P.op("dve", lambda e: e.tensor_scalar(A.f32(o_iot + 32, 1), rf, 64.0, -64.0, ALU.is_ge, ALU.mult),
             reads=[Biot], writes=[Biot])
        P.op("dve", lambda e: e.tensor_tensor(rf, rf, A.f32(o_iot + 32, 1), ALU.add), reads=[Biot], writes=[Biot])
        for half, shift in ((0, 0.0), (1, math.pi / 2)):
            P.op("dve", lambda e: e.tensor_scalar(ang, qf, rf, None, ALU.mult), reads=[Btmp, Biot], writes=[Btmp])
            P.op("dve", lambda e, shift=shift: e.tensor_scalar(ang, ang, shift, None, ALU.add), reads=[Btmp], writes=[Btmp])
            sin_reduced(ang, kf, Etab[:, half * 512:(half + 1) * 512], Btmp, Bconst)

    def sin_reduced(ang, kf, out, Bin, Bout, eng="dve"):
        Bin = list(Bin) if isinstance(Bin, (list, tuple)) else [Bin]
        P.op("act", lambda e: e.activation(kf, ang, AF.Identity, bias=MAGIC, scale=1.0 / TWO_PI), reads=Bin, writes=Bin)
        P.op("act", lambda e: e.activation(kf, kf, AF.Identity, bias=-MAGIC, scale=1.0), reads=Bin, writes=Bin)
        if eng == "dve":
            P.op(eng, lambda e: e.scalar_tensor_tensor(ang, kf, -TWO_PI, ang, ALU.mult, ALU.add), reads=Bin, writes=Bin)
        else:
            P.op(eng, lambda e: e.tensor_scalar(kf, kf, -TWO_PI, None, ALU.mult), reads=Bin, writes=Bin)
            P.op(eng, lambda e: e.tensor_tensor(ang, ang, kf, ALU.add), reads=Bin, writes=Bin)
        P.op(eng, lambda e: e.tensor_scalar(ang, ang, -math.pi, math.pi, ALU.max, ALU.min), reads=Bin, writes=Bin)
        P.op("act", lambda e: e.activation(out, ang, AF.Sin), reads=Bin, writes=[Bout])

    def phase_mod():
        top = o_region
        o_sc = top; top += 32
        o_sg = top; top += 32
        o_bcol = top; top += 64
        o_gcol = top; top += 32
        Bsc, Bbcol, Bgcol = Buf("sc"), Buf("bcol"), Buf("gcol")
        sc = A.f32(o_sc, 32).rearrange("p (k m) -> p k m", k=16)
        sg = A.f32(o_sg, 32).rearrange("p (k m) -> p k m", k=16)
        for m in range(2):
            P.dma("sp", sc[:, :, m], cvec[m].rearrange("(k p) -> p k", p=128), writes=[Bsc],
                  allow_slow_non_contiguous=True)
        P.op("act", lambda e: e.activation(sg, sc, AF.Sigmoid), reads=[Bsc], writes=[Bsc])
        P.op("dve", lambda e: e.tensor_tensor(sc, sc, sg, ALU.mult), reads=[Bsc], writes=[Bsc])
        bcol = A.f32(o_bcol, 64).rearrange("p (k f) -> p k f", k=4)
        kind_off = [0, D, 3 * D, 4 * D]
        for k in range(4):
            P.dma("sp", bcol[:, k, :], b_ada[kind_off[k]:kind_off[k] + D].rearrange("(f p) -> p f", p=128),
                  writes=[Bbcol], allow_slow_non_contiguous=True)
        gcol = A.f32(o_gcol, 32).rearrange("p (k f) -> p k f", k=2)
        P.dma("sp", gcol[:, 0, :], g_mix.rearrange("(f p) -> p f", p=128), writes=[Bgcol], allow_slow_non_contiguous=True)
        P.dma("sp", gcol[:, 1, :], g_mlp.rearrange("(f p) -> p f", p=128), writes=[Bgcol], allow_slow_non_contiguous=True)
        bk = nextbank()
        sc16 = A.bf(o_sg, 32).rearrange("p (k m) -> p k m", k=16)
        P.op("dve", lambda e: e.tensor_copy(sc16, sc), reads=[Bsc], writes=[Bsc])
        for k in range(4):
            for ch in range(8):
                wsb, Bw = wload(w_ada3, 0, 16, kind_off[k] + ch * 256, 256)

                def fn(e, wsb=wsb, k=k, ch=ch, bk=bk):
                    ins = None
                    for f2 in range(2):
                        col = (k * 16 + ch * 2 + f2) * 2
                        for kt in range(16):
                            ins = e.matmul(ps32(bk, 2, col), wsb[:, kt, f2 * 128:(f2 + 1) * 128], sc16[:, kt, :],
                                           start=(kt == 0), stop=(kt == 15))
                    return ins
                P.op("pe", fn, reads=[Bw, Bsc], writes=[pb[bk]])
        mraw = ps32(bk, 128).rearrange("p (k f m) -> p k f m", k=4, f=16)
        P.op("dve", lambda e: e.tensor_tensor(modc, mraw, bcol.unsqueeze(3).broadcast_to([128, 4, 16, 2]), ALU.add),
             reads=[pb[bk], Bbcol], writes=[Bmod])
        for k, gi in ((1, 0), (3, 1)):
            P.op("dve", lambda e, k=k, gi=gi: e.scalar_tensor_tensor(
                modc[:, k], modc[:, k], 1.0, gcol[:, gi, :].unsqueeze(2).broadcast_to([128, 16, 2]), ALU.add, ALU.mult),
                reads=[Bmod, Bgcol], writes=[Bmod])

    def compute_gates(ms):
        top = o_region
        o_sc = top; top += 16
        o_sg = top; top += 16
        o_rep = top; top += 2048
        o_brow = top; top += 2 * D
        o_one = top; top += 128
        Bsc, Brep, Bbrow = Buf("gsc"), Buf("grep"), Buf("gbrow")
        sc = A.f32(o_sc, 16)
        sg = A.f32(o_sg, 16)
        P.dma("sp", sc, cvec[ms].rearrange("(k p) -> p k", p=128), writes=[Bsc], allow_slow_non_contiguous=True)
        P.op("act", lambda e: e.activation(sg, sc, AF.Sigmoid), reads=[Bsc], writes=[Bsc])
        P.op("dve", lambda e: e.tensor_tensor(sc, sc, sg, ALU.mult), reads=[Bsc], writes=[Bsc])
        rep = A.bf(o_rep, 2048).rearrange("p (k m) -> p k m", k=16)
        P.op("dve", lambda e: e.tensor_copy(rep, sc.unsqueeze(2).broadcast_to([128, 16, 128])), reads=[Bsc], writes=[Brep])
        P.dma("sp", gate_bc[:, 0:D], b_ada[2 * D:3 * D].partition_broadcast(128), writes=[Bgate])
        P.dma("sp", gate_bc[:, D:2 * D], b_ada[5 * D:6 * D].partition_broadcast(128), writes=[Bgate])
        for gi, coff in ((0, 2 * D), (1, 5 * D)):
            for ch in range(8):
                wsb, Bw = wload(w_ada3, 0, 16, coff + ch * 256, 256)
                bk = nextbank()

                def fn(e, wsb=wsb, bk=bk):
                    ins = None
                    for kt in range(16):
                        ins = e.matmul(ps32(bk, 256), rep[:, kt, :], wsb[:, kt, :], start=(kt == 0), stop=(kt == 15))
                    return ins
                P.op("pe", fn, reads=[Bw, Brep], writes=[pb[bk]])
                gsl = gate_bc[:, gi * D + ch * 256:gi * D + ch * 256 + 256]
                P.op("dve", lambda e, bk=bk, gsl=gsl: e.tensor_tensor(gsl, gsl, ps32(bk, 256), ALU.add),
                     reads=[pb[bk], Bgate], writes=[Bgate])

    def norm_to_hT(xs, Bxs, xn, Bxn, hT, BhT, ms, kA, kS, junk, Bjunk):
        P.op("dve", lambda e: e.memset(cols[:, 0:4], 0.0), writes=[Bcols])
        for t in range(4):
            ssc = cols[:, t:t + 1]
            P.op("act", lambda e, t=t, ssc=ssc: e.activation(junk, xs[t], AF.Square, accum_out=ssc),
                 reads=[Bxs[t]], writes=[Bjunk, Bcols])
        P.op("act", lambda e: e.activation(cols[:, 8:12], cols[:, 0:4], AF.Sqrt, bias=EPS, scale=1.0 / D),
             reads=[Bcols], writes=[Bcols])
        P.op("dve", lambda e: e.reciprocal(cols[:, 8:12], cols[:, 8:12]), reads=[Bcols], writes=[Bcols])
        for t in range(4):
            rsc = cols[:, 8 + t:9 + t]
            eng = evac_eng()
            if eng == "act":
                P.op("act", lambda e, t=t, rsc=rsc: e.activation(xn[t], xs[t], AF.Copy, scale=rsc),
                     reads=[Bxs[t], Bcols], writes=[Bxn[t]])
            else:
                P.op("dve", lambda e, t=t, rsc=rsc: e.tensor_scalar(xn[t], xs[t], rsc, None, ALU.mult),
                     reads=[Bxs[t], Bcols], writes=[Bxn[t]])
        transposes_to_hT(xn, Bxn, hT, BhT, ms, kA, kS)

    def transposes_to_hT(xn, Bxn, hT, BhT, ms, kA, kS):
        for ft in range(16):
            bk = nextbank()

            def fn(e, ft=ft, bk=bk):
                ins = None
                for t in range(4):
                    ins = e.transpose(ps16(bk, 128, t * 128), xn[t][:, ft * 128:(ft + 1) * 128], id16)
                return ins
            P.op("pe", fn, reads=list(Bxn) + [Bconst], writes=[pb[bk]])
            eng = evac_eng()
            Ac = modc[:, kA, ft, ms:ms + 1]
            Sc = modc[:, kS, ft, ms:ms + 1]
            if eng == "act":
                P.op("act", lambda e, ft=ft, bk=bk, Ac=Ac, Sc=Sc: e.activation(
                    hT[:, ft, :], ps16(bk, 512), AF.Identity, bias=Sc, scale=Ac),
                    reads=[pb[bk], Bmod], writes=[BhT[ft]])
            else:
                P.op("dve", lambda e, ft=ft, bk=bk, Ac=Ac, Sc=Sc: e.tensor_scalar(
                    hT[:, ft, :], ps16(bk, 512), Ac, Sc, ALU.mult, ALU.add),
                    reads=[pb[bk], Bmod], writes=[BhT[ft]])

    def add_emb(xt_ap, Bx, ohcol, ohsb, Boh):
        P.dma("sp", ohsb, oh[:, ohcol:ohcol + 128], writes=[Boh])
        q = nextquad()
        for hh in range(2):
            for c2 in range(2):
                bk = q + hh * 2 + c2
                P.op("pe", lambda e, hh=hh, c2=c2, bk=bk: e.matmul(
                    ps32(bk), ohsb[hh * 64:(hh + 1) * 64, :], Etab[hh * 64:(hh + 1) * 64, c2 * 512:(c2 + 1) * 512],
                    start=True, stop=True), reads=[Boh, Bconst], writes=[pb[bk]])
        P.op("dve", lambda e, q=q: e.tensor_tensor(xt_ap, xt_ap, psum[:, q * 512:q * 512 + 2048], ALU.add),
             reads=[pb[q], pb[q + 1], pb[q + 2], pb[q + 3], Bx], writes=[Bx])

    def phase2(S):
        top = o_region
        o_xt = top; top += 4 * D
        o_hT = top; top += 4096
        o_R = top; top += 10240
        o_tmp = top; top += 3 * 512
        o_wst = top; top += 4 * 256
        o_oh = top; top += 128
        assert top - o_region <= REGION_WORDS, (top - o_region, REGION_WORDS)
        xt = [A.f32(o_xt + t * D, D) for t in range(4)]
        Bxt = [Buf(f"xt{t}") for t in range(4)]
        hT = A.bf(o_hT, 8192).rearrange("p (k n) -> p k n", k=16)
        BhT = [Buf(f"hT{k}") for k in range(16)]
        uT = A.bf(o_R, 4096).rearrange("p (k n) -> p k n", k=8)
        BuT = [Buf(f"uT{k}") for k in range(8)]
        vtk = [A.bf(o_R + 2048 + t * 512, 1024) for t in range(4)]
        Bv = [Buf(f"v{t}") for t in range(4)]
        yaT = A.bf(o_R + 4096, 4096).rearrange("p (k n) -> p k n", k=8)
        ByaT = [Buf(f"yaT{k}") for k in range(8)]
        mT = A.bf(o_R + 6144, 8192).rearrange("p (k n) -> p k n", k=16)
        BmT = [Buf(f"mT{k}") for k in range(16)]
        aT = A.bf(o_R, 16384).rearrange("p (k n) -> p k n", k=32)
        BaT = [Buf(f"aT{k}") for k in range(32)]
        xn = [A.bf(o_R + t * 1024, D) for t in range(4)]
        Bxn = [Buf(f"xn{t}") for t in range(4)]
        junk = A.bf(o_R + 4096, D)
        Bjunk = Buf("junk")
        tmpA = A.f32(o_tmp, 512)
        tmpB = A.f32(o_tmp + 512, 512)
        tmpC = A.f32(o_tmp + 1024, 512)
        BtA, BtB, BtC = Buf("tA"), Buf("tB"), Buf("tC")
        wst = [A.bf(o_wst + t * 256, 512).rearrange("p (g i) -> p g i", g=4) for t in range(4)]
        Bwst = [Buf(f"wst{t}") for t in range(4)]
        ohsb = A.f32(o_oh, 128)
        Boh = Buf("oh")
        Rall = BuT + Bv + ByaT + BmT + BaT + Bxn + [Bjunk]
        ms = S["ms"]
        g1 = gate_bc[:, 0:D]
        g2 = gate_bc[:, D:2 * D]

        def alias_guard(new_bufs, old_bufs):
            for nb in new_bufs:
                for ob in old_bufs:
                    if ob.w is not None:
                        nb.r[("w", id(ob))] = ob.w
                    for k, t in ob.r.items():
                        nb.r[(k, id(ob))] = t

        for tile in range(S["ntok"] // 512):
            t0 = tile * 512
            alias_guard(Bxn + [Bjunk], BaT + BmT)
            for t in range(4):
                P.dma("sp", xt[t], S["x"][t0 + t * 128:t0 + (t + 1) * 128, :], writes=[Bxt[t]])
                if S["emb"]:
                    add_emb(xt[t], Bxt[t], S["ohoff"] + t0 + t * 128, ohsb, Boh)
            norm_to_hT(xt, Bxt, xn, Bxn, hT, BhT, ms, 1, 0, junk, Bjunk)
            if tile == 0 and S["name"] == "prompt":
                dbg("hT", hT, BhT)
            alias_guard(BuT + Bv + ByaT, Bxn + [Bjunk])
            for ch in range(4):
                wsb, Bw = wl(("u", ch), 16, 256)
                for m2 in range(2):
                    mt = ch * 2 + m2
                    bk = nextbank()

                    def fn(e, wsb=wsb, m2=m2, bk=bk):
                        ins = None
                        for kt in range(16):
                            ins = e.matmul(ps32(bk), wsb[:, kt, m2 * 128:(m2 + 1) * 128], hT[:, kt, :],
                                           start=(kt == 0), stop=(kt == 15))
                        return ins
                    P.op("pe", fn, reads=[Bw] + BhT, writes=[pb[bk]])
                    P.op("act", lambda e, mt=mt, bk=bk: e.activation(uT[:, mt, :], ps32(bk), AF.Gelu_apprx_tanh),
                         reads=[pb[bk]], writes=[BuT[mt]])
            for ncn in range(2):
                wA, BwA = wl(("v", ncn, 0), 8, 512)
                wB, BwB = wl(("v", ncn, 1), 8, 512)
                for t in range(4):
                    bk = nextbank()

                    def fn1(e, wA=wA, t=t, bk=bk):
                        ins = None
                        for kt in range(8):
                            ins = e.matmul(ps32(bk), hT[:, kt, t * 128:(t + 1) * 128], wA[:, kt, :],
                                           start=(kt == 0), stop=False)
                        return ins

                    def fn2(e, wB=wB, t=t, bk=bk):
                        ins = None
                        for kt in range(8):
                            ins = e.matmul(ps32(bk), hT[:, 8 + kt, t * 128:(t + 1) * 128], wB[:, kt, :],
                                           start=False, stop=(kt == 7))
                        return ins
                    P.op("pe", fn1, reads=[BwA] + BhT, writes=[pb[bk]])
                    P.op("pe", fn2, reads=[BwB] + BhT, writes=[pb[bk]])
                    P.op("act", lambda e, t=t, ncn=ncn, bk=bk: e.activation(
                        vtk[t][:, ncn * 512:(ncn + 1) * 512], ps32(bk), AF.Gelu_apprx_tanh),
                        reads=[pb[bk]], writes=[Bv[t]])
            P.op("dve", lambda e: e.memset(cols[:, 16:20], 0.0), writes=[Bcols])
            for t in range(4):
                ssc = cols[:, 16 + t:17 + t]
                P.op("act", lambda e, t=t, ssc=ssc: e.activation(A.bf(o_tmp, 1024), vtk[t], AF.Square, accum_out=ssc),
                     reads=[Bv[t]], writes=[BtA, BtB, Bcols])
            P.op("act", lambda e: e.activation(cols[:, 24:28], cols[:, 16:20], AF.Sqrt, bias=EPS, scale=1.0 / DA),
                 reads=[Bcols], writes=[Bcols])
            P.op("dve", lambda e: e.reciprocal(cols[:, 24:28], cols[:, 24:28]), reads=[Bcols], writes=[Bcols])
            for t in range(4):
                rsc = cols[:, 24 + t:25 + t]
                P.op("dve", lambda e, t=t, rsc=rsc: e.tensor_scalar(wst[t], wsT, rsc, None, ALU.mult),
                     reads=[Bcols, Bconst], writes=[Bwst[t]])
            for mt in range(8):
                ga = mt // 2
                bk = nextbank()

                def fn(e, mt=mt, ga=ga, bk=bk):
                    ins = None
                    for t in range(4):
                        ins = e.matmul(ps32(bk, 128, t * 128), vtk[t][:, mt * 128:(mt + 1) * 128], wst[t][:, ga, :],
                                       start=True, stop=True)
                    return ins
                P.op("pe", fn, reads=Bv + Bwst, writes=[pb[bk]])
                P.op("dve", lambda e, mt=mt, ga=ga, bk=bk: e.scalar_tensor_tensor(
                    tmpA.rearrange("p (t i) -> p t i", t=4), ps32(bk).rearrange("p (t i) -> p t i", t=4),
                    gsgu_c[:, mt:mt + 1], bs_bc[:, ga, :].unsqueeze(1).broadcast_to([128, 4, 128]), ALU.mult, ALU.add),
                    reads=[pb[bk], Bconst], writes=[BtA])
                P.op("dve", lambda e, mt=mt: e.tensor_tensor(yaT[:, mt, :], tmpA, uT[:, mt, :], ALU.mult),
                     reads=[BtA, BuT[mt]], writes=[ByaT[mt]])
            if tile == 0 and S["name"] == "prompt":
                dbg("uT", uT, BuT)
                dbg("v0", vtk[0], [Bv[0]])
                dbg("yaT", yaT, ByaT)
                dbg("wst0", wst[0], [Bwst[0]])
            for ft2 in range(8):
                pA, pB, Bpw = wl(("p", ft2), 8, 256, two=True)
                for f in range(2):
                    ft = ft2 * 2 + f
                    gA, gB, Bgw = wl(("g", ft), 16, 128, two=True)
                    q = nextquad()
                    bga, bpa, bgb, bpb = q, q + 1, q + 2, q + 3

                    def fga(e, gA=gA, bk=bga):
                        ins = None
                        for kt in range(16):
                            ins = e.matmul(ps32(bk), gA[:, kt, :], hT[:, kt, :], start=(kt == 0), stop=(kt == 15))
                        return ins

                    def fgb(e, gB=gB, bk=bgb):
                        ins = None
                        for kt in range(16):
                            ins = e.matmul(ps32(bk), gB[:, kt, :], hT[:, kt, :], start=(kt == 0), stop=(kt == 15))
                        return ins

                    def fpa(e, pA=pA, f=f, bk=bpa):
                        ins = None
                        for kt in range(8):
                            ins = e.matmul(ps32(bk), pA[:, kt, f * 128:(f + 1) * 128], yaT[:, kt, :],
                                           start=(kt == 0), stop=(kt == 7))
                        return ins

                    def fpb(e, pB=pB, f=f, bk=bpb, t0=t0):
                        ins = None
                        for kt in range(8):
                            ins = e.matmul(ps32(bk), pB[:, kt, f * 128:(f + 1) * 128], ybT[:, kt, S["yoff"] + t0:S["yoff"] + t0 + 512],
                                           start=(kt == 0), stop=(kt == 7))
                        return ins
                    P.op("pe", fga, reads=[Bgw] + BhT, writes=[pb[bga]])
                    P.op("pe", fpa, reads=[Bpw] + ByaT, writes=[pb[bpa]])
                    P.op("pe", fgb, reads=[Bgw] + BhT, writes=[pb[bgb]])
                    P.op("pe", fpb, reads=[Bpw] + BybT, writes=[pb[bpb]])
                    P.op("act", lambda e, bk=bga: e.activation(tmpA, ps32(bk), AF.Sigmoid), reads=[pb[bga]], writes=[BtA])
                    P.op("dve", lambda e, bk=bpa: e.tensor_tensor(tmpA, tmpA, ps32(bk), ALU.mult),
                         reads=[BtA, pb[bpa]], writes=[BtA])
                    P.op("act", lambda e, bk=bgb: e.activation(tmpB, ps32(bk), AF.Sigmoid), reads=[pb[bgb]], writes=[BtB])
                    P.op("dve", lambda e, bk=bpb: e.tensor_tensor(tmpB, tmpB, ps32(bk), ALU.mult),
                         reads=[BtB, pb[bpb]], writes=[BtB])
                    P.op("dve", lambda e, ft=ft: e.tensor_tensor(mT[:, ft, :], tmpA, tmpB, ALU.add),
                         reads=[BtA, BtB], writes=[BmT[ft]])
            if tile == 0 and S["name"] == "prompt":
                dbg("mT", mT, BmT)
            for c in range(4):
                wA, BwA = wl(("o", c, 0), 8, 512)
                wB, BwB = wl(("o", c, 1), 8, 512)
                for t in range(4):
                    bk = nextbank()

                    def fn1(e, wA=wA, t=t, bk=bk):
                        ins = None
                        for kt in range(8):
                            ins = e.matmul(ps32(bk), mT[:, kt, t * 128:(t + 1) * 128], wA[:, kt, :],
                                           start=(kt == 0), stop=False)
                        return ins

                    def fn2(e, wB=wB, t=t, bk=bk):
                        ins = None
                        for kt in range(8):
                            ins = e.matmul(ps32(bk), mT[:, 8 + kt, t * 128:(t + 1) * 128], wB[:, kt, :],
                                           start=False, stop=(kt == 7))
                        return ins
                    P.op("pe", fn1, reads=[BwA] + BmT, writes=[pb[bk]])
                    P.op("pe", fn2, reads=[BwB] + BmT, writes=[pb[bk]])
                    P.op("dve", lambda e, c=c, bk=bk: e.tensor_tensor(tmpC, ps32(bk), g1[:, c * 512:(c + 1) * 512], ALU.mult),
                         reads=[pb[bk], Bgate], writes=[BtC])
                    P.op("dve", lambda e, c=c, t=t: e.tensor_tensor(
                        xt[t][:, c * 512:(c + 1) * 512], xt[t][:, c * 512:(c + 1) * 512], tmpC, ALU.add),
                        reads=[BtC, Bxt[t]], writes=[Bxt[t]])
            if tile == 0 and S["name"] == "prompt":
                dbg("xmid0", xt[0], [Bxt[0]])
            alias_guard(Bxn + [Bjunk], BuT + Bv + ByaT + BmT)
            norm_to_hT(xt, Bxt, xn, Bxn, hT, BhT, ms, 3, 2, junk, Bjunk)
            if tile == 0 and S["name"] == "prompt":
                dbg("h2T", hT, BhT)
            alias_guard(BaT, Bxn + [Bjunk] + BuT + Bv + ByaT + BmT)
            for hh in range(2):
                for ch in range(16):
                    wsb, Bw = wl(("mi", hh, ch), 16, 256)
                    for f2 in range(2):
                        fl = ch * 2 + f2
                        bk = nextbank()

                        def fn(e, wsb=wsb, f2=f2, bk=bk):
                            ins = None
                            for kt in range(16):
                                ins = e.matmul(ps32(bk), wsb[:, kt, f2 * 128:(f2 + 1) * 128], hT[:, kt, :],
                                               start=(kt == 0), stop=(kt == 15))
                            return ins
                        P.op("pe", fn, reads=[Bw] + BhT, writes=[pb[bk]])
                        tt, Bt = (tmpA, BtA) if fl % 2 == 0 else (tmpB, BtB)
                        P.op("act", lambda e, bk=bk, tt=tt: e.activation(tt, ps32(bk), AF.Relu), reads=[pb[bk]], writes=[Bt])
                        P.op("dve", lambda e, fl=fl, tt=tt: e.tensor_tensor(aT[:, fl, :], tt, tt, ALU.mult),
                             reads=[Bt], writes=[BaT[fl]])
                for c in range(4):
                    q = nextquad()
                    for f8 in range(4):
                        wsb, Bw = wl(("mo", hh, c, f8), 8, 512)

                        def fn(e, wsb=wsb, f8=f8, q=q):
                            ins = None
                            for f in range(8):
                                for t in range(4):
                                    ins = e.matmul(ps32(q + t), aT[:, f8 * 8 + f, t * 128:(t + 1) * 128], wsb[:, f, :],
                                                   start=(f8 == 0 and f == 0), stop=(f8 == 3 and f == 7))
                            return ins
                        P.op("pe", fn, reads=[Bw] + BaT[f8 * 8:(f8 + 1) * 8], writes=[pb[q + t] for t in range(4)])
                    for t in range(4):
                        P.op("dve", lambda e, c=c, bk=q + t: e.tensor_tensor(tmpC, ps32(bk), g2[:, c * 512:(c + 1) * 512], ALU.mult),
                             reads=[pb[q + t], Bgate], writes=[BtC])
                        P.op("dve", lambda e, c=c, t=t: e.tensor_tensor(
                            xt[t][:, c * 512:(c + 1) * 512], xt[t][:, c * 512:(c + 1) * 512], tmpC, ALU.add),
                            reads=[BtC, Bxt[t]], writes=[Bxt[t]])
            P.op("dve", lambda e: e.memset(cols[:, 32:36], 0.0), writes=[Bcols])
            for t in range(4):
                ssc = cols[:, 32 + t:33 + t]
                P.op("act", lambda e, t=t, ssc=ssc: e.activation(A.bf(o_hT, D), xt[t], AF.Square, accum_out=ssc),
                     reads=[Bxt[t]], writes=BhT[0:4] + [Bcols])
            P.op("act", lambda e: e.activation(cols[:, 40:44], cols[:, 32:36], AF.Sqrt, bias=EPS, scale=1.0 / D),
                 reads=[Bcols], writes=[Bcols])
            P.op("dve", lambda e: e.reciprocal(cols[:, 40:44], cols[:, 40:44]), reads=[Bcols], writes=[Bcols])
            for t in range(4):
                rsc = cols[:, 40 + t:41 + t]
                P.op("dve", lambda e, t=t, rsc=rsc: e.scalar_tensor_tensor(xt[t], xt[t], rsc, gfin_bc, ALU.mult, ALU.mult),
                     reads=[Bxt[t], Bcols, Bconst], writes=[Bxt[t]])
                P.dma("sp", S["y"][t0 + t * 128:t0 + (t + 1) * 128, :], xt[t], reads=[Bxt[t]], is_output=True)
        return Rall + Bxt + BhT + [BtA, BtB, BtC] + Bwst + [Boh]

    o_s5p = s5p_off
    rho8 = A.f32(o_s5p, 64)
    phi8 = A.f32(o_s5p + 64, 64)
    h0r_s = A.f32(o_s5p + 128, 64)
    h0i_s = A.f32(o_s5p + 192, 64)
    car_r = A.f32(o_s5p + 256, 64)
    car_i = A.f32(o_s5p + 320, 64)
    ini_r = A.f32(o_s5p + 384, 64)
    ini_i = A.f32(o_s5p + 448, 64)
    Bs5p = Buf("s5p")

    def half(ap, d):
        return ap[d * 64:(d + 1) * 64]

    def phase_s5build():
        top = [o_region]

        def al(n):
            o = top[0]
            top[0] += n
            assert top[0] - o_region <= REGION_WORDS, (top[0] - o_region, REGION_WORDS)
            return o
        o_sm = al(1024)
        sm = [A.f32(o_sm + i * 64, 64) for i in range(16)]
        lamre, lamim, dtt, lrd, lid, mag, abre, abim, den, ur, fre, fim, s1, s2, s3, s4 = sm
        o_ex = al(32)
        EX = A.f32(o_ex, 32).rearrange("p (t k) -> p t k", t=4)
        o_k8 = al(16)
        o_big = al(6144)
        o_pwre = al(2048)
        o_pwim = al(2048)
        o_b = al(2048)
        o_fb = al(2048)
        o_ct = al(2048)
        o_cn = al(256)
        o_t = al(2048)
        o_w3 = al(1024)
        o_w2 = al(1024)
        o_m1 = al(512)
        o_m1t = al(256)
        o_msk = al(256)
        Bsm, Bex, Bbig, Bpw, Bb, Bfb, Bct, Bcn, Bt, Bw3, Bw2, Bm1, Bm1t, Bmsk = [
            Buf(n) for n in "sm ex big pw b fb ct cn t w3 w2 m1 m1t msk".split()]
        nat = A.f32(o_t, 128)
        Bnat = Buf("nat")
        for (src, dst, Bd) in ((a_re, lamre, Bsm), (a_im, lamim, Bsm), (h0re, h0r_s, Bs5p), (h0im, h0i_s, Bs5p)):
            P.dma("sp", nat[0:64].rearrange("p (d s) -> p d s", d=2), src.rearrange("d g s -> g d s"), writes=[Bnat])
            bk = nextbank()
            P.op("pe", lambda e, bk=bk: e.transpose(ps32(bk, 64), nat[0:64], id32[0:64, 0:64]), reads=[Bnat, Bconst], writes=[pb[bk]])
            P.op("dve", lambda e, dst=dst, bk=bk: e.tensor_copy(dst, ps32(bk, 64)), reads=[pb[bk]], writes=[Bd])
        for d in range(2):
            P.dma("sp", half(dtt, d), log_dt[d].partition_broadcast(64), writes=[Bsm])
            for g8 in range(8):
                gsl = slice(g8 * 8, (g8 + 1) * 8)
                P.dma("sp", half(A.f32(o_b, 1024), d).rearrange("p (g c) -> p g c", g=64)[:, gsl, :],
                      b_re[d, gsl].rearrange("g s c -> s g c"), writes=[Bb])
                P.dma("sp", half(A.f32(o_b + 1024, 1024), d).rearrange("p (g c) -> p g c", g=64)[:, gsl, :],
                      b_im[d, gsl].rearrange("g s c -> s g c"), writes=[Bb])
        Bre_ = A.f32(o_b, 1024).rearrange("p (g c) -> p g c", g=64)
        Bim_ = A.f32(o_b + 1024, 1024).rearrange("p (g c) -> p g c", g=64)
        Fbre = A.f32(o_fb, 1024).rearrange("p (g c) -> p g c", g=64)
        Fbim = A.f32(o_fb + 1024, 1024).rearrange("p (g c) -> p g c", g=64)
        CTre = A.f32(o_ct, 1024).rearrange("p (g c) -> p g c", g=64)
        CTim = A.f32(o_ct + 1024, 1024).rearrange("p (g c) -> p g c", g=64)
        V = lambda fn, r, w: P.op("dve", fn, reads=r, writes=w)
        P.op("act", lambda e: e.activation(dtt, dtt, AF.Exp), reads=[Bsm], writes=[Bsm])
        V(lambda e: e.tensor_tensor(lrd, lamre, dtt, ALU.mult), [Bsm], [Bsm])
        V(lambda e: e.tensor_tensor(lid, lamim, dtt, ALU.mult), [Bsm], [Bsm])
        P.op("act", lambda e: e.activation(mag, lrd, AF.Exp), reads=[Bsm], writes=[Bsm])
        P.op("act", lambda e: e.activation(rho8, lrd, AF.Exp, scale=8.0), reads=[Bsm], writes=[Bs5p])
        V(lambda e: e.tensor_scalar(s1, lid, 8.0 / TWO_PI, MAGIC, ALU.mult, ALU.add), [Bsm], [Bsm])
        V(lambda e: e.tensor_scalar(s1, s1, MAGIC, None, ALU.subtract), [Bsm], [Bsm])
        V(lambda e: e.tensor_scalar(s2, lid, 8.0, None, ALU.mult), [Bsm], [Bsm])
        V(lambda e: e.scalar_tensor_tensor(phi8, s1, -TWO_PI, s2, ALU.mult, ALU.add), [Bsm], [Bs5p])
        V(lambda e: e.tensor_copy(s3, lid), [Bsm], [Bsm])
        sin_reduced(s3, s4, abim, Bsm, Bsm)
        V(lambda e: e.tensor_scalar(s3, lid, math.pi / 2, None, ALU.add), [Bsm], [Bsm])
        sin_reduced(s3, s4, abre, Bsm, Bsm)
        V(lambda e: e.tensor_tensor(abre, abre, mag, ALU.mult), [Bsm], [Bsm])
        V(lambda e: e.tensor_tensor(abim, abim, mag, ALU.mult), [Bsm], [Bsm])
        V(lambda e: e.tensor_tensor(den, lamre, lamre, ALU.mult), [Bsm], [Bsm])
        V(lambda e: e.tensor_tensor(s1, lamim, lamim, ALU.mult), [Bsm], [Bsm])
        V(lambda e: e.tensor_tensor(den, den, s1, ALU.add), [Bsm], [Bsm])
        V(lambda e: e.reciprocal(den, den), [Bsm], [Bsm])
        V(lambda e: e.tensor_scalar(ur, abre, -1.0, None, ALU.add), [Bsm], [Bsm])
        V(lambda e: e.tensor_tensor(s1, ur, lamre, ALU.mult), [Bsm], [Bsm])
        V(lambda e: e.tensor_tensor(s2, abim, lamim, ALU.mult), [Bsm], [Bsm])
        V(lambda e: e.tensor_tensor(fre, s1, s2, ALU.add), [Bsm], [Bsm])
        V(lambda e: e.tensor_tensor(fre, fre, den, ALU.mult), [Bsm], [Bsm])
        V(lambda e: e.tensor_tensor(s1, abim, lamre, ALU.mult), [Bsm], [Bsm])
        V(lambda e: e.tensor_tensor(s2, ur, lamim, ALU.mult), [Bsm], [Bsm])
        V(lambda e: e.tensor_tensor(fim, s1, s2, ALU.subtract), [Bsm], [Bsm])
        V(lambda e: e.tensor_tensor(fim, fim, den, ALU.mult), [Bsm], [Bsm])
        SP_ = int(os.environ.get("MK_S5PART", "9"))
        if SP_ <= 1:
            return
        t1 = A.f32(o_t, 1024)
        t2 = A.f32(o_t + 1024, 1024)
        t1g = t1.rearrange("p (g c) -> p g c", g=64)
        t2g = t2.rearrange("p (g c) -> p g c", g=64)
        fre_b = fre.unsqueeze(2).broadcast_to([128, 64, 16])
        fim_b = fim.unsqueeze(2).broadcast_to([128, 64, 16])
        V(lambda e: e.tensor_tensor(t1g, Bre_, fre_b, ALU.mult), [Bsm, Bb], [Bt])
        V(lambda e: e.tensor_tensor(t2g, Bim_, fim_b, ALU.mult), [Bsm, Bb], [Bt])
        V(lambda e: e.tensor_tensor(Fbre, t1g, t2g, ALU.subtract), [Bt], [Bfb])
        V(lambda e: e.tensor_tensor(t1g, Bim_, fre_b, ALU.mult), [Bsm, Bb, Bfb], [Bt])
        V(lambda e: e.tensor_tensor(t2g, Bre_, fim_b, ALU.mult), [Bsm, Bb], [Bt])
        V(lambda e: e.tensor_tensor(Fbim, t1g, t2g, ALU.add), [Bt], [Bfb])
        k8i = A.i32(o_k8, 8)
        k8 = A.f32(o_k8 + 8, 8)
        P.op("pool", lambda e: e.iota(k8i, [[1, 8]], base=0, channel_multiplier=0), writes=[Bex])
        V(lambda e: e.tensor_copy(k8, k8i), [Bex], [Bex])
        spec = {0: [(1.0, 0.0), (-1.0, 0.0), (-1.0, 7.0), (1.0, 1.0)],
                1: [(-1.0, 0.0), (1.0, 0.0), (1.0, 0.0), (-1.0, 8.0)]}
        for d in range(2):
            for tb in range(4):
                a_, b_ = spec[d][tb]
                V(lambda e, d=d, tb=tb, a_=a_, b_=b_: e.tensor_scalar(half(EX[:, tb, :], d), half(k8, d), a_, b_, ALU.mult, ALU.add),
                  [Bex], [Bex])
        if SP_ <= 2:
            return
        PWmag = A.f32(o_big, 2048)
        ang = A.f32(o_big + 2048, 2048)
        kfs = A.f32(o_big + 4096, 2048)
        PWre = A.f32(o_pwre, 2048)
        PWim = A.f32(o_pwim, 2048)
        v3 = lambda ap: ap.rearrange("p (g x) -> p g x", g=64)
        EXf = A.f32(o_ex, 32)
        lrd_b = lrd.unsqueeze(2).broadcast_to([128, 64, 32])
        lid_b = lid.unsqueeze(2).broadcast_to([128, 64, 32])
        EX_b = EXf.unsqueeze(1).broadcast_to([128, 64, 32])
        V(lambda e: e.tensor_tensor(v3(PWmag), lrd_b, EX_b, ALU.mult), [Bsm, Bex], [Bbig])
        P.op("act", lambda e: e.activation(PWmag, PWmag, AF.Exp), reads=[Bbig], writes=[Bbig])
        V(lambda e: e.tensor_tensor(v3(ang), lid_b, EX_b, ALU.mult), [Bsm, Bex], [Bbig])
        sin_reduced(ang, kfs, PWim, Bbig, Bpw)
        V(lambda e: e.tensor_tensor(v3(ang), lid_b, EX_b, ALU.mult), [Bsm, Bex, Bbig], [Bbig])
        V(lambda e: e.tensor_scalar(ang, ang, math.pi / 2, None, ALU.add), [Bbig], [Bbig])
        sin_reduced(ang, kfs, PWre, Bbig, Bpw)
        V(lambda e: e.tensor_tensor(PWre, PWre, PWmag, ALU.mult), [Bpw, Bbig], [Bpw])
        V(lambda e: e.tensor_tensor(PWim, PWim, PWmag, ALU.mult), [Bpw, Bbig], [Bpw])
        if SP_ <= 3:
            return
        mski = A.i32(o_m1t, 256)
        msk2 = A.f32(o_msk, 256)
        pj_i = A.i32(o_k8, 1)
        pj = A.f32(o_k8 + 8, 1)
        P.op("pool", lambda e: e.iota(mski, [[0, 2], [1, 8], [0, 16]], base=0, channel_multiplier=0), reads=[Bm1t], writes=[Bm1t])
        V(lambda e: e.tensor_copy(msk2, mski), [Bm1t], [Bmsk])
        P.op("pool", lambda e: e.iota(pj_i, [[1, 1]], base=0, channel_multiplier=1), reads=[Bex], writes=[Bex])
        V(lambda e: e.tensor_single_scalar(pj_i, pj_i, 4, ALU.arith_shift_right), [Bex], [Bex])
        V(lambda e: e.tensor_copy(pj, pj_i), [Bex], [Bex])
        V(lambda e: e.tensor_scalar(msk2, msk2, pj, None, ALU.subtract), [Bmsk, Bex], [Bmsk])
        V(lambda e: e.tensor_scalar(msk2[:, 0:128], msk2[:, 0:128], 0.0, None, ALU.is_ge), [Bmsk], [Bmsk])
        V(lambda e: e.tensor_scalar(msk2[:, 128:256], msk2[:, 128:256], 0.0, None, ALU.is_le), [Bmsk], [Bmsk])
        if SP_ <= 4:
            return
        P.barrier()
        for gt in range(8):
            for ri, csrc in ((0, c_re), (1, c_im)):
                cn = A.f32(o_cn + ri * 128, 128)
                P.dma("sp", cn.rearrange("p (d s) -> p d s", d=2), csrc[:, gt * 8:(gt + 1) * 8].rearrange("d g c s -> (g c) d s"),
                      writes=[Bcn])
                bk = nextbank()
                P.op("pe", lambda e, cn=cn, bk=bk: e.transpose(ps32(bk, 128), cn, id32), reads=[Bcn, Bconst], writes=[pb[bk]])
                dst = (CTre if ri == 0 else CTim)[:, gt * 8:(gt + 1) * 8, :]
                V(lambda e, dst=dst, bk=bk: e.tensor_copy(dst, ps32(bk, 128).rearrange("p (g c) -> p g c", g=8)), [pb[bk]], [Bct])
        if SP_ <= 5:
            return
        bufs6 = [A.f32(o_big + i * 1024, 1024) for i in range(6)]
        Lre, Lim, Rmre, nRmim, W2r, W2i = bufs6
        v4 = lambda ap: ap.rearrange("p (g k c) -> p g k c", g=8, k=8)
        PW4r = PWre.rearrange("p (g t k) -> p g t k", g=64, t=4)
        PW4i = PWim.rearrange("p (g t k) -> p g t k", g=64, t=4)
        W3re16 = A.bf(o_w3, 1024)
        W3im16 = A.bf(o_w3 + 512, 1024)
        W2re16 = A.bf(o_w2, 1024)
        W2im16 = A.bf(o_w2 + 512, 1024)
        M116 = A.bf(o_m1, 1024)
        m1t = A.f32(o_m1t, 256)

        def cmul(out_re, out_im, ar, ai, br, bi, rd, wr, neg_im=False, out_is_bf=False):
            V(lambda e: e.tensor_tensor(v4(t1), ar, br, ALU.mult), rd, [Bt])
            V(lambda e: e.tensor_tensor(v4(t2), ai, bi, ALU.mult), rd, [Bt])
            V(lambda e: e.tensor_tensor(out_re, t1, t2, ALU.subtract), [Bt], wr)
            V(lambda e: e.tensor_tensor(v4(t1), ar, bi, ALU.mult), rd + wr, [Bt])
            V(lambda e: e.tensor_tensor(v4(t2), ai, br, ALU.mult), rd, [Bt])
            if neg_im:
                V(lambda e: e.scalar_tensor_tensor(out_im, t1, -1.0, t2, ALU.mult, ALU.subtract), [Bt], wr)
            else:
                V(lambda e: e.tensor_tensor(out_im, t1, t2, ALU.add), [Bt], wr)

        for gt in range(8):
            gs = slice(gt * 8, (gt + 1) * 8)
            ctr = CTre[:, gs, :].unsqueeze(2).broadcast_to([128, 8, 8, 16])
            cti = CTim[:, gs, :].unsqueeze(2).broadcast_to([128, 8, 8, 16])
            fbr = Fbre[:, gs, :].unsqueeze(2).broadcast_to([128, 8, 8, 16])
            fbi = Fbim[:, gs, :].unsqueeze(2).broadcast_to([128, 8, 8, 16])
            pw = lambda P4, tb: P4[:, gs, tb, :].unsqueeze(3).broadcast_to([128, 8, 8, 16])
            BL, BR, BW2 = Buf("L"), Buf("R"), Buf("W2p")
            for b_ in (BL, BR, BW2):
                b_.r[("g", 0)] = Bbig.w
            cmul(Lre, Lim, ctr, cti, pw(PW4r, 0), pw(PW4i, 0), [Bct, Bpw], [BL])
            cmul(Rmre, nRmim, fbr, fbi, pw(PW4r, 1), pw(PW4i, 1), [Bfb, Bpw], [BR], neg_im=True)
            cmul(W2r, W2i, fbr, fbi, pw(PW4r, 2), pw(PW4i, 2), [Bfb, Bpw], [BW2])
            cmul(W3re16, W3im16, ctr, cti, pw(PW4r, 3), pw(PW4i, 3), [Bct, Bpw], [Bw3], neg_im=True)
            if SP_ <= 6:
                continue
            P.dma("sp", smat[3, :, gs, :], W3re16.rearrange("p (g n) -> p g n", g=8), reads=[Bw3])
            P.dma("sp", smat[4, :, gs, :], W3im16.rearrange("p (g n) -> p g n", g=8), reads=[Bw3])
            if SP_ <= 7:
                continue
            for gl in range(8):
                for ri, src, dst in ((0, W2r, W2re16), (1, W2i, W2im16)):
                    bk = nextbank()
                    P.op("pe", lambda e, src=src, gl=gl, bk=bk: e.transpose(ps32(bk, 128), src[:, gl * 128:(gl + 1) * 128], id32),
                         reads=[BW2, Bconst], writes=[pb[bk]])
                    P.op("act", lambda e, dst=dst, gl=gl, bk=bk: e.activation(dst[:, gl * 128:(gl + 1) * 128], ps32(bk, 128), AF.Copy),
                         reads=[pb[bk]], writes=[Bw2])
                if SP_ <= 8:
                    continue
                bk = nextbank()
                bk2 = nextbank()

                def fm(e, gl=gl, bk=bk, bk2=bk2):
                    ins = None
                    for d in range(2):
                        cs = slice(gl * 128, (gl + 1) * 128)
                        ob = ps32(bk if d == 0 else bk2, 128)
                        e.matmul(ob, half(Rmre[:, cs], d), half(Lre[:, cs], d), start=True, stop=False)
                        ins = e.matmul(ob, half(nRmim[:, cs], d), half(Lim[:, cs], d), start=False, stop=True)
                    return ins
                P.op("pe", fm, reads=[BL, BR], writes=[pb[bk], pb[bk2]])
                V(lambda e, bk=bk: e.tensor_tensor(m1t[:, 0:128], ps32(bk, 128), msk2[:, 0:128], ALU.mult), [pb[bk], Bmsk], [Bm1t])
                V(lambda e, bk2=bk2: e.tensor_tensor(m1t[:, 128:256], ps32(bk2, 128), msk2[:, 128:256], ALU.mult), [pb[bk2], Bmsk], [Bm1t])
                V(lambda e: e.tensor_tensor(m1t[:, 0:128], m1t[:, 0:128], m1t[:, 128:256], ALU.add), [Bm1t], [Bm1t])
                V(lambda e, gl=gl, gt=gt: e.scalar_tensor_tensor(M116[:, gl * 128:(gl + 1) * 128], id32, dsk_c[:, gt * 8 + gl:gt * 8 + gl + 1],
                                                                m1t[:, 0:128], ALU.mult, ALU.add), [Bm1t, Bconst], [Bm1])
            P.dma("sp", smat[0, :, gs, :], M116.rearrange("p (g n) -> p g n", g=8), reads=[Bm1])
            P.dma("sp", smat[1, :, gs, :], W2re16.rearrange("p (g n) -> p g n", g=8), reads=[Bw2])
            P.dma("sp", smat[2, :, gs, :], W2im16.rearrange("p (g n) -> p g n", g=8), reads=[Bw2])

    def phase1(S):
        top = [o_region]

        def al(n):
            o = top[0]
            top[0] += n
            assert top[0] - o_region <= REGION_WORDS, (top[0] - o_region, REGION_WORDS)
            return o
        o_U = al(8192)
        o_sel = al(4096)
        o_work = top[0]
        NU = 256 if S["name"] == "sample" else 128
        U = A.bf(o_U, 64 * NU).rearrange("p (g n) -> p g n", g=64)
        BU = [Buf(f"U{g}") for g in range(8)]
        Sel = A.bf(o_sel, 8192).rearrange("p (m n) -> p m n", m=64)
        Bsel = Buf("sel")
        o_w = al(256)
        wi = A.i32(o_w, 128)
        wf = A.f32(o_w, 128)
        pi_ = A.i32(o_w + 128, 1)
        pf_ = A.f32(o_w + 144, 1)
        Bw_ = Buf("selw")
        P.op("pool", lambda e: e.iota(wi, [[1, 128]], base=0, channel_multiplier=-1), writes=[Bw_])
        P.op("pool", lambda e: e.iota(pi_, [[1, 1]], base=0, channel_multiplier=1), writes=[Bw_])
        P.op("dve", lambda e: e.tensor_single_scalar(pi_, pi_, 4, ALU.arith_shift_right), reads=[Bw_], writes=[Bw_])
        P.op("dve", lambda e: e.tensor_copy(pf_, pi_), reads=[Bw_], writes=[Bw_])
        P.op("dve", lambda e: e.tensor_copy(wf, wi), reads=[Bw_], writes=[Bw_])
        P.op("dve", lambda e: e.tensor_scalar(pf_, pf_, 1024.0, None, ALU.mult), reads=[Bw_], writes=[Bw_])
        P.op("dve", lambda e: e.tensor_scalar(wf, wf, pf_, None, ALU.add), reads=[Bw_], writes=[Bw_])
        for a in range(8):
            for b in range(8):
                P.op("dve", lambda e, a=a, b=b: e.tensor_scalar(Sel[:, a * 8 + b, :], wf, float(16 * (b - a) + 1024 * a), None, ALU.is_equal),
                     reads=[Bw_], writes=[Bsel])
        top[0] = o_work
        P.barrier()

        def build_U(xsrc, ntok, emb, ohoff):
            t_ = [o_work]

            def al2(n):
                o = t_[0]
                t_[0] += n
                assert t_[0] - o_region <= REGION_WORDS, (t_[0] - o_region, REGION_WORDS)
                return o
            o_xb = al2(2 * D)
            o_xn = al2(4096)
            o_hT = al2(4096)
            o_xbT = al2(2048)
            o_oh = al2(128)
            xblk = [A.f32(o_xb + (t % 2) * D, D) for t in range(4)]
            Bxb2 = [Buf("xb0"), Buf("xb1")]
            Bxb = [Bxb2[t % 2] for t in range(4)]
            xn = [A.bf(o_xn + t * 1024, D) for t in range(4)]
            Bxn = [Buf(f"p1xn{t}") for t in range(4)]
            hT = A.bf(o_hT, 8192).rearrange("p (k n) -> p k n", k=16)
            BhT = [Buf(f"p1hT{k}") for k in range(16)]
            xbT = A.bf(o_xbT, 4096).rearrange("p (k n) -> p k n", k=8)
            BxbT = [Buf(f"xbT{k}") for k in range(8)]
            junk = A.bf(o_xbT, D)
            Bjunk = Buf("p1junk")
            ohsb = A.f32(o_oh, 128)
            Boh = Buf("p1oh")
            ms = S["ms"]
            for tile in range(ntok // 512):
                t0 = tile * 512
                for b_ in BxbT:
                    Bjunk.r[("x", id(b_))] = b_.w
                    for k_, v_ in b_.r.items():
                        Bjunk.r[(k_, id(b_))] = v_
                for t in range(4):
                    P.dma("sp", xblk[t], xsrc[t0 + t * 128:t0 + (t + 1) * 128, :], writes=[Bxb[t]])
                    if emb:
                        add_emb(xblk[t], Bxb[t], ohoff + t0 + t * 128, ohsb, Boh)
                    ssc = cols[:, t:t + 1]
                    rsc = cols[:, 8 + t:9 + t]
                    P.op("dve", lambda e, ssc=ssc: e.memset(ssc, 0.0), writes=[Bcols])
                    P.op("act", lambda e, t=t, ssc=ssc: e.activation(junk, xblk[t], AF.Square, accum_out=ssc),
                         reads=[Bxb[t]], writes=[Bjunk, Bcols])
                    P.op("act", lambda e, ssc=ssc, rsc=rsc: e.activation(rsc, ssc, AF.Sqrt, bias=EPS, scale=1.0 / D),
                         reads=[Bcols], writes=[Bcols])
                    P.op("dve", lambda e, rsc=rsc: e.reciprocal(rsc, rsc), reads=[Bcols], writes=[Bcols])
                    P.op("dve", lambda e, t=t, rsc=rsc: e.tensor_scalar(xn[t], xblk[t], rsc, None, ALU.mult),
                         reads=[Bxb[t], Bcols], writes=[Bxn[t]])
                for b_ in BxbT:
                    b_.r[("j", 0)] = Bjunk.w
                transposes_to_hT(xn, Bxn, hT, BhT, ms, 1, 0)
                for ch in range(4):
                    wsb, Bw = wl(("xb", ch), 16, 256)
                    for m2 in range(2):
                        mt = ch * 2 + m2
                        bk = nextbank()

                        def fn(e, wsb=wsb, m2=m2, bk=bk):
                            ins = None
                            for kt in range(16):
                                ins = e.matmul(ps32(bk), wsb[:, kt, m2 * 128:(m2 + 1) * 128], hT[:, kt, :],
                                               start=(kt == 0), stop=(kt == 15))
                            return ins
                        P.op("pe", fn, reads=[Bw] + BhT, writes=[pb[bk]])
                        if evac_eng() == "act":
                            P.op("act", lambda e, mt=mt, bk=bk: e.activation(xbT[:, mt, :], ps32(bk), AF.Copy),
                                 reads=[pb[bk]], writes=[BxbT[mt]])
                        else:
                            P.op("dve", lambda e, mt=mt, bk=bk: e.tensor_copy(xbT[:, mt, :], ps32(bk)),
                                 reads=[pb[bk]], writes=[BxbT[mt]])
                U4 = U.rearrange("p (a b) n -> p a b n", b=8)
                for gl in range(8):
                    bk = nextbank()

                    def fr(e, gl=gl, bk=bk):
                        ins = None
                        for j in range(8):
                            ins = e.matmul(ps32(bk).rearrange("p (a n) -> p a n", a=8), Sel[:, gl * 8 + j, :], xbT[:, :, j:512:8],
                                           start=(j == 0), stop=(j == 7))
                        return ins
                    P.op("pe", fr, reads=[Bsel] + BxbT, writes=[pb[bk]])
                    dst = U4[:, :, gl, tile * 64:(tile + 1) * 64]
                    if evac_eng() == "act":
                        P.op("act", lambda e, dst=dst, bk=bk: e.activation(dst, ps32(bk).rearrange("p (a n) -> p a n", a=8), AF.Copy),
                             reads=[pb[bk]], writes=BU)
                    else:
                        P.op("dve", lambda e, dst=dst, bk=bk: e.tensor_copy(dst, ps32(bk).rearrange("p (a n) -> p a n", a=8)),
                             reads=[pb[bk]], writes=BU)

        def scan_stream(kind, N, nseq, L):
            t_ = [o_work]

            def al2(n):
                o = t_[0]
                t_[0] += n
                assert t_[0] - o_region <= REGION_WORDS, (t_[0] - o_region, REGION_WORDS)
                return o
            bg = 1024 // N
            slots = [al2(1024) for _ in range(9)]
            cosT, sinT, p1, p2, tre, tim, D0, Tre, Tim = [A.f32(o, 1024) for o in slots]
            Hre, Him = tre, tim
            o_hsh = al2(1024)
            Hshr = A.bf(o_hsh, 1024)
            Hshi = A.bf(o_hsh + 512, 1024)
            o_smb = al2(2560)
            SMb = A.bf(o_smb, 5 * bg * 128).rearrange("p (k g n) -> p k g n", k=5, g=bg)
            o_ysb = al2(1024)
            Ysb = A.bf(o_ysb, 8 * N).rearrange("p (g n) -> p g n", g=8)
            o_fs = al2(512)
            FSr = A.f32(o_fs, 256).rearrange("p (s g) -> p s g", s=4)
            FSi = A.f32(o_fs + 256, 256).rearrange("p (s g) -> p s g", s=4)
            o_idx = al2(256)
            idxi = A.i32(o_idx, L)
            idxf = A.f32(o_idx, L)
            Bc, Bs, Bp1, Bp2, Btr, Bti, BD0, BTr, BTi, Bhshr, Bsmb, Bysb, Bfs, Bidx = [
                Buf(n) for n in "cos sin p1 p2 tre tim D0 Tre Tim hshr smb ysb fs idx".split()]
            Bhshi, Bp3, Bp4, BGre, BGim = [Buf(n) for n in "hshi p3 p4 Gre Gim".split()]
            p3, p4, Gre, Gim = [A.f32(o_gate + k_ * 1024, 1024) for k_ in range(4)]
            V = lambda fn, r, w: P.op("dve", fn, reads=r, writes=w)
            G_ = lambda fn, r, w: P.op("dve", fn, reads=r, writes=w)
            v4 = lambda ap: ap.rearrange("p (g s l) -> p g s l", g=bg, s=nseq)
            v3 = lambda ap: ap.rearrange("p (g x) -> p g x", g=bg)
            P.op("pool", lambda e: e.iota(idxi, [[1, L]], base=1, channel_multiplier=0), writes=[Bidx])
            V(lambda e: e.tensor_copy(idxf, idxi), [Bidx], [Bidx])
            V(lambda e: e.tensor_scalar(half(idxf, 1), half(idxf, 1), -1.0, float(L + 1), ALU.mult, ALU.add), [Bidx], [Bidx])
            if kind != "prompt":
                for (ini, h0s, car) in ((ini_r, h0r_s, car_r), (ini_i, h0i_s, car_i)):
                    if kind == "other":
                        V(lambda e, ini=ini, h0s=h0s: e.tensor_tensor(ini, h0s, rho8, ALU.mult), [Bs5p], [Bs5p])
                    else:
                        V(lambda e, ini=ini, h0s=h0s: e.tensor_tensor(half(ini, 0), half(h0s, 0), half(rho8, 0), ALU.mult), [Bs5p], [Bs5p])
                        V(lambda e, ini=ini, car=car: e.tensor_tensor(half(ini, 1), half(car, 1), half(rho8, 1), ALU.mult), [Bs5p], [Bs5p])
            def batch(bi):
                g0 = bi * bg
                gs = slice(g0, g0 + bg)
                P.dma("sp", SMb, smat[:, :, gs, :].rearrange("k p g n -> p k g n"), writes=[Bsmb])
                q = nextquad()
                for ri in range(2):
                    def fg(e, ri=ri, q=q, g0=g0):
                        ins = None
                        for gl in range(bg):
                            ins = e.matmul(psum[:, (q + 2 * ri) * 512 + gl * N:(q + 2 * ri) * 512 + (gl + 1) * N],
                                           SMb[:, 1 + ri, gl, :], U[:, g0 + gl, 0:N], start=True, stop=True)
                        return ins
                    P.op("pe", fg, reads=[Bsmb, BU[g0 // 8]], writes=[pb[q + 2 * ri], pb[q + 2 * ri + 1]])
                Gre_p = psum[:, q * 512:q * 512 + 1024]
                Gim_p = psum[:, (q + 2) * 512:(q + 2) * 512 + 1024]
                Gre, Gim = Gre_p, Gim_p
                BGr = [pb[q], pb[q + 1]]
                BGi = [pb[q + 2], pb[q + 3]]
                ph_b = phi8[:, gs].unsqueeze(2).broadcast_to([128, bg, L])
                ix_b = idxf.unsqueeze(1).broadcast_to([128, bg, L])
                a3 = lambda ap: ap[:, 0:bg * L].rearrange("p (g l) -> p g l", g=bg)
                V(lambda e: e.tensor_tensor(a3(p1), ph_b, ix_b, ALU.mult), [Bs5p, Bidx], [Bp1])
                sin_reduced(p1[:, 0:bg * L], p2[:, 0:bg * L], sinT[:, 0:bg * L], [Bp1, Bp2], Bs)
                G_(lambda e: e.tensor_tensor(a3(p3), ph_b, ix_b, ALU.mult), [Bs5p, Bidx], [Bp3])
                G_(lambda e: e.tensor_scalar(p3[:, 0:bg * L], p3[:, 0:bg * L], math.pi / 2, None, ALU.add), [Bp3], [Bp3])
                sin_reduced(p3[:, 0:bg * L], p4[:, 0:bg * L], cosT[:, 0:bg * L], [Bp3, Bp4], Bc)
                cb = a3(cosT).unsqueeze(2).broadcast_to([128, bg, nseq, L])
                sb = a3(sinT).unsqueeze(2).broadcast_to([128, bg, nseq, L])
                V(lambda e: e.tensor_tensor(v4(p1), v4(Gre), cb, ALU.mult), BGr + [Bc], [Bp1])
                V(lambda e: e.tensor_tensor(v4(p2), v4(Gim), sb, ALU.mult), BGi + [Bs], [Bp2])
                V(lambda e: e.tensor_tensor(tre, p1, p2, ALU.add), [Bp1, Bp2], [Btr])
                G_(lambda e: e.tensor_tensor(v4(p3), v4(Gim), cb, ALU.mult), BGi + [Bc], [Bp3])
                G_(lambda e: e.tensor_tensor(v4(p4), v4(Gre), sb, ALU.mult), BGr + [Bs], [Bp4])
                G_(lambda e: e.tensor_tensor(tim, p3, p4, ALU.subtract), [Bp3, Bp4], [Bti])
                V(lambda e: e.tensor_copy(v3(D0), rho8[:, gs].unsqueeze(2).broadcast_to([128, bg, N])), [Bs5p], [BD0])
                V(lambda e: e.memset(half(v4(D0), 0)[:, :, :, 0:1], 0.0), [], [BD0])
                V(lambda e: e.memset(half(v4(D0), 1)[:, :, :, L - 1:L], 0.0), [], [BD0])
                if kind != "prompt":
                    for (tt, Bt_, ini, E_) in ((tre, Btr, ini_r, V), (tim, Bti, ini_i, G_)):
                        E_(lambda e, tt=tt, ini=ini: e.tensor_tensor(half(v3(tt), 0)[:, :, 0:1], half(v3(tt), 0)[:, :, 0:1],
                                                                    half(ini, 0)[:, gs].unsqueeze(2), ALU.add), [Bs5p, Bt_], [Bt_])
                        E_(lambda e, tt=tt, ini=ini: e.tensor_tensor(half(v3(tt), 1)[:, :, N - 1:N], half(v3(tt), 1)[:, :, N - 1:N],
                                                                    half(ini, 1)[:, gs].unsqueeze(2), ALU.add), [Bs5p, Bt_], [Bt_])
                for (src, Bsrc, dst, Bdst, E_) in ((tre, Btr, Tre, BTr, V), (tim, Bti, Tim, BTi, V)):
                    E_(lambda e, src=src, dst=dst: e.tensor_tensor_scan(half(dst, 0), half(D0, 0), half(src, 0), 0.0, ALU.mult, ALU.add),
                       [Bsrc, BD0], [Bdst])
                    E_(lambda e, src=src, dst=dst: e.tensor_tensor_scan(half(dst, 1)[:, ::-1], half(D0, 1)[:, ::-1], half(src, 1)[:, ::-1],
                                                                       0.0, ALU.mult, ALU.add), [Bsrc, BD0], [Bdst])
                if kind == "other":
                    c0 = half(a3(cosT), 1)[:, :, 0:1]
                    s0 = half(a3(sinT), 1)[:, :, 0:1]
                    tr0 = half(v3(Tre), 1)[:, :, 0:1]
                    ti0 = half(v3(Tim), 1)[:, :, 0:1]
                    q1 = half(v3(p1), 1)[:, :, 0:1]
                    q2 = half(v3(p2), 1)[:, :, 0:1]
                    V(lambda e: e.tensor_tensor(q1, c0, tr0, ALU.mult), [Bc, BTr], [Bp1])
                    V(lambda e: e.tensor_tensor(q2, s0, ti0, ALU.mult), [Bs, BTi], [Bp2])
                    V(lambda e: e.tensor_tensor(half(car_r, 1)[:, gs].unsqueeze(2), q1, q2, ALU.subtract), [Bp1, Bp2], [Bs5p])
                    V(lambda e: e.tensor_tensor(q1, s0, tr0, ALU.mult), [Bs, BTr, Bs5p], [Bp1])
                    V(lambda e: e.tensor_tensor(q2, c0, ti0, ALU.mult), [Bc, BTi, Bs5p], [Bp2])
                    V(lambda e: e.tensor_tensor(half(car_i, 1)[:, gs].unsqueeze(2), q1, q2, ALU.add), [Bp1, Bp2], [Bs5p])
                    return
                if kind == "prompt" and bi == 0:
                    dbg("idxf", idxf, [Bidx])
                    dbg("cosT", cosT, [Bc]); dbg("sinT", sinT, [Bs]); dbg("tre", tre, [Btr]); dbg("tim", tim, [Bti])
                    dbg("D0", D0, [BD0]); dbg("Tre", Tre, [BTr]); dbg("Tim", Tim, [BTi])
                V(lambda e: e.tensor_tensor(v4(p1), v4(Tre), cb, ALU.mult), [BTr, Bc], [Bp1])
                V(lambda e: e.tensor_tensor(v4(p2), v4(Tim), sb, ALU.mult), [BTi, Bs], [Bp2])
                V(lambda e: e.tensor_tensor(Hre, p1, p2, ALU.subtract), [Bp1, Bp2], [Btr])
                G_(lambda e: e.tensor_tensor(v4(p3), v4(Tre), sb, ALU.mult), [BTr, Bs], [Bp3])
                G_(lambda e: e.tensor_tensor(v4(p4), v4(Tim), cb, ALU.mult), [BTi, Bc], [Bp4])
                G_(lambda e: e.tensor_tensor(Him, p3, p4, ALU.add), [Bp3, Bp4], [Bti])
                for (Hs, Hh, BH, h0s, car, E_, Bh_) in ((Hshr, Hre, Btr, h0r_s, car_r, V, Bhshr), (Hshi, Him, Bti, h0i_s, car_i, G_, Bhshi)):
                    P.op("act", lambda e, Hs=Hs, Hh=Hh: e.activation(half(v4(Hs), 0)[:, :, :, 1:L], half(v4(Hh), 0)[:, :, :, 0:L - 1], AF.Copy),
                         reads=[BH], writes=[Bh_])
                    P.op("act", lambda e, Hs=Hs, Hh=Hh: e.activation(half(v4(Hs), 1)[:, :, :, 0:L - 1], half(v4(Hh), 1)[:, :, :, 1:L], AF.Copy),
                         reads=[BH], writes=[Bh_])
                    if kind == "prompt":
                        E_(lambda e, Hs=Hs: e.memset(half(v4(Hs), 0)[:, :, :, 0:1], 0.0), [], [Bh_])
                        E_(lambda e, Hs=Hs: e.memset(half(v4(Hs), 1)[:, :, :, L - 1:L], 0.0), [], [Bh_])
                    else:
                        E_(lambda e, Hs=Hs, h0s=h0s: e.tensor_copy(half(v3(Hs), 0)[:, :, 0:1], half(h0s, 0)[:, gs].unsqueeze(2)), [Bs5p], [Bh_])
                        E_(lambda e, Hs=Hs, car=car: e.tensor_copy(half(v3(Hs), 1)[:, :, N - 1:N], half(car, 1)[:, gs].unsqueeze(2)), [Bs5p], [Bh_])
                if kind == "prompt":
                    for (FS, Hh, BH, E_) in ((FSr, Hre, Btr, V), (FSi, Him, Bti, G_)):
                        E_(lambda e, FS=FS, Hh=Hh: e.tensor_copy(half(FS, 0)[:, :, gs].rearrange("p s g -> p g s"),
                                                                 half(v4(Hh), 0)[:, :, :, L - 1]), [BH], [Bfs])
                        E_(lambda e, FS=FS, Hh=Hh: e.tensor_copy(half(FS, 1)[:, :, gs].rearrange("p s g -> p g s"),
                                                                 half(v4(Hh), 1)[:, :, :, 0]), [BH], [Bfs])
                if kind == "prompt" and bi == 0:
                    dbg("Hre", Hre, [Btr]); dbg("Him", Him, [Bti]); dbg("Hshr", Hshr, [Bhshr]); dbg("Hshi", Hshi, [Bhshi])
                for gl in range(bg):
                    g = g0 + gl
                    bk = nextbank()

                    def fy(e, gl=gl, g=g, bk=bk):
                        e.matmul(ps32(bk, N), SMb[:, 0, gl, :], U[:, g, 0:N], start=True, stop=False)
                        e.matmul(ps32(bk, N), SMb[:, 3, gl, :], v3(Hshr)[:, gl, :], start=False, stop=False)
                        return e.matmul(ps32(bk, N), SMb[:, 4, gl, :], v3(Hshi)[:, gl, :], start=False, stop=True)
                    P.op("pe", fy, reads=[Bsmb, BU[g // 8], Bhshr, Bhshi], writes=[pb[bk]])
                    P.op("act", lambda e, g=g, bk=bk: e.activation(Ysb[:, g % 8, :], ps32(bk, N), AF.Copy), reads=[pb[bk]], writes=[Bysb])
                if kind == "prompt" and bi == 0:
                    dbg("Ysb", Ysb, [Bysb])
                if (g0 + bg) % 8 == 0:
                    gt = g0 // 8
                    for nb in range(N // 64):
                        bk = nextbank()

                        def fb(e, nb=nb, bk=bk):
                            ins = None
                            for i in range(8):
                                for gl in range(8):
                                    ins = e.matmul(ps32(bk)[:, i:512:8], Sel[:, i * 8 + gl, :], Ysb[:, gl, nb * 64:(nb + 1) * 64],
                                                   start=(gl == 0), stop=(gl == 7))
                            return ins
                        P.op("pe", fb, reads=[Bsel, Bysb], writes=[pb[bk]])
                        P.op("act", lambda e, gt=gt, nb=nb, bk=bk: e.activation(ybT[:, gt, nb * 512:(nb + 1) * 512], ps32(bk), AF.Gelu_apprx_tanh),
                             reads=[pb[bk]], writes=[BybT[gt]])

            for bi in range(64 // bg):
                batch(bi)
            if kind == "prompt":
                fsT = A.f32(slots[0], 128)
                BfsT = Buf("fsT")
                for (FS, dst) in ((FSr, sre), (FSi, sim)):
                    for sq in range(4):
                        bk = nextbank()
                        P.op("pe", lambda e, FS=FS, sq=sq, bk=bk: e.transpose(ps32(bk, 128)[0:64], FS[:, sq, :], id32),
                             reads=[Bfs, Bconst], writes=[pb[bk]])
                        V(lambda e, bk=bk: e.tensor_copy(fsT[0:64], ps32(bk, 128)[0:64]), [pb[bk], Bc], [BfsT])
                        P.dma("sp", dst[sq].rearrange("d g s -> g d s"), fsT[0:64].rearrange("p (d s) -> p d s", d=2),
                              reads=[BfsT], is_output=True)

        def glu(ntok):
            t_ = [o_work]

            def al2(n):
                o = t_[0]
                t_[0] += n
                return o
            o_wg = al2(4096)
            o_yt = al2(2048)
            o_sg = al2(1024)
            wg = A.bf(o_wg, 8192).rearrange("p (k n) -> p k n", k=8)
            Bwg = Buf("wglu")
            ytmp = A.bf(o_yt, 4096).rearrange("p (k n) -> p k n", k=8)
            Byt = Buf("ytmp")
            sgs = [A.f32(o_sg, 512), A.f32(o_sg + 512, 512)]
            Bsg = [Buf("sg0"), Buf("sg1")]
            for kk in range(2):
                P.dma("pool", wg[:, kk * 4:(kk + 1) * 4, :], w_glu3[:, kk * 4:(kk + 1) * 4, :], writes=[Bwg])
            for tile in range(ntok // 512):
                ts = slice(tile * 512, (tile + 1) * 512)
                for mt in range(8):
                    bk = nextbank()

                    def fgl(e, mt=mt, bk=bk, ts=ts):
                        ins = None
                        for kt in range(8):
                            ins = e.matmul(ps32(bk), wg[:, kt, mt * 128:(mt + 1) * 128], ybT[:, kt, ts], start=(kt == 0), stop=(kt == 7))
                        return ins
                    P.op("pe", fgl, reads=[Bwg] + BybT, writes=[pb[bk]])
                    sg, Bs_ = sgs[mt % 2], Bsg[mt % 2]
                    P.op("act", lambda e, sg=sg, mt=mt, bk=bk: e.activation(sg, ps32(bk), AF.Sigmoid, bias=bglu_c[:, mt:mt + 1]),
                         reads=[pb[bk], Bconst], writes=[Bs_])
                    P.op("dve", lambda e, sg=sg, mt=mt, ts=ts: e.tensor_tensor(ytmp[:, mt, :], sg, ybT[:, mt, ts], ALU.mult),
                         reads=[Bs_, BybT[mt]], writes=[Byt])
                P.op("dve", lambda e, ts=ts: e.tensor_copy(ybT[:, :, ts], ytmp), reads=[Byt], writes=BybT)

        if S["name"] == "prompt":
            build_U(xp, 1024, False, 0)
            P.barrier()
            dbg("U", U, BU)
            dbg("s5p", A.f32(o_s5p, 512), [Bs5p])
            scan_stream("prompt", 128, 4, 32)
            P.barrier()
            dbg("yg", ybT[:, :, 0:1024], BybT)
        else:
            build_U(xs_oth, 2048, True, 2048)
            P.barrier()
            scan_stream("other", 256, 1, 256)
            P.barrier()
            build_U(xs_own, 2048, True, 0)
            P.barrier()
            scan_stream("own", 256, 1, 256)
        P.barrier()
        glu(S["ntok"])

    phase0()
    P.barrier()
    phase_mod()
    P.barrier()
    convert_weights(True)
    sets = [
        dict(name="prompt", ms=0, emb=False, x=xp, y=yp, ntok=1024, yoff=0, ohoff=0),
        dict(name="sample", ms=1, emb=True, x=xs_own, y=ys, ntok=2048, yoff=0, ohoff=0),
    ]
    if STAGE >= 2:
        phase_s5build()
        P.barrier()
    else:
        P.op("dve", lambda e: e.memset(ybT, 0.0), writes=BybT)
    for S in sets:
        if STAGE >= 3:
            P.barrier()
            phase1(S)
        P.barrier()
        compute_gates(S["ms"])
        if S["name"] == "prompt":
            convert_weights(False)
        P.barrier()
        phase2(S)
    P.emit()
    nc._dbg_names = dbg_names
    return nc


def _core_inputs(r, inp):
    b, hf = r // 2, r % 2
    f = np.ascontiguousarray
    xs = inp["x_sample"][b]
    if hf == 0:
        own, oth = xs[0:2048], xs[2048:4096]
        pos = np.arange(4096)
    else:
        own, oth = xs[4095:2047:-1], xs[2047::-1]
        pos = 4095 - np.arange(4096)
    xpr = inp["x_prompt"][4 * r:4 * r + 4]
    if hf == 1:
        xpr = xpr[:, ::-1]
    ohm = np.zeros((128, 4096), np.float32)
    ohm[pos // 64, np.arange(4096)] = 1.0
    ohm[64 + pos % 64, np.arange(4096)] = 1.0
    dsel = slice(None) if hf == 0 else slice(None, None, -1)
    w_s = inp["w_spatial"][0]
    b_s = inp["b_spatial"][0]
    if hf == 1:
        w_s = w_s[:, ::-1, ::-1]
        b_s = b_s[:, ::-1]
    m = {
        "xs_own": f(own), "xs_oth": f(oth), "xp": f(xpr.reshape(1024, D)), "oh": ohm,
        "cvec": f(np.stack([inp["c_ctx"], inp["c"][b]])),
        "h0re": f(inp["state_ssm_re"][b, 0][dsel]), "h0im": f(inp["state_ssm_im"][b, 0][dsel]),
        "w_ada": inp["w_ada"][0], "b_ada": inp["b_ada"][0], "g_mix": inp["g_norm_mix"][0],
        "w_in": inp["w_in"][0], "g_sgu": inp["g_sgu"][0], "w_s": f(w_s), "b_s": f(b_s),
        "a_re": f(inp["ssm_a_re"][0][dsel]), "a_im": f(inp["ssm_a_im"][0][dsel]),
        "log_dt": f(inp["ssm_log_dt"][0][dsel]),
        "b_re": f(inp["ssm_b_re"][0][dsel]), "b_im": f(inp["ssm_b_im"][0][dsel]),
        "c_re": f(inp["ssm_c_re"][0][dsel]), "c_im": f(inp["ssm_c_im"][0][dsel]),
        "ssm_d": inp["ssm_d"][0], "w_glu": inp["w_glu"][0], "b_glu": inp["b_glu"][0],
        "w_pa": inp["w_proj_a"][0], "w_pb": inp["w_proj_b"][0], "w_out": inp["w_out"][0],
        "g_mlp": inp["g_norm_mlp"][0], "w_mi": inp["w_mlp_in"][0], "w_mo": inp["w_mlp_out"][0],
        "g_fin": inp["g_final"],
    }
    return {k: np.ascontiguousarray(np.asarray(v, dtype=np.float32)) for k, v in m.items()}


_NC_CACHE = {}


def kernel(**inputs):
    inp = {k: np.asarray(v) for k, v in inputs.items()}
    if "nc" not in _NC_CACHE:
        _NC_CACHE["nc"] = build_program()
    nc = _NC_CACHE["nc"]
    ncores = int(os.environ.get("MK_NCORES", "8"))
    in_maps = [_core_inputs(r, inp) for r in range(ncores)]
    if os.environ.get("MK_TRACE", "0") == "1":
        res = run_bass_kernel_spmd(nc, in_maps, core_ids=list(range(ncores)), trace=True)
        print("MK exec_time_ns", res.exec_time_ns)
    else:
        res = run_bass_kernel_spmd(nc, in_maps, core_ids=list(range(ncores)))
    if os.environ.get("MK_DBG", "0") == "1":
        _NC_CACHE["dbg"] = {k: np.asarray(res.results[0][k]).astype(np.float32) for k in list(nc._dbg_names) + ["smat"]}
    y_prompt = np.zeros((32, 256, D), np.float32)
    y_sample = np.zeros((4, 4096, D), np.float32)
    new_re = np.zeros((32, 1, 2, G, NP), np.float32)
    new_im = np.zeros((32, 1, 2, G, NP), np.float32)
    for r in range(ncores):
        o = res.results[r]
        b, hf = r // 2, r % 2
        ypr = o["yp"].reshape(4, 256, D)
        ysr = o["ys"]
        s_re = o["sre"]
        s_im = o["sim"]
        if hf == 1:
            ypr = ypr[:, ::-1]
            ysr = ysr[::-1]
            s_re = s_re[:, ::-1]
            s_im = s_im[:, ::-1]
            y_sample[b, 2048:4096] = ysr
        else:
            y_sample[b, 0:2048] = ysr
        y_prompt[4 * r:4 * r + 4] = ypr
        new_re[4 * r:4 * r + 4, 0] = s_re
        new_im[4 * r:4 * r + 4, 0] = s_im
    return (y_prompt, y_sample, new_re, new_im)
```
